# Optimizing a Trainium2 kernel written in Bass

```python
import math
import jax, jax.numpy as jnp
from jax import lax
import numpy as np

D_MODEL = 2048
BATCH = 4
SEQ = 2048
DEPTH = 2
DEC_BATCH = 32
DEC_SEQ = 1
PAST_LEN = 16384
PAGE_SIZE = 128

N_HEADS = 16
KV_HEADS = 4
HEAD_DIM = 64
GROUP = N_HEADS // KV_HEADS
ATTN_W = N_HEADS * HEAD_DIM
KV_W = KV_HEADS * HEAD_DIM
WINDOW = 128
ATTN_SCALE = HEAD_DIM ** -0.5
NEG_INF = -1e30
N_BUCKETS = 32
MAX_DISTANCE = 128
CONV_W = D_MODEL // 4
CONV_K = 3
POOL_W = D_MODEL // 4
POOL_WINDOWS = (2, 4, 8, 16)
N_POOL_GROUPS = 4
POOL_GROUP = POOL_W // N_POOL_GROUPS
POOL_PREV = 15
IN_W = ATTN_W + 2 * KV_W + 3 * CONV_W + POOL_W
D_FF = 5632
DEEPNORM_ALPHA = (2 * DEPTH) ** 0.25
DEEPNORM_BETA = (8 * DEPTH) ** -0.25
LN_EPS = 1e-5

kernel_name = "hymba_swa_conv_pool_deepnorm_step"


def _ln(x, g, b):
    xf = x.astype(jnp.float32)
    mu = jnp.mean(xf, -1, keepdims=True)
    var = jnp.mean(jnp.square(xf - mu), -1, keepdims=True)
    y = (xf - mu) * lax.rsqrt(var + LN_EPS) * g.astype(jnp.float32) + b.astype(jnp.float32)
    return y.astype(x.dtype)


def _t5_bucket(n):
    max_exact = N_BUCKETS // 2
    nf = jnp.maximum(n, 1).astype(jnp.float32)
    large = max_exact + (jnp.log(nf / max_exact) / math.log(MAX_DISTANCE / max_exact)
                         * (N_BUCKETS - max_exact)).astype(jnp.int32)
    large = jnp.minimum(large, N_BUCKETS - 1)
    return jnp.where(n < max_exact, n, large)


def _sink_attend(q, k, v, dist, valid, bias_by_dist, sinks):
    s = jnp.einsum('...qhgd,...khd->...hgqk', q, k).astype(jnp.float32) * ATTN_SCALE
    bias = bias_by_dist[jnp.clip(dist, 0, WINDOW - 1)]
    bias = jnp.moveaxis(bias, -1, 0).reshape(KV_HEADS, GROUP, *dist.shape)
    s = jnp.where(valid, s + bias, NEG_INF)
    sink = sinks.astype(jnp.float32).reshape(KV_HEADS, GROUP, 1, 1)
    m = jnp.maximum(jnp.max(s, -1, keepdims=True), sink)
    p = jnp.exp(s - m)
    p = (p / (jnp.sum(p, -1, keepdims=True) + jnp.exp(sink - m))).astype(v.dtype)
    return jnp.einsum('...hgqk,...khd->...qhgd', p, v)


def _swa_prompt(q, k, v, bias_by_dist, sinks):
    b, s = q.shape[:2]
    nb = s // WINDOW
    qb = q.reshape(b, nb, WINDOW, KV_HEADS, GROUP, HEAD_DIM)

    def band(t):
        tb = t.reshape(b, nb, WINDOW, KV_HEADS, HEAD_DIM)
        prev = jnp.concatenate([jnp.zeros_like(tb[:, :1]), tb[:, :-1]], 1)
        return jnp.concatenate([prev, tb], 2)

    i = jnp.arange(WINDOW)[:, None]
    j = jnp.arange(2 * WINDOW)[None, :]
    dist = i + WINDOW - j
    first = (jnp.arange(nb) == 0)[:, None, None]
    valid = (dist >= 0) & (dist < WINDOW) & ~(first & (j < WINDOW))
    o = _sink_attend(qb, band(k), band(v), dist, valid[:, None, None], bias_by_dist, sinks)
    return o.reshape(b, s, ATTN_W)


def _swa_decode(q, k, v, ck, cv, bias_by_dist, sinks):
    b, t = q.shape[:2]
    wc = ck.shape[1]
    kk = jnp.concatenate([ck, k], 1)
    vv = jnp.concatenate([cv, v], 1)
    dist = jnp.arange(t)[:, None] + wc - jnp.arange(wc + t)[None, :]
    valid = (dist >= 0) & (dist < WINDOW)
    o = _sink_attend(q.reshape(b, t, KV_HEADS, GROUP, HEAD_DIM), kk, vv, dist, valid,
                     bias_by_dist, sinks)
    return o.reshape(b, t, ATTN_W), kk[:, -wc:], vv[:, -wc:]


def _dwconv3(ext, w):
    L = ext.shape[1] - (CONV_K - 1)
    y = ext[:, 0:L] * w[0]
    for kk in range(1, CONV_K):
        y = y + ext[:, kk:kk + L] * w[kk]
    return y


def _pool_mix(p_ext, pos0, pool_w, pool_scale):
    L = p_ext.shape[1] - POOL_PREV
    xf = p_ext.astype(jnp.float32)
    csz = jnp.concatenate([jnp.zeros_like(xf[:, :1]), jnp.cumsum(xf, 1)], 1)
    cur = xf[:, POOL_PREV:]
    pos = pos0 + jnp.arange(L)
    outs = []
    for g, w in enumerate(POOL_WINDOWS):
        lo, hi = g * POOL_GROUP, (g + 1) * POOL_GROUP
        win = (csz[:, POOL_PREV + 1:POOL_PREV + 1 + L, lo:hi]
               - csz[:, POOL_PREV + 1 - w:POOL_PREV + 1 - w + L, lo:hi])
        cnt = jnp.minimum(pos + 1, w).astype(jnp.float32)[None, :, None]
        d = (win / cnt - cur[..., lo:hi]).astype(p_ext.dtype)
        outs.append(d @ pool_w[g])
    return jnp.concatenate(outs, -1) * pool_scale


def _layer(x, pos0, kv_prev, conv_prev, pool_prev, ffn_prev, bias_by_dist,
           w_in, conv_w, pool_w, pool_scale, sinks, w_o, ln1_g, ln1_b,
           w_up, ffn_conv_w, w_down, ln2_g, ln2_b):
    b, L, _ = x.shape
    z = x @ w_in
    sizes = (ATTN_W, KV_W, KV_W, CONV_W, CONV_W, CONV_W, POOL_W)
    idx = []
    acc = 0
    for sz in sizes[:-1]:
        acc += sz
        idx.append(acc)
    q, k, v, gb, gc, h, pin = jnp.split(z, idx, -1)
    q = q.reshape(b, L, N_HEADS, HEAD_DIM)
    k = k.reshape(b, L, KV_HEADS, HEAD_DIM)
    v = v.reshape(b, L, KV_HEADS, HEAD_DIM)
    if kv_prev is None:
        a = _swa_prompt(q, k, v, bias_by_dist, sinks)
        nk, nv = k[:, -WINDOW:], v[:, -WINDOW:]
    else:
        a, nk, nv = _swa_decode(q, k, v, kv_prev[0], kv_prev[1], bias_by_dist, sinks)
    u_ext = jnp.concatenate([conv_prev, gc * h], 1)
    c = gb * _dwconv3(u_ext, conv_w)
    p_ext = jnp.concatenate([pool_prev, pin], 1)
    pm = _pool_mix(p_ext, pos0, pool_w, pool_scale)
    mix = jnp.concatenate([a, c, pm], -1) @ w_o
    x = _ln(DEEPNORM_ALPHA * x + mix, ln1_g, ln1_b)
    up_ext = jnp.concatenate([ffn_prev, x @ w_up], 1)
    hc = _dwconv3(up_ext, ffn_conv_w)
    g, val = jnp.split(hc, 2, -1)
    f = (jax.nn.silu(g) * val) @ w_down
    x = _ln(DEEPNORM_ALPHA * x + f, ln2_g, ln2_b)
    new = (nk, nv, u_ext[:, -(CONV_K - 1):], p_ext[:, -POOL_PREV:], up_ext[:, -(CONV_K - 1):])
    return x, new


def setup_inputs(seed: int = 0) -> dict:
    key = jax.random.key(seed)
    ks = jax.random.split(key, 24)
    nrm = lambda kk, shape, s=1.0: jax.random.normal(kk, shape, jnp.float32) * s
    wc = min(WINDOW, PAST_LEN)
    return {
        'x_prompt': nrm(ks[0], (BATCH, SEQ, D_MODEL)),
        'x_sample': nrm(ks[1], (DEC_BATCH, DEC_SEQ, D_MODEL)),
        'cache_k': nrm(ks[2], (DEPTH, DEC_BATCH, wc, KV_HEADS, HEAD_DIM)),
        'cache_v': nrm(ks[3], (DEPTH, DEC_BATCH, wc, KV_HEADS, HEAD_DIM)),
        'state_conv': nrm(ks[4], (DEPTH, DEC_BATCH, CONV_K - 1, CONV_W)),
        'state_pool': nrm(ks[5], (DEPTH, DEC_BATCH, POOL_PREV, POOL_W)),
        'state_ffn': nrm(ks[6], (DEPTH, DEC_BATCH, CONV_K - 1, 2 * D_FF)),
        'rel_table': nrm(ks[7], (N_BUCKETS, N_HEADS), 0.5),
        'w_in': nrm(ks[8], (DEPTH, D_MODEL, IN_W), D_MODEL ** -0.5),
        'conv_w': nrm(ks[9], (DEPTH, CONV_K, CONV_W), CONV_K ** -0.5),
        'pool_w': nrm(ks[10], (DEPTH, N_POOL_GROUPS, POOL_GROUP, POOL_GROUP), POOL_GROUP ** -0.5),
        'pool_scale': 1.0 + nrm(ks[11], (DEPTH, POOL_W), 0.1),
        'sinks': nrm(ks[12], (DEPTH, N_HEADS), 0.5),
        'w_o': nrm(ks[13], (DEPTH, D_MODEL, D_MODEL), D_MODEL ** -0.5 * DEEPNORM_BETA),
        'ln1_g': 1.0 + nrm(ks[14], (DEPTH, D_MODEL), 0.02),
        'ln1_b': nrm(ks[15], (DEPTH, D_MODEL), 0.02),
        'w_up': nrm(ks[16], (DEPTH, D_MODEL, 2 * D_FF), D_MODEL ** -0.5),
        'ffn_conv_w': nrm(ks[17], (DEPTH, CONV_K, 2 * D_FF), CONV_K ** -0.5),
        'w_down': nrm(ks[18], (DEPTH, D_FF, D_MODEL), D_FF ** -0.5 * DEEPNORM_BETA),
        'ln2_g': 1.0 + nrm(ks[19], (DEPTH, D_MODEL), 0.02),
        'ln2_b': nrm(ks[20], (DEPTH, D_MODEL), 0.02),
    }


def reference(x_prompt, x_sample, cache_k, cache_v, state_conv, state_pool, state_ffn,
              rel_table, w_in, conv_w, pool_w, pool_scale, sinks, w_o, ln1_g, ln1_b,
              w_up, ffn_conv_w, w_down, ln2_g, ln2_b):
    bias_by_dist = rel_table.astype(jnp.float32)[_t5_bucket(jnp.arange(WINDOW))]
    bp = x_prompt.shape[0]
    zc = jnp.zeros((bp, CONV_K - 1, CONV_W), x_prompt.dtype)
    zp = jnp.zeros((bp, POOL_PREV, POOL_W), x_prompt.dtype)
    zf = jnp.zeros((bp, CONV_K - 1, 2 * D_FF), x_prompt.dtype)
    xp, xs = x_prompt, x_sample
    sp, ss = [], []
    for l in range(DEPTH):
        wts = (w_in[l], conv_w[l], pool_w[l], pool_scale[l], sinks[l], w_o[l], ln1_g[l], ln1_b[l],
               w_up[l], ffn_conv_w[l], w_down[l], ln2_g[l], ln2_b[l])
        xp, np_ = _layer(xp, 0, None, zc, zp, zf, bias_by_dist, *wts)
        xs, ns_ = _layer(xs, PAST_LEN, (cache_k[l], cache_v[l]), state_conv[l], state_pool[l],
                         state_ffn[l], bias_by_dist, *wts)
        sp.append(np_)
        ss.append(ns_)
    st = lambda lst, i: jnp.stack([e[i] for e in lst], 0)
    return (xp, xs,
            st(sp, 0), st(sp, 1), st(sp, 2), st(sp, 3), st(sp, 4),
            st(ss, 0), st(ss, 1), st(ss, 2), st(ss, 3), st(ss, 4))
```

```python
import contextlib
import math

import numpy as np
import concourse.bass as bass
import concourse.mybir as mybir
from concourse.bass_utils import run_bass_kernel_spmd

F32 = mybir.dt.float32
BF16 = mybir.dt.bfloat16
ALU = mybir.AluOpType
AF = mybir.ActivationFunctionType
AX = mybir.AxisListType

D = 2048
NCH = 16
SEQ = 2048
H = 260
NM = 1024
NP = H + NM
NS = 4
NT = NP + NS
INW = 3584
DFF = 5632
NFT = 88
NHEAD = 16
WIN = 128
ALPHA = 4.0 ** 0.25
SCALE = 64 ** -0.5
EPS = 1e-5
NEG = -1e30
OQ, OK_, OV, OGB, OGC, OH_, OPIN = 0, 1024, 1280, 1536, 2048, 2560, 3072
JB = 11
NBATCH = 4


class _Ins:
    __slots__ = ("eng", "fn", "deps", "dma", "sig", "val", "semi", "pos", "fsz")

    def __init__(self, eng, fn, dma):
        self.eng = eng
        self.fn = fn
        self.deps = set()
        self.dma = dma
        self.sig = False
        self.val = 0
        self.semi = -1
        self.pos = 0
        self.fsz = 0


class _Rec:
    def __getattr__(self, name):
        return lambda *a, **k: (name, a, k)


_REC = _Rec()


class Prog:
    ENGS = ("pe", "act", "dve", "pool", "sp")
    NDMA = 8

    def __init__(self, nc):
        self.nc = nc
        self.ins = []
        self.lastw = {}
        self.readers = {}
        self.dma_n = {e: 0 for e in self.ENGS}
        self.dma_hist = {e: [] for e in self.ENGS}
        self.npos = {e: 0 for e in self.ENGS}

    def add(self, eng, fn, reads=(), writes=(), dma=False):
        idx = len(self.ins)
        ins = _Ins(eng, fn(_REC), dma)
        psr = [r for r in reads if isinstance(r, tuple) and r[0] == "ps"]
        if psr:
            writes = list(writes) + psr
        for r in reads:
            w = self.lastw.get(r)
            if w is not None:
                ins.deps.add(w)
        for r in writes:
            w = self.lastw.get(r)
            if w is not None:
                ins.deps.add(w)
            rl = self.readers.get(r)
            if rl:
                ins.deps.update(rl)
        for r in reads:
            self.readers.setdefault(r, []).append(idx)
        for r in writes:
            self.lastw[r] = idx
            self.readers[r] = []
        if dma:
            n = self.dma_n[eng]
            self.dma_n[eng] = n + 1
            ins.semi = n % self.NDMA
            ins.val = 16 * (n // self.NDMA + 1)
            hist = self.dma_hist[eng]
            if n >= self.NDMA:
                ins.deps.add(hist[n - self.NDMA])
            hist.append(idx)
        ins.deps.discard(idx)
        ins.pos = self.npos[eng]
        self.npos[eng] += 1
        try:
            o_ = ins.fn[2].get("out")
            ins.fsz = o_.free_size() if o_ is not None else ins.fn[1][0].free_size()
        except Exception:
            ins.fsz = 0
        self.ins.append(ins)
        return idx

    def pe(self, fn, reads=(), writes=()):
        return self.add("pe", fn, reads, writes)

    def act(self, fn, reads=(), writes=()):
        return self.add("act", fn, reads, writes)

    def dve(self, fn, reads=(), writes=()):
        return self.add("dve", fn, reads, writes)

    def dma(self, eng, fn, reads=(), writes=()):
        return self.add(eng, fn, reads, writes, dma=True)

    def emit(self, final_eng="sp"):
        nc = self.nc
        ins_all = list(self.ins)
        fin = _Ins(final_eng, None, False)
        last_by_eng = {}
        for i, ins in enumerate(ins_all):
            if ins.dma:
                fin.deps.add(i)
            last_by_eng[ins.eng] = i
        for e, i in last_by_eng.items():
            fin.deps.add(i)
        ins_all.append(fin)
        def self_hazard(ins, p):
            if ins.eng not in ("dve", "act", "pool") or p.dma or p.eng != ins.eng:
                return False
            return True

        for ins in ins_all:
            for d in ins.deps:
                p = ins_all[d]
                if (not p.dma) and (p.eng != ins.eng or self_hazard(ins, p)):
                    p.sig = True
        cnt = {e: 0 for e in self.ENGS}
        for ins in ins_all:
            if (not ins.dma) and ins.sig:
                cnt[ins.eng] += 1
                ins.val = cnt[ins.eng]
        per_eng = {e: [] for e in self.ENGS}
        for ins in ins_all:
            per_eng[ins.eng].append(ins)
        self.stats = {e: len(v) for e, v in per_eng.items()}

        with contextlib.ExitStack() as st:
            esem = {e: st.enter_context(nc.semaphore(f"es_{e}")) for e in self.ENGS}
            dsem = {
                e: [st.enter_context(nc.semaphore(f"ds_{e}{i}")) for i in range(self.NDMA)]
                for e in self.ENGS
                if self.dma_n[e] > 0
            }
            block = st.enter_context(nc.Block())

            def run(engname, engobj):
                waited = {}
                for ins in per_eng[engname]:
                    need = {}
                    for d in ins.deps:
                        p = ins_all[d]
                        if p.dma:
                            key = ("d", p.eng, p.semi)
                        elif p.eng != engname or self_hazard(ins, p):
                            key = ("e", p.eng)
                        else:
                            continue
                        if p.val > need.get(key, 0):
                            need[key] = p.val
                    for key, v in need.items():
                        if waited.get(key, 0) >= v:
                            continue
                        sem = dsem[key[1]][key[2]] if key[0] == "d" else esem[key[1]]
                        engobj.wait_ge(sem, v)
                        waited[key] = v
                    if ins.fn is None:
                        continue
                    name, a_, k_ = ins.fn
                    bi = getattr(engobj, name)(*a_, **k_)
                    if ins.dma:
                        bi.then_inc(dsem[engname][ins.semi], 16)
                    elif ins.sig:
                        bi.then_inc(esem[engname], 1)

            if per_eng["pe"]:
                block.tensor(lambda e: run("pe", e))
            if per_eng["act"]:
                block.scalar(lambda e: run("act", e))
            if per_eng["dve"]:
                block.vector(lambda e: run("dve", e))
            if per_eng["pool"]:
                block.gpsimd(lambda e: run("pool", e))
            if per_eng["sp"]:
                block.sync(lambda e: run("sp", e))


def groups(lo, hi, mx=512):
    n = hi - lo
    k = (n + mx - 1) // mx
    out = []
    base, rem = divmod(n, k)
    a = lo
    for i in range(k):
        b = a + base + (1 if i < rem else 0)
        out.append((a, b))
        a = b
    return out


class SB:
    def __init__(self, name, tile, col0=0):
        self.name, self.t, self.c0 = name, tile, col0

    def ap(self, c, lo, hi, p0=0, p1=128):
        return self.t[p0:p1, c, lo - self.c0:hi - self.c0]

    def res(self, c, lo, hi):
        return [(self.name, c, b) for b in range(lo // 128, (hi - 1) // 128 + 1)]


class _Stop(Exception):
    pass


def build(stop=None, dbg=False):
    def stage(name):
        nonlocal stop
        if stop == name:
            raise _Stop()

    nc = bass.Bass("TRN2", target_bir_lowering=False)

    def din(name, shape, dt=F32):
        return nc.dram_tensor(name, list(shape), dt, kind="ExternalInput").ap()

    def dout(name, shape):
        return nc.dram_tensor(name, list(shape), F32, kind="ExternalOutput").ap()

    xh = din("xh", [NP, D])
    xs = din("xs", [NS, D])
    ck = din("ck", [2, NS, 128, 256])
    cv = din("cv", [2, NS, 128, 256])
    s_conv = din("s_conv", [2, NS * 2, 512])
    s_pool = din("s_pool", [2, NS * 15, 512])
    s_ffn = din("s_ffn", [2, NS * 2, 2 * DFF])
    w_in = din("w_in", [2, D, INW])
    w_o = din("w_o", [2, D, D])
    w_up = din("w_up", [2, D, 2 * DFF])
    w_down = din("w_down", [2, DFF, D])
    conv_w = din("conv_w", [6, 512])
    ffn_cw = din("ffn_cw", [6, 2 * DFF])
    pool_w = din("pool_w", [2, 4, 128, 128])
    pool_scale = din("pool_scale", [2, 512])
    lnp = din("lnp", [8, D])
    sinks = din("sinks", [128, 32])
    tbl = din("tbl", [NHEAD, 128, 256])
    gmask_d = din("gmask", [128, 256])
    ident_d = din("ident", [128, 128])
    keymask_d = din("keymask", [128, 11])
    smask_d = din("smask", [128, 16])
    prat_d = din("prat", [128, 64])
    yp = dout("yp", [NM, D])
    ys = dout("ys", [NS, D])
    kp = dout("kp", [2, 128, 256])
    vp = dout("vp", [2, 128, 256])
    convp = dout("convp", [2, 2, 512])
    poolp = dout("poolp", [2, 15, 512])
    ffnp = dout("ffnp", [2, 2, 2 * DFF])
    ks = dout("ks", [2, NS, 128, 256])
    vs = dout("vs", [2, NS, 128, 256])
    convs = dout("convs", [2, NS * 2, 512])
    pools = dout("pools", [2, NS * 15, 512])
    ffns = dout("ffns", [2, NS * 2, 2 * DFF])

    if dbg:
        dbg_xT = dout("dbg_xT", [128, NCH * (NT - 128)])
        dbg_X = dout("dbg_X", [128, NCH * NT])
        dbg_R2a = dout("dbg_R2a", [128, 8 * (NT - 128)])
        dbg_R2b = dout("dbg_R2b", [128, 3 * NT])
    st = contextlib.ExitStack()
    with st:
        def sb(name, shape, dt=F32):
            return st.enter_context(nc.sbuf_tensor(name, list(shape), dt))

        xT_t = sb("xT", [128, NCH, NT - 128])
        X_t = sb("X", [128, NCH, NT], BF16)
        R2a_t = sb("R2a", [128, 8, NT - 128], BF16)
        R2b_t = sb("R2b", [128, 3, NT], BF16)
        Vt = sb("Vt", [128, 11, 4, 66], BF16)
        NSLAB = 5
        slabs = [sb(f"slab{i}", [128, 16, 128], BF16) for i in range(NSLAB)]
        NSCR = 8
        scr = [sb(f"scr{i}", [128, 512]) for i in range(NSCR)]
        Gt = sb("Gt", [128, 4, 256])
        gmask = sb("gmask_s", [128, 256])
        ident_f = sb("ident_f", [128, 128])
        ident_b = sb("ident_b", [128, 128], BF16)
        ones_b = sb("ones_b", [128, 128], BF16)
        ones_f = sb("ones_f", [128, 128])
        eps_t = sb("eps_t", [128, 2])
        rstd_t = [sb(f"rstd{i}", [128, 392]) for i in range(3)]
        lnp_s = sb("lnp_s", [128, NCH, 8])
        cw_s = sb("cw_s", [128, 4, 6])
        fcw_s = sb("fcw_s", [128, NFT, 6])
        psc_s = sb("psc_s", [128, 4, 2])
        pw_s = sb("pw_s", [128, 8, 128], BF16)
        es_s = sb("es_s", [128, 32])
        keymask = sb("keymask_s", [128, 11])
        smask = sb("smask_s", [128, 16])
        prat = sb("prat_s", [128, 64])
        stc = sb("stc", [128, 2, 4, 8])
        stp = sb("stp", [128, 2, 4, 60])
        stf = sb("stf", [128, NFT, 8])
        fst = sb("fst", [128, 2 * JB, 6])
        kcT = sb("kcT", [128, NS, 2, 128], BF16)
        veff = sb("veff", [128, NS, 4, 64], BF16)
        cst = sb("cst", [128, 4, 10])
        pst = sb("pst", [128, 4, 75])
        sbias = sb("sbias", [128, 4])
        small = sb("small", [128, 64])

        PSB = [st.enter_context(nc.psum_tensor(f"ps{i}", [128, 512], F32)) for i in range(8)]

        P = Prog(nc)
        xT = SB("xT", xT_t, 128)
        X = SB("X", X_t, 0)

        class R2cls:
            def ap(self, c, lo, hi, p0=0, p1=128):
                if c < 8:
                    return R2a_t[p0:p1, c, lo - 128:hi - 128]
                return R2b_t[p0:p1, c - 8, lo:hi]

            def res(self, c, lo, hi):
                return [("R2", c, b) for b in range(lo // 128, (hi - 1) // 128 + 1)]

        R2 = R2cls()

        state = {"ps": 0, "scr": 0, "slab": 0}

        def ps_next():
            i = state["ps"] % 8
            state["ps"] += 1
            return PSB[i], ("ps", i)

        def scr_next():
            i = state["scr"] % NSCR
            state["scr"] += 1
            return scr[i], ("scr", i)

        def slab_next():
            i = state["slab"] % NSLAB
            state["slab"] += 1
            return slabs[i], ("slab", i)

        def load_slab(pieces, nk):
            sl, sres = slab_next()
            for src, c0 in pieces:
                ncols = src.shape[1]
                P.dma("pool", lambda e, sl=sl, src=src, c0=c0, ncols=ncols: e.dma_start(
                    out=sl[:, 0:nk, c0:c0 + ncols], in_=src.rearrange("(k p) c -> p k c", p=128)),
                    writes=[sres])
            return sl, sres

        def mm_group(ps, psres, sl, sres, nk, rhs, lo, hi, m0=0, m1=128):
            for k in range(nk):
                rap, rres = rhs(k, lo, hi)
                P.pe(lambda e, k=k, rap=rap: e.matmul(ps[m0:m1, 0:hi - lo], lhsT=sl[:, k, m0:m1], rhs=rap,
                                                       start=(k == 0), stop=(k == nk - 1)),
                     reads=[sres] + rres, writes=[psres])

        def rhsX(k, lo, hi):
            return X.ap(k, lo, hi), X.res(k, lo, hi)

        def rhsR2(k, lo, hi):
            return R2.ap(k, lo, hi), R2.res(k, lo, hi)

        def load_T(src, R, C, dst_fn, dres, q="sp"):
            per = max(1, min(512 // R, 4))
            pc = 0
            while pc < C:
                w = min(512, C - pc)
                stg, sres = scr_next()
                P.dma(q, lambda e, stg=stg, pc=pc, w=w: e.dma_start(out=stg[0:R, 0:w], in_=src[:, pc:pc + w]),
                      writes=[sres])
                nchunk = w // 128
                c = 0
                while c < nchunk:
                    n = min(per, nchunk - c)
                    ps, psres = ps_next()
                    for i in range(n):
                        P.pe(lambda e, ps=ps, stg=stg, c=c, i=i: e.transpose(
                            out=ps[:, i * R:(i + 1) * R], in_=stg[0:R, (c + i) * 128:(c + i + 1) * 128],
                            identity=ident_f[0:R, 0:R]), reads=[sres, "ident_f"], writes=[psres])
                    dst = dst_fn(pc // 128 + c, n)
                    P.act(lambda e, ps=ps, dst=dst, n=n: e.activation(
                        out=dst, in_=ps[:, 0:n * R].rearrange("p (n r) -> p n r", r=R), func=AF.Copy),
                        reads=[psres], writes=[dres])
                    c += n
                pc += w

        try:
            def simple_load(tile_ap, src, res, q="sp"):
                P.dma(q, lambda e: e.dma_start(out=tile_ap, in_=src), writes=[res])

            simple_load(ident_f[:, :], ident_d, "ident_f")
            simple_load(gmask[:, :], gmask_d, "gmask")
            simple_load(keymask[:, :], keymask_d, "keymask")
            simple_load(smask[:, :], smask_d, "smask")
            simple_load(prat[:, :], prat_d, "prat")
            simple_load(es_s[:, :], sinks, "es")
            P.dma("pool", lambda e: e.dma_start(out=pw_s[:, :, :], in_=pool_w.rearrange("l g k m -> k (l g) m")),
                  writes=["pw"])
            P.act(lambda e: e.activation(out=es_s[:, :], in_=es_s[:, :], func=AF.Exp), reads=["es"], writes=["es"])
            P.dve(lambda e: e.tensor_copy(out=ident_b[:, :], in_=ident_f[:, :]), reads=["ident_f"], writes=["ident_b"])
            P.dve(lambda e: e.memset(ones_b[:, :], 1.0), writes=["ones_b"])
            P.dve(lambda e: e.memset(ones_f[:, :], 1.0), writes=["ones_f"])
            P.dve(lambda e: e.memset(eps_t[:, :], EPS), writes=["eps"])
            P.dve(lambda e: e.memset(Vt[:, :, :, 64:66], 1.0), writes=["Vt_ones"])
            P.dve(lambda e: e.memset(Vt[:, :, :, 0:64], 0.0), writes=[("Vt", b) for b in range(11)])

            stage('s0')
            load_T(lnp, 8, D, lambda c, n: lnp_s[:, c:c + n, :], "lnp")
            load_T(conv_w, 6, 512, lambda c, n: cw_s[:, c:c + n, :], "cw")
            load_T(ffn_cw, 6, 2 * DFF, lambda c, n: fcw_s[:, c:c + n, :], "fcw")
            load_T(pool_scale, 2, 512, lambda c, n: psc_s[:, c:c + n, :], "psc")
            for l in range(2):
                load_T(s_conv[l], 8, 512, lambda c, n, l=l: stc[:, l, c:c + n, :], "stc")
                load_T(s_pool[l], 60, 512, lambda c, n, l=l: stp[:, l, c:c + n, :], "stp")

            stage('s1')
            def load_x_block(src, ntok, col0):
                for hf in range(4):
                    stg, sres = scr_next()
                    P.dma("sp" if hf % 2 == 0 else "pool", lambda e, stg=stg, hf=hf: e.dma_start(out=stg[0:ntok, :], in_=src[:, hf * 512:(hf + 1) * 512]),
                          writes=[sres])
                    ps, psres = ps_next()
                    for i in range(4):
                        P.pe(lambda e, ps=ps, stg=stg, i=i: e.transpose(
                            out=ps[:, i * 128:i * 128 + ntok], in_=stg[0:ntok, i * 128:(i + 1) * 128],
                            identity=ident_f[0:ntok, 0:ntok]), reads=[sres, "ident_f"], writes=[psres])
                    c0 = hf * 4
                    src_ps = ps[:, :].rearrange("p (n r) -> p n r", r=128)[:, :, 0:ntok]
                    wr = []
                    for c in range(c0, c0 + 4):
                        wr += X.res(c, col0, col0 + ntok)
                    P.act(lambda e, src_ps=src_ps, c0=c0: e.activation(
                        out=X_t[:, c0:c0 + 4, col0:col0 + ntok], in_=src_ps, func=AF.Copy), reads=[psres], writes=wr)
                    if col0 >= 128:
                        wr = []
                        for c in range(c0, c0 + 4):
                            wr += xT.res(c, col0, col0 + ntok)
                        P.dve(lambda e, src_ps=src_ps, c0=c0: e.tensor_copy(
                            out=xT_t[:, c0:c0 + 4, col0 - 128:col0 - 128 + ntok], in_=src_ps), reads=[psres], writes=wr)

            for b in range(10):
                load_x_block(xh[b * 128:(b + 1) * 128, :], 128, b * 128)
            load_x_block(xh[1280:1284, :], 4, 1280)
            load_x_block(xs[:, :], NS, NP)

        except _Stop:
            stop = '__done__'
        def layer(l):
            RIN = (0 if l == 0 else 128, NT)
            RMX = (128 if l == 0 else 256, NT)
            RPRE = (RMX[0] - 16, NT)

            def lnvec(c, which):
                return lnp_s[:, c, l * 4 + which:l * 4 + which + 1]

            load_T(s_ffn[l], 8, 2 * DFF, lambda c, n: stf[:, c:c + n, :], "stf")
            for s in range(NS):
                stg, sres = scr_next()
                P.dma("sp", lambda e, stg=stg, s=s: e.dma_start(out=stg[:, 0:256], in_=ck[l, s]), writes=[sres])
                ps, psres = ps_next()
                for c in range(2):
                    P.pe(lambda e, ps=ps, stg=stg, c=c: e.transpose(
                        out=ps[:, c * 128:(c + 1) * 128], in_=stg[:, c * 128:(c + 1) * 128], identity=ident_f[:, :]),
                        reads=[sres, "ident_f"], writes=[psres])
                P.act(lambda e, ps=ps, s=s: e.activation(
                    out=kcT[:, s, :, :], in_=ps[:, 0:256].rearrange("p (c k) -> p c k", c=2), func=AF.Copy),
                    reads=[psres], writes=[("kcT", s)])
                P.dma("pool", lambda e, s=s: e.dma_start(
                    out=veff[:, s, :, :], in_=cv[l, s].rearrange("k (h d) -> k h d", h=4)), writes=[("veff", s)])
                P.dma("sp", lambda e, s=s: e.dma_start(out=ks[l, s, 0:127, :], in_=ck[l, s, 1:128, :]),
                      writes=[("o_ks", l, s, 0)])
                P.dma("sp", lambda e, s=s: e.dma_start(out=vs[l, s, 0:127, :], in_=cv[l, s, 1:128, :]),
                      writes=[("o_vs", l, s, 0)])

            def proj_tile(pieces, rng, rhs=rhsX, nk=16):
                sl, sres = load_slab(pieces, nk)
                outs = []
                for (lo, hi) in groups(*rng):
                    ps, psres = ps_next()
                    mm_group(ps, psres, sl, sres, nk, rhs, lo, hi)
                    outs.append((ps, psres, lo, hi))
                return outs

            def wcols(w, c0, n=128, r0=0, r1=D):
                return w[l, r0:r1, c0:c0 + n]

            for j in range(4):
                slg = load_slab([(wcols(w_in, OGB + 128 * j), 0)], 16)
                slc = load_slab([(wcols(w_in, OGC + 128 * j), 0)], 16)
                slh = load_slab([(wcols(w_in, OH_ + 128 * j), 0)], 16)
                w0 = cw_s[:, j, l * 3 + 0:l * 3 + 1]
                w1 = cw_s[:, j, l * 3 + 1:l * 3 + 2]
                w2 = cw_s[:, j, l * 3 + 2:l * 3 + 3]
                prev = None
                for (lo, hi) in groups(*RPRE):
                    psg, rg = ps_next()
                    mm_group(psg, rg, slg[0], slg[1], 16, rhsX, lo, hi)
                    psc, rc = ps_next()
                    mm_group(psc, rc, slc[0], slc[1], 16, rhsX, lo, hi)
                    psh, rh = ps_next()
                    mm_group(psh, rh, slh[0], slh[1], 16, rhsX, lo, hi)
                    n = hi - lo
                    gbt, gbr = scr_next()
                    gct, gcr = scr_next()
                    ut, ur = scr_next()
                    cvt, cvr = scr_next()
                    P.act(lambda e, gbt=gbt, psg=psg, n=n: e.activation(out=gbt[:, 0:n], in_=psg[:, 0:n], func=AF.Copy),
                          reads=[rg], writes=[gbr])
                    P.act(lambda e, gct=gct, psc=psc, n=n: e.activation(out=gct[:, 0:n], in_=psc[:, 0:n], func=AF.Copy),
                          reads=[rc], writes=[gcr])
                    P.dve(lambda e, ut=ut, gct=gct, psh=psh, n=n: e.tensor_tensor(
                        out=ut[:, 2:2 + n], in0=gct[:, 0:n], in1=psh[:, 0:n], op=ALU.mult), reads=[gcr, rh], writes=[ur])
                    if prev is None:
                        P.dve(lambda e, ut=ut: e.memset(ut[:, 0:2], 0.0), writes=[ur])
                    else:
                        put, pur, pn = prev
                        P.dve(lambda e, ut=ut, put=put, pn=pn: e.tensor_copy(out=ut[:, 0:2], in_=put[:, pn:pn + 2]),
                              reads=[pur], writes=[ur])
                    if lo <= H - 4 and H <= hi:
                        P.dve(lambda e, ut=ut, lo=lo: e.tensor_tensor(
                            out=ut[:, 2 + H - 4 - lo:2 + H - lo], in0=ut[:, 2 + H - 4 - lo:2 + H - lo],
                            in1=smask[:, 0:4], op=ALU.mult), reads=["smask", ur], writes=[ur])
                    prev = (ut, ur, n)
                    phi = min(hi, NP)
                    a = max(lo, RMX[0])
                    if phi > a:
                        m = phi - a
                        o = a - lo
                        P.act(lambda e, cvt=cvt, ut=ut, o=o, m=m: e.activation(
                            out=cvt[:, 0:m], in_=ut[:, o:o + m], func=AF.Copy, scale=w0), reads=[ur, "cw"], writes=[cvr])
                        P.dve(lambda e, cvt=cvt, ut=ut, o=o, m=m: e.scalar_tensor_tensor(
                            out=cvt[:, 0:m], in0=ut[:, o + 1:o + 1 + m], scalar=w1, in1=cvt[:, 0:m],
                            op0=ALU.mult, op1=ALU.add), reads=[ur, "cw", cvr], writes=[cvr])
                        P.dve(lambda e, cvt=cvt, ut=ut, o=o, m=m: e.scalar_tensor_tensor(
                            out=cvt[:, 0:m], in0=ut[:, o + 2:o + 2 + m], scalar=w2, in1=cvt[:, 0:m],
                            op0=ALU.mult, op1=ALU.add), reads=[ur, "cw", cvr], writes=[cvr])
                        P.dve(lambda e, cvt=cvt, gbt=gbt, o=o, m=m, a=a, j=j: e.tensor_tensor(
                            out=R2.ap(j, a, a + m), in0=gbt[:, o:o + m], in1=cvt[:, 0:m], op=ALU.mult),
                            reads=[gbr, cvr], writes=R2.res(j, a, a + m))
                    if hi == NT:
                        so = NP - lo
                        st0 = stc[:, l, j, :].rearrange("p (s r) -> p r s", r=2)[:, 0, :]
                        st1 = stc[:, l, j, :].rearrange("p (s r) -> p r s", r=2)[:, 1, :]
                        cs = small[:, 0:4]
                        P.dve(lambda e, cs=cs, st0=st0: e.tensor_scalar(out=cs, in0=st0, scalar1=w0, scalar2=None, op0=ALU.mult),
                              reads=["stc", "cw"], writes=["small"])
                        P.dve(lambda e, cs=cs, st1=st1: e.scalar_tensor_tensor(
                            out=cs, in0=st1, scalar=w1, in1=cs, op0=ALU.mult, op1=ALU.add), reads=["stc", "cw", "small"], writes=["small"])
                        P.dve(lambda e, cs=cs, ut=ut, so=so: e.scalar_tensor_tensor(
                            out=cs, in0=ut[:, 2 + so:2 + so + 4], scalar=w2, in1=cs, op0=ALU.mult, op1=ALU.add),
                            reads=[ur, "cw", "small"], writes=["small"])
                        P.dve(lambda e, cs=cs, gbt=gbt, so=so, j=j: e.tensor_tensor(
                            out=R2.ap(j, NP, NT), in0=gbt[:, so:so + 4], in1=cs, op=ALU.mult),
                            reads=[gbr, "small"], writes=R2.res(j, NP, NT))
                        P.dve(lambda e, ut=ut, so=so, j=j: e.tensor_copy(out=cst[:, j, 0:2], in_=ut[:, so:so + 2]),
                              reads=[ur], writes=[("cst", j)])
                        cv_ = cst[:, j, 2:10].rearrange("p (s r) -> p r s", r=2)
                        P.dve(lambda e, cv_=cv_, st1=st1: e.tensor_copy(out=cv_[:, 0, :], in_=st1), reads=["stc"], writes=[("cst", j)])
                        P.dve(lambda e, cv_=cv_, ut=ut, so=so: e.tensor_copy(out=cv_[:, 1, :], in_=ut[:, 2 + so:2 + so + 4]),
                              reads=[ur], writes=[("cst", j)])
            ps, psres = ps_next()
            for j in range(4):
                P.pe(lambda e, ps=ps, j=j: e.transpose(out=ps[0:10, j * 128:(j + 1) * 128], in_=cst[:, j, :], identity=ident_f[:, :]),
                     reads=[("cst", j), "ident_f"], writes=[psres])
            ot, otr = scr_next()
            P.act(lambda e, ot=ot, ps=ps: e.activation(out=ot[0:10, :], in_=ps[0:10, :], func=AF.Copy), reads=[psres], writes=[otr])
            P.dma("sp", lambda e, ot=ot: e.dma_start(out=convp[l], in_=ot[0:2, :]), reads=[otr], writes=[("o_convp", l)])
            P.dma("sp", lambda e, ot=ot: e.dma_start(out=convs[l], in_=ot[2:10, :]), reads=[otr], writes=[("o_convs", l)])

            stage(f'L{l}_conv')
            tps = {}
            prevp_d = {}

            def pool_body(g, gi):
                wsz = 2 << g
                tp = tps[g]
                psp, rp, lo, hi = tp[gi]
                n = hi - lo
                a0, a0r = scr_next()
                P.act(lambda e, a0=a0, psp=psp, n=n: e.activation(out=a0[:, 16:16 + n], in_=psp[:, 0:n], func=AF.Copy),
                      reads=[rp], writes=[a0r])
                prevp = prevp_d.get(g)
                if prevp is None:
                    P.dve(lambda e, a0=a0: e.memset(a0[:, 0:16], 0.0), writes=[a0r])
                else:
                    pa, par, pn = prevp
                    P.dve(lambda e, a0=a0, pa=pa, pn=pn: e.tensor_copy(out=a0[:, 0:16], in_=pa[:, pn:pn + 16]),
                          reads=[par], writes=[a0r])
                if lo <= H - 16 and H <= hi:
                    o = 16 + H - 16 - lo
                    P.dve(lambda e, a0=a0, o=o: e.tensor_tensor(out=a0[:, o:o + 16], in0=a0[:, o:o + 16], in1=smask[:, :], op=ALU.mult),
                          reads=["smask", a0r], writes=[a0r])
                prevp_d[g] = (a0, a0r, n)
                yield
                phi = min(hi, NP)
                a = max(lo, RMX[0])
                if phi > a:
                    m = phi - a
                    o = 16 + a - lo
                    ta, tar = scr_next()
                    tb, tbr = scr_next()
                    cur, curr, ext = a0, a0r, 15
                    bufs = [(ta, tar), (tb, tbr)]
                    sh = 1
                    lvl = 0
                    while sh < wsz:
                        nxt, nxtr = bufs[lvl % 2]
                        ext2 = ext - sh
                        P.dve(lambda e, nxt=nxt, cur=cur, o=o, m=m, ext2=ext2, sh=sh: e.tensor_tensor(
                            out=nxt[:, o - ext2:o + m], in0=cur[:, o - ext2:o + m], in1=cur[:, o - ext2 - sh:o + m - sh], op=ALU.add),
                            reads=[curr], writes=[nxtr])
                        cur, curr, ext = nxt, nxtr, ext2
                        yield
                        sh *= 2
                        lvl += 1
                    if a <= H and H + 16 <= phi:
                        oo = o + H - a
                        P.dve(lambda e, cur=cur, oo=oo, g=g: e.tensor_tensor(
                            out=cur[:, oo:oo + 16], in0=cur[:, oo:oo + 16], in1=prat[:, g * 16:(g + 1) * 16], op=ALU.mult),
                            reads=["prat", curr], writes=[curr])
                    dt_, dtr = bufs[lvl % 2]
                    dtb = dt_[:, :].bitcast(BF16)
                    P.dve(lambda e, dtb=dtb, cur=cur, a0=a0, o=o, m=m, wsz=wsz: e.scalar_tensor_tensor(
                        out=dtb[:, 0:m], in0=cur[:, o:o + m], scalar=1.0 / wsz, in1=a0[:, o:o + m], op0=ALU.mult, op1=ALU.subtract),
                        reads=[curr, a0r], writes=[dtr])
                    yield
                    ps2, ps2r = ps_next()
                    P.pe(lambda e, ps2=ps2, dtb=dtb, m=m, g=g: e.matmul(ps2[:, 0:m], lhsT=pw_s[:, l * 4 + g, :], rhs=dtb[:, 0:m], start=True, stop=True),
                         reads=[dtr, "pw"], writes=[ps2r])
                    P.act(lambda e, ps2=ps2, m=m, a=a, g=g: e.activation(
                        out=R2.ap(4 + g, a, a + m), in_=ps2[:, 0:m], func=AF.Copy, scale=psc_s[:, g, l:l + 1]),
                        reads=[ps2r, "psc"], writes=R2.res(4 + g, a, a + m))
                if hi == NT:
                    so = 16 + NP - lo
                    P.dve(lambda e, a0=a0, so=so, g=g: e.tensor_copy(out=pst[:, g, 0:15], in_=a0[:, so - 15:so]),
                          reads=[a0r], writes=[("pst", g)])
                    pv = pst[:, g, 15:75].rearrange("p (s r) -> p s r", r=15)
                    sv = stp[:, l, g, :].rearrange("p (s r) -> p s r", r=15)
                    P.dve(lambda e, pv=pv, sv=sv: e.tensor_copy(out=pv[:, :, 0:14], in_=sv[:, :, 1:15]), reads=["stp"], writes=[("pst", g)])
                    P.dve(lambda e, pv=pv, a0=a0, so=so: e.tensor_copy(out=pv[:, :, 14], in_=a0[:, so:so + 4]),
                          reads=[a0r], writes=[("pst", g)])
                    ws_ = small[:, 8:12]
                    if wsz == 16:
                        P.dve(lambda e, ws_=ws_, sv=sv: e.tensor_reduce(out=ws_, in_=sv[:, :, 0:15], axis=AX.X, op=ALU.add),
                              reads=["stp"], writes=["small"])
                    else:
                        P.dve(lambda e, ws_=ws_, sv=sv, wsz=wsz: e.tensor_reduce(out=ws_, in_=sv[:, :, 16 - wsz:15], axis=AX.X, op=ALU.add),
                              reads=["stp"], writes=["small"])
                    P.dve(lambda e, ws_=ws_, a0=a0, so=so: e.tensor_tensor(out=ws_, in0=ws_, in1=a0[:, so:so + 4], op=ALU.add),
                          reads=[a0r, "small"], writes=["small"])
                    dsb = small[:, 16:20].bitcast(BF16)
                    P.dve(lambda e, ws_=ws_, a0=a0, so=so, dsb=dsb, wsz=wsz: e.scalar_tensor_tensor(
                        out=dsb[:, 0:4], in0=ws_, scalar=1.0 / wsz, in1=a0[:, so:so + 4], op0=ALU.mult, op1=ALU.subtract),
                        reads=[a0r, "small"], writes=["small"])
                    ps2, ps2r = ps_next()
                    P.pe(lambda e, ps2=ps2, dsb=dsb, g=g: e.matmul(ps2[:, 0:4], lhsT=pw_s[:, l * 4 + g, :], rhs=dsb[:, 0:4], start=True, stop=True),
                         reads=["small", "pw"], writes=[ps2r])
                    P.act(lambda e, ps2=ps2, g=g: e.activation(
                        out=R2.ap(4 + g, NP, NT), in_=ps2[:, 0:4], func=AF.Copy, scale=psc_s[:, g, l:l + 1]),
                        reads=[ps2r, "psc"], writes=R2.res(4 + g, NP, NT))

            for gp in (0, 2):
                tps[gp] = proj_tile([(wcols(w_in, OPIN + 128 * gp), 0)], RPRE)
                tps[gp + 1] = proj_tile([(wcols(w_in, OPIN + 128 * (gp + 1)), 0)], RPRE)
                for gi in range(len(tps[gp])):
                    gens = [pool_body(gp, gi), pool_body(gp + 1, gi)]
                    while gens:
                        for gen in list(gens):
                            try:
                                next(gen)
                            except StopIteration:
                                gens.remove(gen)
            ps, psres = ps_next()
            for g in range(4):
                P.pe(lambda e, ps=ps, g=g: e.transpose(out=ps[0:75, g * 128:(g + 1) * 128], in_=pst[:, g, :], identity=ident_f[:, :]),
                     reads=[("pst", g), "ident_f"], writes=[psres])
            ot, otr = scr_next()
            P.act(lambda e, ot=ot, ps=ps: e.activation(out=ot[0:75, :], in_=ps[0:75, :], func=AF.Copy), reads=[psres], writes=[otr])
            P.dma("sp", lambda e, ot=ot: e.dma_start(out=poolp[l], in_=ot[0:15, :]), reads=[otr], writes=[("o_poolp", l)])
            P.dma("sp", lambda e, ot=ot: e.dma_start(out=pools[l], in_=ot[15:75, :]), reads=[otr], writes=[("o_pools", l)])

            stage(f'L{l}_pool')
            for m in range(NCH):
                sl, sres = load_slab([(w_o[l, 1024:2048, m * 128:(m + 1) * 128], 0)], 8)
                for (lo, hi) in groups(*RMX):
                    ps, psres = ps_next()
                    mm_group(ps, psres, sl, sres, 8, rhsR2, lo, hi)
                    P.dve(lambda e, ps=ps, lo=lo, hi=hi, m=m: e.scalar_tensor_tensor(
                        out=xT.ap(m, lo, hi), in0=xT.ap(m, lo, hi), scalar=ALPHA, in1=ps[:, 0:hi - lo], op0=ALU.mult, op1=ALU.add),
                        reads=[psres] + xT.res(m, lo, hi), writes=xT.res(m, lo, hi))

            stage(f'L{l}_wo1')
            qhead = {}
            for t in range(8):
                gA = 2 * (t // 4)
                hA = 4 * gA + t % 4
                hB = 4 * (gA + 1) + t % 4
                qhead[hA] = (t, 0)
                qhead[hB] = (t, 64)
                outs = proj_tile([(wcols(w_in, OQ + 64 * hA, 64), 0), (wcols(w_in, OQ + 64 * hB, 64), 64)], RMX)
                for (ps, psres, lo, hi) in outs:
                    P.act(lambda e, ps=ps, lo=lo, hi=hi, t=t: e.activation(out=R2.ap(t, lo, hi), in_=ps[:, 0:hi - lo], func=AF.Copy),
                          reads=[psres], writes=R2.res(t, lo, hi))
            for c in range(2):
                sl, sres = load_slab([(wcols(w_in, OK_ + 128 * c), 0)], 16)
                for (lo, hi) in groups(*RIN):
                    ps, psres = ps_next()
                    mm_group(ps, psres, sl, sres, 16, rhsX, lo, hi)
                    P.act(lambda e, ps=ps, lo=lo, hi=hi, c=c: e.activation(out=R2.ap(8 + c, lo, hi), in_=ps[:, 0:hi - lo], func=AF.Copy),
                          reads=[psres], writes=R2.res(8 + c, lo, hi))
                ps, psres = ps_next()
                for k in range(16):
                    P.pe(lambda e, ps=ps, k=k, sl=sl: e.matmul(ps[:, 0:128], lhsT=X.ap(k, NP - 128, NP), rhs=sl[:, k, :], start=(k == 0), stop=(k == 15)),
                         reads=[sres] + X.res(k, NP - 128, NP), writes=[psres])
                ot, otr = scr_next()
                P.act(lambda e, ps=ps, ot=ot: e.activation(out=ot[:, 0:128], in_=ps[:, 0:128], func=AF.Copy), reads=[psres], writes=[otr])
                P.dma("sp", lambda e, ot=ot, c=c: e.dma_start(out=kp[l, :, c * 128:(c + 1) * 128], in_=ot[:, 0:128]), reads=[otr], writes=[("o_kp", l, c)])
                ps, psres = ps_next()
                for k in range(16):
                    P.pe(lambda e, ps=ps, k=k, sl=sl: e.matmul(ps[0:NS, 0:128], lhsT=X.ap(k, NP, NT), rhs=sl[:, k, :], start=(k == 0), stop=(k == 15)),
                         reads=[sres] + X.res(k, NP, NT), writes=[psres])
                ot, otr = scr_next()
                P.act(lambda e, ps=ps, ot=ot: e.activation(out=ot[0:NS, 0:128], in_=ps[0:NS, 0:128], func=AF.Copy), reads=[psres], writes=[otr])
                P.dma("sp", lambda e, ot=ot, c=c: e.dma_start(out=ks[l, :, 127, c * 128:(c + 1) * 128], in_=ot[0:NS, 0:128]), reads=[otr],
                      writes=[("o_ks", l, 9, 1 + c)])
            for s in range(NS):
                P.dve(lambda e, s=s: e.tensor_copy(out=kcT[:, s, :, 0], in_=R2b_t[:, 0:2, NP + s]),
                      reads=R2.res(8, NP, NT) + R2.res(9, NP, NT), writes=[("kcT", s)])
            blocks = [(b * 128, min((b + 1) * 128, NT)) for b in range(RIN[0] // 128, 11)]
            for c in range(2):
                sl, sres = load_slab([(wcols(w_in, OV + 128 * c), 0)], 16)
                for bi in range(0, len(blocks), 4):
                    ps, psres = ps_next()
                    blk = blocks[bi:bi + 4]
                    for i, (lo, hi) in enumerate(blk):
                        for k in range(16):
                            P.pe(lambda e, ps=ps, k=k, sl=sl, lo=lo, hi=hi, i=i: e.matmul(
                                ps[0:hi - lo, i * 128:(i + 1) * 128], lhsT=X.ap(k, lo, hi), rhs=sl[:, k, :], start=(k == 0), stop=(k == 15)),
                                reads=[sres] + X.res(k, lo, hi), writes=[psres])
                    b0 = blk[0][0] // 128
                    nb = len(blk)
                    full = [x for x in blk if x[1] - x[0] == 128]
                    nf = len(full)
                    if nf:
                        P.act(lambda e, ps=ps, b0=b0, nf=nf, c=c: e.activation(
                            out=Vt[:, b0:b0 + nf, 2 * c:2 * c + 2, 0:64],
                            in_=ps[:, 0:nf * 128].rearrange("p (b h d) -> p b h d", h=2, d=64), func=AF.Copy),
                            reads=[psres], writes=[("Vt", b) for b in range(b0, b0 + nf)])
                    if nf < nb:
                        lo, hi = blk[-1]
                        n = hi - lo
                        P.act(lambda e, ps=ps, b0=b0, nf=nf, c=c, n=n: e.activation(
                            out=Vt[0:n, b0 + nf, 2 * c:2 * c + 2, 0:64],
                            in_=ps[0:n, nf * 128:(nf + 1) * 128].rearrange("p (h d) -> p h d", h=2), func=AF.Copy),
                            reads=[psres], writes=[("Vt", b0 + nf)])
                ps, psres = ps_next()
                for k in range(16):
                    P.pe(lambda e, ps=ps, k=k, sl=sl: e.matmul(ps[:, 0:128], lhsT=X.ap(k, NP - 128, NP), rhs=sl[:, k, :], start=(k == 0), stop=(k == 15)),
                         reads=[sres] + X.res(k, NP - 128, NP), writes=[psres])
                ot, otr = scr_next()
                P.act(lambda e, ps=ps, ot=ot: e.activation(out=ot[:, 0:128], in_=ps[:, 0:128], func=AF.Copy), reads=[psres], writes=[otr])
                P.dma("sp", lambda e, ot=ot, c=c: e.dma_start(out=vp[l, :, c * 128:(c + 1) * 128], in_=ot[:, 0:128]), reads=[otr], writes=[("o_vp", l, c)])
                ps, psres = ps_next()
                for k in range(16):
                    P.pe(lambda e, ps=ps, k=k, sl=sl: e.matmul(ps[0:NS, 0:128], lhsT=X.ap(k, NP, NT), rhs=sl[:, k, :], start=(k == 0), stop=(k == 15)),
                         reads=[sres] + X.res(k, NP, NT), writes=[psres])
                ot, otr = scr_next()
                P.act(lambda e, ps=ps, ot=ot: e.activation(out=ot[0:NS, 0:128], in_=ps[0:NS, 0:128], func=AF.Copy), reads=[psres], writes=[otr])
                P.dma("sp", lambda e, ot=ot, c=c: e.dma_start(out=vs[l, :, 127, c * 128:(c + 1) * 128], in_=ot[0:NS, 0:128]), reads=[otr],
                      writes=[("o_vs", l, 9, 1 + c)])
                for s in range(NS):
                    ps, psres = ps_next()
                    for k in range(16):
                        P.pe(lambda e, ps=ps, k=k, sl=sl, s=s: e.matmul(ps[0:1, 0:128], lhsT=X.ap(k, NP + s, NP + s + 1), rhs=sl[:, k, :],
                                                                          start=(k == 0), stop=(k == 15)),
                             reads=[sres] + X.res(k, NP, NT), writes=[psres])
                    P.act(lambda e, ps=ps, s=s, c=c: e.activation(
                        out=veff[0:1, s, 2 * c:2 * c + 2, :], in_=ps[0:1, 0:128].rearrange("p (h d) -> p h d", h=2), func=AF.Copy),
                        reads=[psres], writes=[("veff", s)])

            stage(f'L{l}_qkv')
            qblocks = [(q0, min(q0 + 128, NP)) for q0 in range(RMX[0], NP, 128)]
            DEPTH = 4
            for kvh in range(4):
                P.dma("sp", lambda e, kvh=kvh: e.dma_start(out=Gt[:, :, :], in_=tbl[4 * kvh:4 * kvh + 4].rearrange("h k q -> k h q")),
                      writes=["Gt"])
                for j in range(4):
                    P.dve(lambda e, j=j: e.tensor_tensor(out=Gt[:, j, :], in0=Gt[:, j, :], in1=gmask[:, :], op=ALU.add),
                          reads=["Gt", "gmask"], writes=["Gt"])
                kc = 8 + kvh // 2
                kb = 64 * (kvh % 2)
                pairs = [(bi, j) for bi in range(len(qblocks)) for j in range(4)]
                pinfo = {}
                binfo = {}
                pendC2 = []

                def stageA(pi):
                    bi, j = pairs[pi]
                    q0, q1 = qblocks[bi]
                    nq = q1 - q0
                    blk = q0 // 128
                    h = 4 * kvh + j
                    qt, qb = qhead[h]
                    assert qb == kb
                    pss, pssr = ps_next()
                    P.pe(lambda e: e.matmul(pss[:, 0:nq], lhsT=R2.ap(kc, q0 - 128, q0, kb, kb + 64), rhs=R2.ap(qt, q0, q1, kb, kb + 64),
                                            start=True, stop=True),
                         reads=R2.res(kc, q0 - 128, q0) + R2.res(qt, q0, q1), writes=[pssr])
                    P.pe(lambda e: e.matmul(pss[0:nq, 128:128 + nq], lhsT=R2.ap(kc, q0, q1, kb, kb + 64), rhs=R2.ap(qt, q0, q1, kb, kb + 64),
                                            start=True, stop=True),
                         reads=R2.res(kc, q0, q1) + R2.res(qt, q0, q1), writes=[pssr])
                    sbt, sbr = scr_next()
                    ptb = sbt[:, :].bitcast(BF16)[:, 512:768]
                    if nq == 128:
                        P.dve(lambda e: e.scalar_tensor_tensor(out=sbt[:, 0:256], in0=pss[:, 0:256], scalar=SCALE, in1=Gt[:, j, :],
                                                               op0=ALU.mult, op1=ALU.add), reads=[pssr, "Gt"], writes=[sbr])
                    else:
                        P.dve(lambda e: e.scalar_tensor_tensor(out=sbt[:, 0:nq], in0=pss[:, 0:nq], scalar=SCALE, in1=Gt[:, j, 0:nq],
                                                               op0=ALU.mult, op1=ALU.add), reads=[pssr, "Gt"], writes=[sbr])
                        P.dve(lambda e: e.scalar_tensor_tensor(out=sbt[0:nq, 128:128 + nq], in0=pss[0:nq, 128:128 + nq], scalar=SCALE,
                                                               in1=Gt[0:nq, j, 128:128 + nq], op0=ALU.mult, op1=ALU.add),
                              reads=[pssr, "Gt"], writes=[sbr])
                    if q0 >= 512 and nq == 128:
                        P.act(lambda e: e.activation(out=ptb[:, 0:256], in_=sbt[:, 0:256], func=AF.Exp), reads=[sbr], writes=[sbr])
                    else:
                        P.act(lambda e: e.activation(out=ptb[:, 0:nq], in_=sbt[:, 0:nq], func=AF.Exp, bias=keymask[:, blk - 1:blk]),
                              reads=[sbr, "keymask"], writes=[sbr])
                        P.act(lambda e: e.activation(out=ptb[0:nq, 128:128 + nq], in_=sbt[0:nq, 128:128 + nq], func=AF.Exp,
                                                     bias=keymask[0:nq, blk:blk + 1]), reads=[sbr, "keymask"], writes=[sbr])
                    pinfo[pi] = (ptb, sbr)

                def stageB(pi):
                    bi, j = pairs[pi]
                    q0, q1 = qblocks[bi]
                    nq = q1 - q0
                    blk = q0 // 128
                    if j == 0:
                        binfo[bi] = ps_next()
                    pso, psor = binfo[bi]
                    ptb, sbr = pinfo.pop(pi)
                    P.pe(lambda e: e.matmul(pso[0:nq, j * 66:j * 66 + 65], lhsT=ptb[:, 0:nq], rhs=Vt[:, blk - 1, kvh, 0:65], start=True, stop=False),
                         reads=[sbr, ("Vt", blk - 1), "Vt_ones"], writes=[psor])
                    P.pe(lambda e: e.matmul(pso[0:nq, j * 66:j * 66 + 65], lhsT=ptb[0:nq, 128:128 + nq], rhs=Vt[0:nq, blk, kvh, 0:65], start=False, stop=True),
                         reads=[sbr, ("Vt", blk), "Vt_ones"], writes=[psor])
                    if j == 3:
                        dn, dnr = scr_next()
                        atb = dn[:, :].bitcast(BF16)[:, 512:768]
                        P.dve(lambda e: e.tensor_tensor(
                            out=dn[0:nq, 0:4], in0=pso[0:nq, 0:264].rearrange("p (j c) -> p j c", c=66)[:, :, 64],
                            in1=es_s[0:nq, l * 16 + 4 * kvh:l * 16 + 4 * kvh + 4], op=ALU.add), reads=[psor, "es"], writes=[dnr])
                        P.dve(lambda e: e.reciprocal(out=dn[0:nq, 4:8], in_=dn[0:nq, 0:4]), reads=[dnr], writes=[dnr])
                        P.dve(lambda e: e.tensor_tensor(
                            out=atb[0:nq, 0:128].rearrange("p (j d) -> p j d", d=64),
                            in0=pso[0:nq, 0:132].rearrange("p (j c) -> p j c", c=66)[:, :, 0:64],
                            in1=dn[0:nq, 4:6].unsqueeze(2).to_broadcast([nq, 2, 64]), op=ALU.mult), reads=[psor, dnr], writes=[dnr])
                        for jj in range(2, 4):
                            P.act(lambda e, jj=jj: e.activation(out=atb[0:nq, jj * 64:(jj + 1) * 64], in_=pso[0:nq, jj * 66:jj * 66 + 64],
                                                               func=AF.Copy, scale=dn[0:nq, 4 + jj:5 + jj]), reads=[psor, dnr], writes=[dnr])
                        pendC2.append((pi + 3, atb, dnr, q0, nq))

                def stageC2(atb, dnr, q0, nq):
                    pst_, pstr = ps_next()
                    pstb = pst_[:, :].bitcast(BF16)
                    for i in range(2):
                        P.pe(lambda e, i=i: e.transpose(out=pstb[:, i * 128:i * 128 + nq], in_=atb[0:nq, i * 128:(i + 1) * 128],
                                                        identity=ident_b[0:nq, 0:nq]), reads=[dnr, "ident_b"], writes=[pstr])
                    P.dve(lambda e: e.tensor_copy(
                        out=X_t[:, 2 * kvh:2 * kvh + 2, q0:q0 + nq], in_=pstb[:, 0:256].rearrange("p (i q) -> p i q", i=2)[:, :, 0:nq]),
                        reads=[pstr], writes=X.res(2 * kvh, q0, q0 + nq) + X.res(2 * kvh + 1, q0, q0 + nq))

                npair = len(pairs)
                for step in range(npair + DEPTH):
                    if step < npair:
                        stageA(step)
                    pb = step - DEPTH
                    if pb >= 0:
                        stageB(pb)
                        while pendC2 and pendC2[0][0] <= pb:
                            _, atb, dnr, q0_, nq_ = pendC2.pop(0)
                            stageC2(atb, dnr, q0_, nq_)
                while pendC2:
                    _, atb, dnr, q0_, nq_ = pendC2.pop(0)
                    stageC2(atb, dnr, q0_, nq_)

                sinfo = []
                for j in range(4):
                    h = 4 * kvh + j
                    qt, qb = qhead[h]
                    P.dve(lambda e, j=j: e.tensor_copy(out=sbias[:, j:j + 1], in_=Gt[:, j, 0:1]), reads=["Gt"], writes=[("sbias", j)])
                    P.dve(lambda e, j=j: e.tensor_copy(out=sbias[0:1, j:j + 1], in_=Gt[0:1, j, 128:129]), reads=["Gt"], writes=[("sbias", j)])
                for j in range(4):
                    h = 4 * kvh + j
                    qt, qb = qhead[h]
                    pss, pssr = ps_next()
                    for s in range(NS):
                        P.pe(lambda e, s=s: e.matmul(pss[:, s:s + 1], lhsT=kcT[kb:kb + 64, s, kvh // 2, :],
                                                     rhs=R2.ap(qt, NP + s, NP + s + 1, kb, kb + 64), start=True, stop=True),
                             reads=[("kcT", s)] + R2.res(qt, NP, NT), writes=[pssr])
                    sbt, sbr = scr_next()
                    P.dve(lambda e, j=j, sbt=sbt, pss=pss: e.tensor_scalar(out=sbt[:, 0:4], in0=pss[:, 0:4], scalar1=SCALE, scalar2=sbias[:, j:j + 1],
                                                                         op0=ALU.mult, op1=ALU.add), reads=[pssr, ("sbias", j)], writes=[sbr])
                    ptb = sbt[:, :].bitcast(BF16)
                    P.act(lambda e, sbt=sbt, ptb=ptb: e.activation(out=ptb[:, 512:516], in_=sbt[:, 0:4], func=AF.Exp), reads=[sbr], writes=[sbr])
                    sinfo.append((ptb, sbr))
                for j in range(4):
                    h = 4 * kvh + j
                    ob = 64 * (h % 2)
                    ptb, sbr = sinfo[j]
                    pso, psor = ps_next()
                    for s in range(NS):
                        P.pe(lambda e, s=s, pso=pso, ptb=ptb: e.matmul(pso[ob:ob + 64, s:s + 1], lhsT=veff[:, s, kvh, :], rhs=ptb[:, 512 + s:513 + s],
                                                                       start=True, stop=True), reads=[sbr, ("veff", s)], writes=[psor])
                    P.pe(lambda e, pso=pso, ptb=ptb: e.matmul(pso[ob:ob + 64, 4:8], lhsT=ones_b[:, 0:64], rhs=ptb[:, 512:516], start=True, stop=True),
                         reads=[sbr, "ones_b"], writes=[psor])
                    dn, dnr = scr_next()
                    P.dve(lambda e, dn=dn, pso=pso: e.tensor_scalar(out=dn[ob:ob + 64, 0:4], in0=pso[ob:ob + 64, 4:8],
                                                                    scalar1=es_s[ob:ob + 64, l * 16 + h:l * 16 + h + 1], scalar2=None, op0=ALU.add),
                          reads=[psor, "es"], writes=[dnr])
                    P.dve(lambda e, dn=dn: e.reciprocal(out=dn[ob:ob + 64, 4:8], in_=dn[ob:ob + 64, 0:4]), reads=[dnr], writes=[dnr])
                    P.dve(lambda e, dn=dn, pso=pso: e.tensor_tensor(out=X.ap(h // 2, NP, NT, ob, ob + 64), in0=pso[ob:ob + 64, 0:4],
                                                                    in1=dn[ob:ob + 64, 4:8], op=ALU.mult),
                          reads=[psor, dnr], writes=X.res(h // 2, NP, NT))

            stage(f'L{l}_attn')
            for m in range(NCH):
                sl, sres = load_slab([(w_o[l, 0:1024, m * 128:(m + 1) * 128], 0)], 8)
                for (lo, hi) in groups(*RMX):
                    ps, psres = ps_next()
                    mm_group(ps, psres, sl, sres, 8, rhsX, lo, hi)
                    P.dve(lambda e, ps=ps, lo=lo, hi=hi, m=m: e.tensor_tensor(
                        out=xT.ap(m, lo, hi), in0=xT.ap(m, lo, hi), in1=ps[:, 0:hi - lo], op=ALU.add),
                        reads=[psres] + xT.res(m, lo, hi), writes=xT.res(m, lo, hi))

            stage(f'L{l}_wo2')
            def layer_norm(gi_, bi_):
                G = groups(*RMX)
                info = {}

                def stats(gi):
                    lo, hi = G[gi]
                    n = hi - lo
                    ps1, ps1r = ps_next()
                    ps2, ps2r = ps_next()
                    for c in range(NCH):
                        P.pe(lambda e, c=c: e.matmul(ps1[:, 0:n], lhsT=ones_f[:, :], rhs=xT.ap(c, lo, hi), start=(c == 0), stop=(c == NCH - 1)),
                             reads=xT.res(c, lo, hi) + ["ones_f"], writes=[ps1r])
                        sq, sqr = scr_next()
                        sqb = sq[:, :].bitcast(BF16)
                        P.act(lambda e, c=c, sqb=sqb: e.activation(out=sqb[:, 0:n], in_=xT.ap(c, lo, hi), func=AF.Square),
                              reads=xT.res(c, lo, hi), writes=[sqr])
                        P.pe(lambda e, c=c, sqb=sqb: e.matmul(ps2[:, 0:n], lhsT=ones_b[:, :], rhs=sqb[:, 0:n], start=(c == 0), stop=(c == NCH - 1)),
                             reads=[sqr, "ones_b"], writes=[ps2r])
                    rstd = rstd_t[gi]
                    rr = ("rstd", gi)
                    P.dve(lambda e: e.tensor_scalar(out=rstd[:, 0:n], in0=ps1[:, 0:n], scalar1=1.0 / D, scalar2=None, op0=ALU.mult),
                          reads=[ps1r], writes=[rr])
                    P.dve(lambda e: e.tensor_tensor(out=rstd[:, 0:n], in0=rstd[:, 0:n], in1=rstd[:, 0:n], op=ALU.mult), reads=[rr], writes=[rr])
                    P.dve(lambda e: e.scalar_tensor_tensor(out=rstd[:, 0:n], in0=ps2[:, 0:n], scalar=1.0 / D, in1=rstd[:, 0:n],
                                                           op0=ALU.mult, op1=ALU.subtract), reads=[ps2r, rr], writes=[rr])
                    P.act(lambda e: e.activation(out=rstd[:, 0:n], in_=rstd[:, 0:n], func=AF.Ln, bias=eps_t[:, 0:1]), reads=[rr, "eps"], writes=[rr])
                    P.act(lambda e: e.activation(out=rstd[:, 0:n], in_=rstd[:, 0:n], func=AF.Exp, scale=-0.5), reads=[rr], writes=[rr])
                    info[gi] = (ps1, ps1r, rstd, rr)

                def norm(gi):
                    lo, hi = G[gi]
                    n = hi - lo
                    ps1, ps1r, rstd, rr = info[gi]
                    pend = []

                    def xcopy(c):
                        if c % 2 == 0:
                            P.act(lambda e: e.activation(out=X.ap(c, lo, hi), in_=xT.ap(c, lo, hi), func=AF.Copy),
                                  reads=xT.res(c, lo, hi), writes=X.res(c, lo, hi))
                        else:
                            P.dve(lambda e: e.tensor_copy(out=X.ap(c, lo, hi), in_=xT.ap(c, lo, hi)),
                                  reads=xT.res(c, lo, hi), writes=X.res(c, lo, hi))

                    def sub_(c):
                        P.dve(lambda e: e.scalar_tensor_tensor(out=xT.ap(c, lo, hi), in0=ps1[:, 0:n], scalar=-1.0 / D, in1=xT.ap(c, lo, hi),
                                                               op0=ALU.mult, op1=ALU.add),
                              reads=[ps1r] + xT.res(c, lo, hi), writes=xT.res(c, lo, hi))

                    def mul_aff(c):
                        P.dve(lambda e: e.tensor_tensor(out=xT.ap(c, lo, hi), in0=xT.ap(c, lo, hi), in1=rstd[:, 0:n], op=ALU.mult),
                              reads=[rr] + xT.res(c, lo, hi), writes=xT.res(c, lo, hi))
                        P.act(lambda e: e.activation(out=xT.ap(c, lo, hi), in_=xT.ap(c, lo, hi), func=AF.Identity,
                                                     scale=lnvec(c, gi_), bias=lnvec(c, bi_)),
                              reads=["lnp"] + xT.res(c, lo, hi), writes=xT.res(c, lo, hi))

                    sub_(0)
                    for c in range(NCH):
                        if c + 1 < NCH:
                            sub_(c + 1)
                        mul_aff(c)
                        pend.append(c)
                        if len(pend) > 2:
                            xcopy(pend.pop(0))
                    while pend:
                        xcopy(pend.pop(0))

                stats(0)
                for gi in range(len(G)):
                    if gi + 1 < len(G):
                        stats(gi + 1)
                    norm(gi)

            layer_norm(0, 1)

            stage(f'L{l}_ln1')
            FG = groups(*RMX)
            P.dma("sp", lambda e: e.dma_start(out=ffns[l].rearrange("(s r) c -> s r c", r=2)[:, 0, :],
                                              in_=s_ffn[l].rearrange("(s r) c -> s r c", r=2)[:, 1, :]), writes=[("o_ffns0", l)])
            for b in range(NBATCH):
                for jj in range(JB):
                    pair = b * JB + jj
                    hv_info = []
                    for hv in range(2):
                        tix = pair + hv * (NFT // 2)
                        sl, sres = load_slab([(wcols(w_up, tix * 128), 0)], 16)
                        hv_info.append((tix, sl, sres))
                    for (lo, hi) in FG:
                        glo = lo - 2
                        n2 = hi - glo
                        phi = min(hi, NP)
                        m = phi - lo
                        cv_pair = []
                        for hv in range(2):
                            tix, sl, sres = hv_info[hv]
                            fw0 = fcw_s[:, tix, l * 3 + 0:l * 3 + 1]
                            fw1 = fcw_s[:, tix, l * 3 + 1:l * 3 + 2]
                            fw2 = fcw_s[:, tix, l * 3 + 2:l * 3 + 3]
                            ps, psres = ps_next()
                            mm_group(ps, psres, sl, sres, 16, rhsX, glo, hi)
                            if glo <= H - 4 and H <= hi:
                                o = H - 4 - glo
                                P.dve(lambda e, ps=ps, o=o: e.tensor_tensor(out=ps[:, o:o + 4], in0=ps[:, o:o + 4], in1=smask[:, 0:4], op=ALU.mult),
                                      reads=["smask", psres], writes=[psres])
                            cvt, cvr = scr_next()
                            P.act(lambda e, cvt=cvt, ps=ps, fw0=fw0: e.activation(out=cvt[:, 0:m], in_=ps[:, 0:m], func=AF.Copy, scale=fw0),
                                  reads=[psres, "fcw"], writes=[cvr])
                            P.dve(lambda e, cvt=cvt, ps=ps, fw1=fw1: e.scalar_tensor_tensor(
                                out=cvt[:, 0:m], in0=ps[:, 1:1 + m], scalar=fw1, in1=cvt[:, 0:m], op0=ALU.mult, op1=ALU.add),
                                reads=[psres, "fcw", cvr], writes=[cvr])
                            P.dve(lambda e, cvt=cvt, ps=ps, fw2=fw2: e.scalar_tensor_tensor(
                                out=cvt[:, 0:m], in0=ps[:, 2:2 + m], scalar=fw2, in1=cvt[:, 0:m], op0=ALU.mult, op1=ALU.add),
                                reads=[psres, "fcw", cvr], writes=[cvr])
                            if hi == NT:
                                so2 = NP - glo
                                fi = jj + hv * JB
                                P.act(lambda e, ps=ps, fi=fi: e.activation(out=fst[:, fi, 0:6], in_=ps[:, so2 - 2:so2 + 4], func=AF.Copy),
                                      reads=[psres], writes=[("fst", fi)])
                            cv_pair.append((cvt, cvr))
                        (cg, cgr), (cvv, cvvr) = cv_pair
                        P.act(lambda e, cg=cg: e.activation(out=cg[:, 0:m], in_=cg[:, 0:m], func=AF.Silu), reads=[cgr], writes=[cgr])
                        P.dve(lambda e, cg=cg, cvv=cvv, lo=lo, jj=jj: e.tensor_tensor(
                            out=R2.ap(jj, lo, lo + m), in0=cg[:, 0:m], in1=cvv[:, 0:m], op=ALU.mult),
                            reads=[cgr, cvvr], writes=R2.res(jj, lo, lo + m))
                st_, str_ = scr_next()
                cs = st_[:, 0:88].rearrange("p (h t s) -> p h t s", h=2, s=4)
                t2 = st_[:, 128:216].rearrange("p (h t s) -> p h t s", h=2, s=4)
                fres = [("fst", i) for i in range(2 * JB)]
                for hv in range(2):
                    t0_ = b * JB + hv * (NFT // 2)
                    svv = stf[:, t0_:t0_ + JB, :].rearrange("p t (s r) -> p t r s", r=2)
                    wv = [fcw_s[:, t0_:t0_ + JB, l * 3 + kk:l * 3 + kk + 1].to_broadcast([128, JB, 4]) for kk in range(3)]
                    us = fst[:, hv * JB:(hv + 1) * JB, 2:6]
                    P.dve(lambda e, hv=hv, svv=svv, wv=wv: e.tensor_tensor(out=cs[:, hv], in0=svv[:, :, 0, :], in1=wv[0], op=ALU.mult),
                          reads=["stf", "fcw"], writes=[str_])
                    P.dve(lambda e, hv=hv, svv=svv, wv=wv: e.tensor_tensor(out=t2[:, hv], in0=svv[:, :, 1, :], in1=wv[1], op=ALU.mult),
                          reads=["stf", "fcw"], writes=[str_])
                    P.dve(lambda e, hv=hv: e.tensor_tensor(out=cs[:, hv], in0=cs[:, hv], in1=t2[:, hv], op=ALU.add), reads=[str_], writes=[str_])
                    P.dve(lambda e, hv=hv, us=us, wv=wv: e.tensor_tensor(out=t2[:, hv], in0=us, in1=wv[2], op=ALU.mult),
                          reads=fres + ["fcw"], writes=[str_])
                    P.dve(lambda e, hv=hv: e.tensor_tensor(out=cs[:, hv], in0=cs[:, hv], in1=t2[:, hv], op=ALU.add), reads=[str_], writes=[str_])
                P.act(lambda e: e.activation(out=cs[:, 0], in_=cs[:, 0], func=AF.Silu), reads=[str_], writes=[str_])
                P.dve(lambda e: e.tensor_tensor(out=R2a_t[:, 0:8, NP - 128:NT - 128], in0=cs[:, 0, 0:8, :], in1=cs[:, 1, 0:8, :], op=ALU.mult),
                      reads=[str_], writes=[r_ for c_ in range(8) for r_ in R2.res(c_, NP, NT)])
                P.dve(lambda e: e.tensor_tensor(out=R2b_t[:, 0:3, NP:NT], in0=cs[:, 0, 8:11, :], in1=cs[:, 1, 8:11, :], op=ALU.mult),
                      reads=[str_], writes=[r_ for c_ in range(8, 11) for r_ in R2.res(c_, NP, NT)])
                for hv in range(2):
                    colbase = hv * DFF + b * JB * 128
                    for t0 in range(0, JB, 4):
                        nt_ = min(4, JB - t0)
                        ps, psres = ps_next()
                        for i in range(nt_):
                            fi = hv * JB + t0 + i
                            P.pe(lambda e, ps=ps, fi=fi, i=i: e.transpose(out=ps[0:6, i * 128:(i + 1) * 128], in_=fst[:, fi, 0:6], identity=ident_f[:, :]),
                                 reads=[("fst", fi), "ident_f"], writes=[psres])
                        ot, otr = scr_next()
                        P.act(lambda e, ot=ot, ps=ps, nt_=nt_: e.activation(out=ot[0:6, 0:nt_ * 128], in_=ps[0:6, 0:nt_ * 128], func=AF.Copy),
                              reads=[psres], writes=[otr])
                        c0 = colbase + t0 * 128
                        P.dma("sp", lambda e, ot=ot, c0=c0, nt_=nt_: e.dma_start(out=ffnp[l, :, c0:c0 + nt_ * 128], in_=ot[0:2, 0:nt_ * 128]),
                              reads=[otr], writes=[("o_ffnp", l, c0)])
                        P.dma("sp", lambda e, ot=ot, c0=c0, nt_=nt_: e.dma_start(
                            out=ffns[l].rearrange("(s r) c -> s r c", r=2)[:, 1, c0:c0 + nt_ * 128], in_=ot[2:6, 0:nt_ * 128]),
                            reads=[otr], writes=[("o_ffns", l, c0)])
                for m in range(NCH):
                    sl, sres = load_slab([(w_down[l, b * JB * 128:(b + 1) * JB * 128, m * 128:(m + 1) * 128], 0)], JB)
                    for (lo, hi) in FG:
                        ps, psres = ps_next()
                        mm_group(ps, psres, sl, sres, JB, rhsR2, lo, hi)
                        if b == 0:
                            P.dve(lambda e, ps=ps, lo=lo, hi=hi, m=m: e.scalar_tensor_tensor(
                                out=xT.ap(m, lo, hi), in0=xT.ap(m, lo, hi), scalar=ALPHA, in1=ps[:, 0:hi - lo], op0=ALU.mult, op1=ALU.add),
                                reads=[psres] + xT.res(m, lo, hi), writes=xT.res(m, lo, hi))
                        else:
                            P.dve(lambda e, ps=ps, lo=lo, hi=hi, m=m: e.tensor_tensor(
                                out=xT.ap(m, lo, hi), in0=xT.ap(m, lo, hi), in1=ps[:, 0:hi - lo], op=ALU.add),
                                reads=[psres] + xT.res(m, lo, hi), writes=xT.res(m, lo, hi))
            stage(f'L{l}_ffn')
            layer_norm(2, 3)
            stage(f'L{l}_ln2')

        try:
            if stop == '__done__':
                raise _Stop()
            stage('setup')
            layer(0)
            layer(1)

            def out_block(col0, ntok, dst):
                for hf in range(4):
                    ps, psres = ps_next()
                    for i in range(4):
                        c = hf * 4 + i
                        P.pe(lambda e, ps=ps, c=c, i=i: e.transpose(out=ps[0:ntok, i * 128:(i + 1) * 128], in_=xT.ap(c, col0, col0 + ntok), identity=ident_f[:, :]),
                             reads=xT.res(c, col0, col0 + ntok) + ["ident_f"], writes=[psres])
                    ot, otr = scr_next()
                    if hf % 2 == 0:
                        P.act(lambda e, ot=ot, ps=ps: e.activation(out=ot[0:ntok, :], in_=ps[0:ntok, :], func=AF.Copy), reads=[psres], writes=[otr])
                    else:
                        P.dve(lambda e, ot=ot, ps=ps: e.tensor_copy(out=ot[0:ntok, :], in_=ps[0:ntok, :]), reads=[psres], writes=[otr])
                    P.dma("sp", lambda e, ot=ot, hf=hf: e.dma_start(out=dst[:, hf * 512:(hf + 1) * 512], in_=ot[0:ntok, :]),
                          reads=[otr], writes=[("o_y", col0, hf)])

            for b in range(8):
                out_block(H + b * 128, 128, yp[b * 128:(b + 1) * 128, :])
            out_block(NP, NS, ys)

        except _Stop:
            pass
        if dbg:
            allres = list(P.lastw.keys())
            P.dma("sp", lambda e: e.dma_start(out=dbg_xT.rearrange("p (c n) -> p c n", c=NCH), in_=xT_t[:, :, :]), reads=allres, writes=["dbg1"])
            P.dma("pool", lambda e: e.dma_start(out=dbg_X.rearrange("p (c n) -> p c n", c=NCH), in_=X_t[:, :, :]), reads=allres, writes=["dbg2"])
            P.dma("pool", lambda e: e.dma_start(out=dbg_R2a.rearrange("p (c n) -> p c n", c=8), in_=R2a_t[:, :, :]), reads=allres, writes=["dbg3"])
            P.dma("pool", lambda e: e.dma_start(out=dbg_R2b.rearrange("p (c n) -> p c n", c=3), in_=R2b_t[:, :, :]), reads=allres, writes=["dbg4"])
        P.emit()
        build.stats = P.stats
    return nc


def _t5_bucket_np(n):
    n = np.asarray(n)
    max_exact = 16
    nf = np.maximum(n, 1).astype(np.float32)
    large = max_exact + (np.log(nf / np.float32(max_exact)) / np.float32(math.log(128 / max_exact))
                         * np.float32(32 - max_exact)).astype(np.int32)
    large = np.minimum(large, 31)
    return np.where(n < max_exact, n, large)


_CACHE = {}


def make_in_maps(x_prompt, x_sample, cache_k, cache_v, state_conv, state_pool, state_ffn,
                 rel_table, w_in, conv_w, pool_w, pool_scale, sinks, w_o, ln1_g, ln1_b,
                 w_up, ffn_conv_w, w_down, ln2_g, ln2_b):
    f = lambda a: np.ascontiguousarray(np.asarray(a), dtype=np.float32)
    x_prompt, x_sample = f(x_prompt), f(x_sample)
    cache_k, cache_v = f(cache_k), f(cache_v)
    state_conv, state_pool, state_ffn = f(state_conv), f(state_pool), f(state_ffn)
    rel_table = f(rel_table)
    kk = np.arange(128)[:, None]
    qq = np.arange(128)[None, :]
    dist = np.concatenate([qq + 128 - kk, qq - kk], axis=1)
    valid = (dist >= 0) & (dist < WIN)
    bucket = _t5_bucket_np(np.arange(WIN))
    bidx = bucket[np.clip(dist, 0, WIN - 1)]
    tbl = np.ascontiguousarray(np.transpose(rel_table[bidx], (2, 0, 1)))
    gmask = np.where(valid, 0.0, NEG).astype(np.float32)
    ident = np.eye(128, dtype=np.float32)
    shared = {
        "w_in": f(w_in), "w_o": f(w_o), "w_up": f(w_up), "w_down": f(w_down),
        "conv_w": f(conv_w).reshape(6, 512), "ffn_cw": f(ffn_conv_w).reshape(6, 2 * DFF),
        "pool_w": f(pool_w), "pool_scale": f(pool_scale),
        "lnp": np.ascontiguousarray(np.stack([f(ln1_g)[0], f(ln1_b)[0], f(ln2_g)[0], f(ln2_b)[0],
                                               f(ln1_g)[1], f(ln1_b)[1], f(ln2_g)[1], f(ln2_b)[1]], 0)),
        "sinks": np.ascontiguousarray(np.broadcast_to(f(sinks).reshape(1, 32), (128, 32))), "tbl": tbl, "gmask": gmask, "ident": ident,
    }
    in_maps = []
    for c in range(8):
        b, half = c // 2, c % 2
        if half == 0:
            xh = np.concatenate([np.zeros((H, D), np.float32), x_prompt[b, 0:NM]], 0)
        else:
            xh = x_prompt[b, NM - H:SEQ]
        keymask = np.zeros((128, 11), np.float32)
        smask = np.ones((128, 16), np.float32)
        prat = np.ones((128, 4, 16), np.float32)
        if half == 0:
            cols = np.arange(11 * 128).reshape(11, 128).T
            keymask[cols < H] = NEG
            smask[:] = 0.0
            pos = np.arange(16)
            for g in range(4):
                w = 2 << g
                prat[:, g, :] = (np.float32(w) / np.minimum(pos + 1, w).astype(np.float32))[None, :]
        sl = slice(4 * c, 4 * c + 4)
        m = dict(shared)
        m.update({
            "xh": np.ascontiguousarray(xh), "xs": np.ascontiguousarray(x_sample[sl, 0, :]),
            "ck": np.ascontiguousarray(cache_k[:, sl].reshape(2, NS, 128, 256)),
            "cv": np.ascontiguousarray(cache_v[:, sl].reshape(2, NS, 128, 256)),
            "s_conv": np.ascontiguousarray(state_conv[:, sl].reshape(2, NS * 2, 512)),
            "s_pool": np.ascontiguousarray(state_pool[:, sl].reshape(2, NS * 15, 512)),
            "s_ffn": np.ascontiguousarray(state_ffn[:, sl].reshape(2, NS * 2, 2 * DFF)),
            "keymask": keymask, "smask": smask, "prat": np.ascontiguousarray(prat.reshape(128, 64)),
        })
        in_maps.append(m)

    return in_maps


def assemble(R):
    y_prompt = np.zeros((4, SEQ, D), np.float32)
    y_sample = np.zeros((32, 1, D), np.float32)
    k_prompt = np.zeros((2, 4, 128, 4, 64), np.float32)
    v_prompt = np.zeros((2, 4, 128, 4, 64), np.float32)
    conv_prompt = np.zeros((2, 4, 2, 512), np.float32)
    pool_prompt = np.zeros((2, 4, 15, 512), np.float32)
    ffn_prompt = np.zeros((2, 4, 2, 2 * DFF), np.float32)
    k_sample = np.zeros((2, 32, 128, 4, 64), np.float32)
    v_sample = np.zeros((2, 32, 128, 4, 64), np.float32)
    conv_sample = np.zeros((2, 32, 2, 512), np.float32)
    pool_sample = np.zeros((2, 32, 15, 512), np.float32)
    ffn_sample = np.zeros((2, 32, 2, 2 * DFF), np.float32)
    for c in range(8):
        b, half = c // 2, c % 2
        r = R[c]
        y_prompt[b, half * NM:(half + 1) * NM] = r["yp"]
        sl = slice(4 * c, 4 * c + 4)
        y_sample[sl, 0] = r["ys"]
        if half == 1:
            k_prompt[:, b] = r["kp"].reshape(2, 128, 4, 64)
            v_prompt[:, b] = r["vp"].reshape(2, 128, 4, 64)
            conv_prompt[:, b] = r["convp"]
            pool_prompt[:, b] = r["poolp"]
            ffn_prompt[:, b] = r["ffnp"]
        k_sample[:, sl] = r["ks"].reshape(2, NS, 128, 4, 64)
        v_sample[:, sl] = r["vs"].reshape(2, NS, 128, 4, 64)
        conv_sample[:, sl] = r["convs"].reshape(2, NS, 2, 512)
        pool_sample[:, sl] = r["pools"].reshape(2, NS, 15, 512)
        ffn_sample[:, sl] = r["ffns"].reshape(2, NS, 2, 2 * DFF)
    return (y_prompt, y_sample, k_prompt, v_prompt, conv_prompt, pool_prompt, ffn_prompt,
            k_sample, v_sample, conv_sample, pool_sample, ffn_sample)


def kernel(**inputs):
    if "nc" not in _CACHE:
        _CACHE["nc"] = build()
    nc = _CACHE["nc"]
    in_maps = make_in_maps(**inputs)
    res = run_bass_kernel_spmd(nc, in_maps, core_ids=list(range(8)))
    return assemble(res.results)
```

```python
import contextlib
import math

import numpy as np
import concourse.bass as bass
import concourse.mybir as mybir
from concourse.bass_utils import run_bass_kernel_spmd

F32 = mybir.dt.float32
BF16 = mybir.dt.bfloat16
ALU = mybir.AluOpType
AF = mybir.ActivationFunctionType
AX = mybir.AxisListType

D = 2048
NCH = 16
SEQ = 2048
H = 260
NM = 1024
NP = H + NM
NS = 4
NT = NP + NS
INW = 3584
DFF = 5632
NFT = 88
NHEAD = 16
WIN = 128
ALPHA = 4.0 ** 0.25
SCALE = 64 ** -0.5
EPS = 1e-5
NEG = -1e30
OQ, OK_, OV, OGB, OGC, OH_, OPIN = 0, 1024, 1280, 1536, 2048, 2560, 3072
JB = 11
NBATCH = 4


class _Ins:
    __slots__ = ("eng", "fn", "deps", "dma", "sig", "val", "semi", "pos", "fsz")

    def __init__(self, eng, fn, dma):
        self.eng = eng
        self.fn = fn
        self.deps = set()
        self.dma = dma
        self.sig = False
        self.val = 0
        self.semi = -1
        self.pos = 0
        self.fsz = 0


class _Rec:
    def __getattr__(self, name):
        return lambda *a, **k: (name, a, k)


_REC = _Rec()


class Prog:
    ENGS = ("pe", "act", "dve", "pool", "sp")
    NDMA = 8

    def __init__(self, nc):
        self.nc = nc
        self.ins = []
        self.lastw = {}
        self.readers = {}
        self.dma_n = {e: 0 for e in self.ENGS}
        self.dma_hist = {e: [] for e in self.ENGS}
        self.npos = {e: 0 for e in self.ENGS}

    def add(self, eng, fn, reads=(), writes=(), dma=False):
        idx = len(self.ins)
        ins = _Ins(eng, fn(_REC), dma)
        psr = [r for r in reads if isinstance(r, tuple) and r[0] == "ps"]
        if psr:
            writes = list(writes) + psr
        for r in reads:
            w = self.lastw.get(r)
            if w is not None:
                ins.deps.add(w)
        for r in writes:
            w = self.lastw.get(r)
            if w is not None:
                ins.deps.add(w)
            rl = self.readers.get(r)
            if rl:
                ins.deps.update(rl)
        for r in reads:
            self.readers.setdefault(r, []).append(idx)
        for r in writes:
            self.lastw[r] = idx
            self.readers[r] = []
        if dma:
            n = self.dma_n[eng]
            self.dma_n[eng] = n + 1
            ins.semi = n % self.NDMA
            ins.val = 16 * (n // self.NDMA + 1)
            hist = self.dma_hist[eng]
            if n >= self.NDMA:
                ins.deps.add(hist[n - self.NDMA])
            hist.append(idx)
        ins.deps.discard(idx)
        ins.pos = self.npos[eng]
        self.npos[eng] += 1
        try:
            o_ = ins.fn[2].get("out")
            ins.fsz = o_.free_size() if o_ is not None else ins.fn[1][0].free_size()
        except Exception:
            ins.fsz = 0
        self.ins.append(ins)
        return idx

    def pe(self, fn, reads=(), writes=()):
        return self.add("pe", fn, reads, writes)

    def act(self, fn, reads=(), writes=()):
        return self.add("act", fn, reads, writes)

    def dve(self, fn, reads=(), writes=()):
        return self.add("dve", fn, reads, writes)

    def dma(self, eng, fn, reads=(), writes=()):
        return self.add(eng, fn, reads, writes, dma=True)

    def emit(self, final_eng="sp"):
        nc = self.nc
        ins_all = list(self.ins)
        fin = _Ins(final_eng, None, False)
        last_by_eng = {}
        for i, ins in enumerate(ins_all):
            if ins.dma:
                fin.deps.add(i)
            last_by_eng[ins.eng] = i
        for e, i in last_by_eng.items():
            fin.deps.add(i)
        ins_all.append(fin)
        def self_hazard(ins, p):
            if ins.eng not in ("dve", "act", "pool") or p.dma or p.eng != ins.eng:
                return False
            return True

        for ins in ins_all:
            for d in ins.deps:
                p = ins_all[d]
                if (not p.dma) and (p.eng != ins.eng or self_hazard(ins, p)):
                    p.sig = True
        cnt = {e: 0 for e in self.ENGS}
        for ins in ins_all:
            if (not ins.dma) and ins.sig:
                cnt[ins.eng] += 1
                ins.val = cnt[ins.eng]
        per_eng = {e: [] for e in self.ENGS}
        for ins in ins_all:
            per_eng[ins.eng].append(ins)
        self.stats = {e: len(v) for e, v in per_eng.items()}

        with contextlib.ExitStack() as st:
            esem = {e: st.enter_context(nc.semaphore(f"es_{e}")) for e in self.ENGS}
            dsem = {
                e: [st.enter_context(nc.semaphore(f"ds_{e}{i}")) for i in range(self.NDMA)]
                for e in self.ENGS
                if self.dma_n[e] > 0
            }
            block = st.enter_context(nc.Block())

            def run(engname, engobj):
                waited = {}
                for ins in per_eng[engname]:
                    need = {}
                    for d in ins.deps:
                        p = ins_all[d]
                        if p.dma:
                            key = ("d", p.eng, p.semi)
                        elif p.eng != engname or self_hazard(ins, p):
                            key = ("e", p.eng)
                        else:
                            continue
                        if p.val > need.get(key, 0):
                            need[key] = p.val
                    for key, v in need.items():
                        if waited.get(key, 0) >= v:
                            continue
                        sem = dsem[key[1]][key[2]] if key[0] == "d" else esem[key[1]]
                        engobj.wait_ge(sem, v)
                        waited[key] = v
                    if ins.fn is None:
                        continue
                    name, a_, k_ = ins.fn
                    bi = getattr(engobj, name)(*a_, **k_)
                    if ins.dma:
                        bi.then_inc(dsem[engname][ins.semi], 16)
                    elif ins.sig:
                        bi.then_inc(esem[engname], 1)

            if per_eng["pe"]:
                block.tensor(lambda e: run("pe", e))
            if per_eng["act"]:
                block.scalar(lambda e: run("act", e))
            if per_eng["dve"]:
                block.vector(lambda e: run("dve", e))
            if per_eng["pool"]:
                block.gpsimd(lambda e: run("pool", e))
            if per_eng["sp"]:
                block.sync(lambda e: run("sp", e))


def groups(lo, hi, mx=512):
    n = hi - lo
    k = (n + mx - 1) // mx
    out = []
    base, rem = divmod(n, k)
    a = lo
    for i in range(k):
        b = a + base + (1 if i < rem else 0)
        out.append((a, b))
        a = b
    return out


class SB:
    def __init__(self, name, tile, col0=0):
        self.name, self.t, self.c0 = name, tile, col0

    def ap(self, c, lo, hi, p0=0, p1=128):
        return self.t[p0:p1, c, lo - self.c0:hi - self.c0]

    def res(self, c, lo, hi):
        return [(self.name, c, b) for b in range(lo // 128, (hi - 1) // 128 + 1)]


class _Stop(Exception):
    pass


def build(stop=None, dbg=False):
    def stage(name):
        nonlocal stop
        if stop == name:
            raise _Stop()

    nc = bass.Bass("TRN2", target_bir_lowering=False)

    def din(name, shape, dt=F32):
        return nc.dram_tensor(name, list(shape), dt, kind="ExternalInput").ap()

    def dout(name, shape):
        return nc.dram_tensor(name, list(shape), F32, kind="ExternalOutput").ap()

    xh = din("xh", [NP, D])
    xs = din("xs", [NS, D])
    ck = din("ck", [2, NS, 128, 256])
    cv = din("cv", [2, NS, 128, 256])
    s_conv = din("s_conv", [2, NS * 2, 512])
    s_pool = din("s_pool", [2, NS * 15, 512])
    s_ffn = din("s_ffn", [2, NS * 2, 2 * DFF])
    w_in = din("w_in", [2, D, INW])
    w_o = din("w_o", [2, D, D])
    w_up = din("w_up", [2, D, 2 * DFF])
    w_down = din("w_down", [2, DFF, D])
    conv_w = din("conv_w", [6, 512])
    ffn_cw = din("ffn_cw", [6, 2 * DFF])
    pool_w = din("pool_w", [2, 4, 128, 128])
    pool_scale = din("pool_scale", [2, 512])
    lnp = din("lnp", [8, D])
    sinks = din("sinks", [128, 32])
    tbl = din("tbl", [NHEAD, 128, 256])
    gmask_d = din("gmask", [128, 256])
    ident_d = din("ident", [128, 128])
    keymask_d = din("keymask", [128, 11])
    smask_d = din("smask", [128, 16])
    prat_d = din("prat", [128, 64])
    yp = dout("yp", [NM, D])
    ys = dout("ys", [NS, D])
    kp = dout("kp", [2, 128, 256])
    vp = dout("vp", [2, 128, 256])
    convp = dout("convp", [2, 2, 512])
    poolp = dout("poolp", [2, 15, 512])
    ffnp = dout("ffnp", [2, 2, 2 * DFF])
    ks = dout("ks", [2, NS, 128, 256])
    vs = dout("vs", [2, NS, 128, 256])
    convs = dout("convs", [2, NS * 2, 512])
    pools = dout("pools", [2, NS * 15, 512])
    ffns = dout("ffns", [2, NS * 2, 2 * DFF])

    if dbg:
        dbg_xT = dout("dbg_xT", [128, NCH * (NT - 128)])
        dbg_X = dout("dbg_X", [128, NCH * NT])
        dbg_R2a = dout("dbg_R2a", [128, 8 * (NT - 128)])
        dbg_R2b = dout("dbg_R2b", [128, 3 * NT])
    st = contextlib.ExitStack()
    with st:
        def sb(name, shape, dt=F32):
            return st.enter_context(nc.sbuf_tensor(name, list(shape), dt))

        xT_t = sb("xT", [128, NCH, NT - 128])
        X_t = sb("X", [128, NCH, NT], BF16)
        R2a_t = sb("R2a", [128, 8, NT - 128], BF16)
        R2b_t = sb("R2b", [128, 3, NT], BF16)
        Vt = sb("Vt", [128, 11, 4, 66], BF16)
        NSLAB = 5
        slabs = [sb(f"slab{i}", [128, 16, 128], BF16) for i in range(NSLAB)]
        NSCR = 8
        scr = [sb(f"scr{i}", [128, 512]) for i in range(NSCR)]
        Gt = sb("Gt", [128, 4, 256])
        gmask = sb("gmask_s", [128, 256])
        ident_f = sb("ident_f", [128, 128])
        ident_b = sb("ident_b", [128, 128], BF16)
        ones_b = sb("ones_b", [128, 128], BF16)
        ones_f = sb("ones_f", [128, 128])
        eps_t = sb("eps_t", [128, 2])
        rstd_t = [sb(f"rstd{i}", [128, 392]) for i in range(3)]
        lnp_s = sb("lnp_s", [128, NCH, 8])
        cw_s = sb("cw_s", [128, 4, 6])
        fcw_s = sb("fcw_s", [128, NFT, 6])
        psc_s = sb("psc_s", [128, 4, 2])
        pw_s = sb("pw_s", [128, 8, 128], BF16)
        es_s = sb("es_s", [128, 32])
        keymask = sb("keymask_s", [128, 11])
        smask = sb("smask_s", [128, 16])
        prat = sb("prat_s", [128, 64])
        stc = sb("stc", [128, 2, 4, 8])
        stp = sb("stp", [128, 2, 4, 60])
        stf = sb("stf", [128, NFT, 8])
        fst = sb("fst", [128, 2 * JB, 6])
        kcT = sb("kcT", [128, NS, 2, 128], BF16)
        veff = sb("veff", [128, NS, 4, 64], BF16)
        cst = sb("cst", [128, 4, 10])
        pst = sb("pst", [128, 4, 75])
        sbias = sb("sbias", [128, 4])
        small = sb("small", [128, 64])

        PSB = [st.enter_context(nc.psum_tensor(f"ps{i}", [128, 512], F32)) for i in range(8)]

        P = Prog(nc)
        xT = SB("xT", xT_t, 128)
        X = SB("X", X_t, 0)

        class R2cls:
            def ap(self, c, lo, hi, p0=0, p1=128):
                if c < 8:
                    return R2a_t[p0:p1, c, lo - 128:hi - 128]
                return R2b_t[p0:p1, c - 8, lo:hi]

            def res(self, c, lo, hi):
                return [("R2", c, b) for b in range(lo // 128, (hi - 1) // 128 + 1)]

        R2 = R2cls()

        state = {"ps": 0, "scr": 0, "slab": 0}

        def ps_next():
            i = state["ps"] % 8
            state["ps"] += 1
            return PSB[i], ("ps", i)

        def scr_next():
            i = state["scr"] % NSCR
            state["scr"] += 1
            return scr[i], ("scr", i)

        def slab_next():
            i = state["slab"] % NSLAB
            state["slab"] += 1
            return slabs[i], ("slab", i)

        def load_slab(pieces, nk):
            sl, sres = slab_next()
            for src, c0 in pieces:
                ncols = src.shape[1]
                P.dma("pool", lambda e, sl=sl, src=src, c0=c0, ncols=ncols: e.dma_start(
                    out=sl[:, 0:nk, c0:c0 + ncols], in_=src.rearrange("(k p) c -> p k c", p=128)),
                    writes=[sres])
            return sl, sres

        def mm_group(ps, psres, sl, sres, nk, rhs, lo, hi, m0=0, m1=128):
            for k in range(nk):
                rap, rres = rhs(k, lo, hi)
                P.pe(lambda e, k=k, rap=rap: e.matmul(ps[m0:m1, 0:hi - lo], lhsT=sl[:, k, m0:m1], rhs=rap,
                                                       start=(k == 0), stop=(k == nk - 1)),
                     reads=[sres] + rres, writes=[psres])

        def rhsX(k, lo, hi):
            return X.ap(k, lo, hi), X.res(k, lo, hi)

        def rhsR2(k, lo, hi):
            return R2.ap(k, lo, hi), R2.res(k, lo, hi)

        def load_T(src, R, C, dst_fn, dres, q="sp"):
            per = max(1, min(512 // R, 4))
            pc = 0
            while pc < C:
                w = min(512, C - pc)
                stg, sres = scr_next()
                P.dma(q, lambda e, stg=stg, pc=pc, w=w: e.dma_start(out=stg[0:R, 0:w], in_=src[:, pc:pc + w]),
                      writes=[sres])
                nchunk = w // 128
                c = 0
                while c < nchunk:
                    n = min(per, nchunk - c)
                    ps, psres = ps_next()
                    for i in range(n):
                        P.pe(lambda e, ps=ps, stg=stg, c=c, i=i: e.transpose(
                            out=ps[:, i * R:(i + 1) * R], in_=stg[0:R, (c + i) * 128:(c + i + 1) * 128],
                            identity=ident_f[0:R, 0:R]), reads=[sres, "ident_f"], writes=[psres])
                    dst = dst_fn(pc // 128 + c, n)
                    P.act(lambda e, ps=ps, dst=dst, n=n: e.activation(
                        out=dst, in_=ps[:, 0:n * R].rearrange("p (n r) -> p n r", r=R), func=AF.Copy),
                        reads=[psres], writes=[dres])
                    c += n
                pc += w

        try:
            def simple_load(tile_ap, src, res, q="sp"):
                P.dma(q, lambda e: e.dma_start(out=tile_ap, in_=src), writes=[res])

            simple_load(ident_f[:, :], ident_d, "ident_f")
            simple_load(gmask[:, :], gmask_d, "gmask")
            simple_load(keymask[:, :], keymask_d, "keymask")
            simple_load(smask[:, :], smask_d, "smask")
            simple_load(prat[:, :], prat_d, "prat")
            simple_load(es_s[:, :], sinks, "es")
            P.dma("pool", lambda e: e.dma_start(out=pw_s[:, :, :], in_=pool_w.rearrange("l g k m -> k (l g) m")),
                  writes=["pw"])
            P.act(lambda e: e.activation(out=es_s[:, :], in_=es_s[:, :], func=AF.Exp), reads=["es"], writes=["es"])
            P.dve(lambda e: e.tensor_copy(out=ident_b[:, :], in_=ident_f[:, :]), reads=["ident_f"], writes=["ident_b"])
            P.dve(lambda e: e.memset(ones_b[:, :], 1.0), writes=["ones_b"])
            P.dve(lambda e: e.memset(ones_f[:, :], 1.0), writes=["ones_f"])
            P.dve(lambda e: e.memset(eps_t[:, :], EPS), writes=["eps"])
            P.dve(lambda e: e.memset(Vt[:, :, :, 64:66], 1.0), writes=["Vt_ones"])
            P.dve(lambda e: e.memset(Vt[:, :, :, 0:64], 0.0), writes=[("Vt", b) for b in range(11)])

            stage('s0')
            load_T(lnp, 8, D, lambda c, n: lnp_s[:, c:c + n, :], "lnp")
            load_T(conv_w, 6, 512, lambda c, n: cw_s[:, c:c + n, :], "cw")
            load_T(ffn_cw, 6, 2 * DFF, lambda c, n: fcw_s[:, c:c + n, :], "fcw")
            load_T(pool_scale, 2, 512, lambda c, n: psc_s[:, c:c + n, :], "psc")
            for l in range(2):
                load_T(s_conv[l], 8, 512, lambda c, n, l=l: stc[:, l, c:c + n, :], "stc")
                load_T(s_pool[l], 60, 512, lambda c, n, l=l: stp[:, l, c:c + n, :], "stp")

            stage('s1')
            def load_x_block(src, ntok, col0):
                for hf in range(4):
                    stg, sres = scr_next()
                    P.dma("sp" if hf % 2 == 0 else "pool", lambda e, stg=stg, hf=hf: e.dma_start(out=stg[0:ntok, :], in_=src[:, hf * 512:(hf + 1) * 512]),
                          writes=[sres])
                    ps, psres = ps_next()
                    for i in range(4):
                        P.pe(lambda e, ps=ps, stg=stg, i=i: e.transpose(
                            out=ps[:, i * 128:i * 128 + ntok], in_=stg[0:ntok, i * 128:(i + 1) * 128],
                            identity=ident_f[0:ntok, 0:ntok]), reads=[sres, "ident_f"], writes=[psres])
                    c0 = hf * 4
                    src_ps = ps[:, :].rearrange("p (n r) -> p n r", r=128)[:, :, 0:ntok]
                    wr = []
                    for c in range(c0, c0 + 4):
                        wr += X.res(c, col0, col0 + ntok)
                    P.act(lambda e, src_ps=src_ps, c0=c0: e.activation(
                        out=X_t[:, c0:c0 + 4, col0:col0 + ntok], in_=src_ps, func=AF.Copy), reads=[psres], writes=wr)
                    if col0 >= 128:
                        wr = []
                        for c in range(c0, c0 + 4):
                            wr += xT.res(c, col0, col0 + ntok)
                        P.dve(lambda e, src_ps=src_ps, c0=c0: e.tensor_copy(
                            out=xT_t[:, c0:c0 + 4, col0 - 128:col0 - 128 + ntok], in_=src_ps), reads=[psres], writes=wr)

            for b in range(10):
                load_x_block(xh[b * 128:(b + 1) * 128, :], 128, b * 128)
            load_x_block(xh[1280:1284, :], 4, 1280)
            load_x_block(xs[:, :], NS, NP)

        except _Stop:
            stop = '__done__'
        def layer(l):
            RIN = (0 if l == 0 else 128, NT)
            RMX = (128 if l == 0 else 256, NT)
            RPRE = (RMX[0] - 16, NT)

            def lnvec(c, which):
                return lnp_s[:, c, l * 4 + which:l * 4 + which + 1]

            load_T(s_ffn[l], 8, 2 * DFF, lambda c, n: stf[:, c:c + n, :], "stf")
            for s in range(NS):
                stg, sres = scr_next()
                P.dma("sp", lambda e, stg=stg, s=s: e.dma_start(out=stg[:, 0:256], in_=ck[l, s]), writes=[sres])
                ps, psres = ps_next()
                for c in range(2):
                    P.pe(lambda e, ps=ps, stg=stg, c=c: e.transpose(
                        out=ps[:, c * 128:(c + 1) * 128], in_=stg[:, c * 128:(c + 1) * 128], identity=ident_f[:, :]),
                        reads=[sres, "ident_f"], writes=[psres])
                P.act(lambda e, ps=ps, s=s: e.activation(
                    out=kcT[:, s, :, :], in_=ps[:, 0:256].rearrange("p (c k) -> p c k", c=2), func=AF.Copy),
                    reads=[psres], writes=[("kcT", s)])
                P.dma("pool", lambda e, s=s: e.dma_start(
                    out=veff[:, s, :, :], in_=cv[l, s].rearrange("k (h d) -> k h d", h=4)), writes=[("veff", s)])
                P.dma("sp", lambda e, s=s: e.dma_start(out=ks[l, s, 0:127, :], in_=ck[l, s, 1:128, :]),
                      writes=[("o_ks", l, s, 0)])
                P.dma("sp", lambda e, s=s: e.dma_start(out=vs[l, s, 0:127, :], in_=cv[l, s, 1:128, :]),
                      writes=[("o_vs", l, s, 0)])

            def proj_tile(pieces, rng, rhs=rhsX, nk=16):
                sl, sres = load_slab(pieces, nk)
                outs = []
                for (lo, hi) in groups(*rng):
                    ps, psres = ps_next()
                    mm_group(ps, psres, sl, sres, nk, rhs, lo, hi)
                    outs.append((ps, psres, lo, hi))
                return outs

            def wcols(w, c0, n=128, r0=0, r1=D):
                return w[l, r0:r1, c0:c0 + n]

            for j in range(4):
                slg = load_slab([(wcols(w_in, OGB + 128 * j), 0)], 16)
                slc = load_slab([(wcols(w_in, OGC + 128 * j), 0)], 16)
                slh = load_slab([(wcols(w_in, OH_ + 128 * j), 0)], 16)
                w0 = cw_s[:, j, l * 3 + 0:l * 3 + 1]
                w1 = cw_s[:, j, l * 3 + 1:l * 3 + 2]
                w2 = cw_s[:, j, l * 3 + 2:l * 3 + 3]
                prev = None
                for (lo, hi) in groups(*RPRE):
                    psg, rg = ps_next()
                    mm_group(psg, rg, slg[0], slg[1], 16, rhsX, lo, hi)
                    psc, rc = ps_next()
                    mm_group(psc, rc, slc[0], slc[1], 16, rhsX, lo, hi)
                    psh, rh = ps_next()
                    mm_group(psh, rh, slh[0], slh[1], 16, rhsX, lo, hi)
                    n = hi - lo
                    gbt, gbr = scr_next()
                    gct, gcr = scr_next()
                    ut, ur = scr_next()
                    cvt, cvr = scr_next()
                    P.act(lambda e, gbt=gbt, psg=psg, n=n: e.activation(out=gbt[:, 0:n], in_=psg[:, 0:n], func=AF.Copy),
                          reads=[rg], writes=[gbr])
                    P.act(lambda e, gct=gct, psc=psc, n=n: e.activation(out=gct[:, 0:n], in_=psc[:, 0:n], func=AF.Copy),
                          reads=[rc], writes=[gcr])
                    P.dve(lambda e, ut=ut, gct=gct, psh=psh, n=n: e.tensor_tensor(
                        out=ut[:, 2:2 + n], in0=gct[:, 0:n], in1=psh[:, 0:n], op=ALU.mult), reads=[gcr, rh], writes=[ur])
                    if prev is None:
                        P.dve(lambda e, ut=ut: e.memset(ut[:, 0:2], 0.0), writes=[ur])
                    else:
                        put, pur, pn = prev
                        P.dve(lambda e, ut=ut, put=put, pn=pn: e.tensor_copy(out=ut[:, 0:2], in_=put[:, pn:pn + 2]),
                              reads=[pur], writes=[ur])
                    if lo <= H - 4 and H <= hi:
                        P.dve(lambda e, ut=ut, lo=lo: e.tensor_tensor(
                            out=ut[:, 2 + H - 4 - lo:2 + H - lo], in0=ut[:, 2 + H - 4 - lo:2 + H - lo],
                            in1=smask[:, 0:4], op=ALU.mult), reads=["smask", ur], writes=[ur])
                    prev = (ut, ur, n)
                    phi = min(hi, NP)
                    a = max(lo, RMX[0])
                    if phi > a:
                        m = phi - a
                        o = a - lo
                        P.act(lambda e, cvt=cvt, ut=ut, o=o, m=m: e.activation(
                            out=cvt[:, 0:m], in_=ut[:, o:o + m], func=AF.Copy, scale=w0), reads=[ur, "cw"], writes=[cvr])
                        P.dve(lambda e, cvt=cvt, ut=ut, o=o, m=m: e.scalar_tensor_tensor(
                            out=cvt[:, 0:m], in0=ut[:, o + 1:o + 1 + m], scalar=w1, in1=cvt[:, 0:m],
                            op0=ALU.mult, op1=ALU.add), reads=[ur, "cw", cvr], writes=[cvr])
                        P.dve(lambda e, cvt=cvt, ut=ut, o=o, m=m: e.scalar_tensor_tensor(
                            out=cvt[:, 0:m], in0=ut[:, o + 2:o + 2 + m], scalar=w2, in1=cvt[:, 0:m],
                            op0=ALU.mult, op1=ALU.add), reads=[ur, "cw", cvr], writes=[cvr])
                        P.dve(lambda e, cvt=cvt, gbt=gbt, o=o, m=m, a=a, j=j: e.tensor_tensor(
                            out=R2.ap(j, a, a + m), in0=gbt[:, o:o + m], in1=cvt[:, 0:m], op=ALU.mult),
                            reads=[gbr, cvr], writes=R2.res(j, a, a + m))
                    if hi == NT:
                        so = NP - lo
                        st0 = stc[:, l, j, :].rearrange("p (s r) -> p r s", r=2)[:, 0, :]
                        st1 = stc[:, l, j, :].rearrange("p (s r) -> p r s", r=2)[:, 1, :]
                        cs = small[:, 0:4]
                        P.dve(lambda e, cs=cs, st0=st0: e.tensor_scalar(out=cs, in0=st0, scalar1=w0, scalar2=None, op0=ALU.mult),
                              reads=["stc", "cw"], writes=["small"])
                        P.dve(lambda e, cs=cs, st1=st1: e.scalar_tensor_tensor(
                            out=cs, in0=st1, scalar=w1, in1=cs, op0=ALU.mult, op1=ALU.add), reads=["stc", "cw", "small"], writes=["small"])
                        P.dve(lambda e, cs=cs, ut=ut, so=so: e.scalar_tensor_tensor(
                            out=cs, in0=ut[:, 2 + so:2 + so + 4], scalar=w2, in1=cs, op0=ALU.mult, op1=ALU.add),
                            reads=[ur, "cw", "small"], writes=["small"])
                        P.dve(lambda e, cs=cs, gbt=gbt, so=so, j=j: e.tensor_tensor(
                            out=R2.ap(j, NP, NT), in0=gbt[:, so:so + 4], in1=cs, op=ALU.mult),
                            reads=[gbr, "small"], writes=R2.res(j, NP, NT))
                        P.dve(lambda e, ut=ut, so=so, j=j: e.tensor_copy(out=cst[:, j, 0:2], in_=ut[:, so:so + 2]),
                              reads=[ur], writes=[("cst", j)])
                        cv_ = cst[:, j, 2:10].rearrange("p (s r) -> p r s", r=2)
                        P.dve(lambda e, cv_=cv_, st1=st1: e.tensor_copy(out=cv_[:, 0, :], in_=st1), reads=["stc"], writes=[("cst", j)])
                        P.dve(lambda e, cv_=cv_, ut=ut, so=so: e.tensor_copy(out=cv_[:, 1, :], in_=ut[:, 2 + so:2 + so + 4]),
                              reads=[ur], writes=[("cst", j)])
            ps, psres = ps_next()
            for j in range(4):
                P.pe(lambda e, ps=ps, j=j: e.transpose(out=ps[0:10, j * 128:(j + 1) * 128], in_=cst[:, j, :], identity=ident_f[:, :]),
                     reads=[("cst", j), "ident_f"], writes=[psres])
            ot, otr = scr_next()
            P.act(lambda e, ot=ot, ps=ps: e.activation(out=ot[0:10, :], in_=ps[0:10, :], func=AF.Copy), reads=[psres], writes=[otr])
            P.dma("sp", lambda e, ot=ot: e.dma_start(out=convp[l], in_=ot[0:2, :]), reads=[otr], writes=[("o_convp", l)])
            P.dma("sp", lambda e, ot=ot: e.dma_start(out=convs[l], in_=ot[2:10, :]), reads=[otr], writes=[("o_convs", l)])

            stage(f'L{l}_conv')
            tps = {}
            prevp_d = {}

            def pool_body(g, gi):
                wsz = 2 << g
                tp = tps[g]
                psp, rp, lo, hi = tp[gi]
                n = hi - lo
                a0, a0r = scr_next()
                P.act(lambda e, a0=a0, psp=psp, n=n: e.activation(out=a0[:, 16:16 + n], in_=psp[:, 0:n], func=AF.Copy),
                      reads=[rp], writes=[a0r])
                prevp = prevp_d.get(g)
                if prevp is None:
                    P.dve(lambda e, a0=a0: e.memset(a0[:, 0:16], 0.0), writes=[a0r])
                else:
                    pa, par, pn = prevp
                    P.dve(lambda e, a0=a0, pa=pa, pn=pn: e.tensor_copy(out=a0[:, 0:16], in_=pa[:, pn:pn + 16]),
                          reads=[par], writes=[a0r])
                if lo <= H - 16 and H <= hi:
                    o = 16 + H - 16 - lo
                    P.dve(lambda e, a0=a0, o=o: e.tensor_tensor(out=a0[:, o:o + 16], in0=a0[:, o:o + 16], in1=smask[:, :], op=ALU.mult),
                          reads=["smask", a0r], writes=[a0r])
                prevp_d[g] = (a0, a0r, n)
                yield
                phi = min(hi, NP)
                a = max(lo, RMX[0])
                if phi > a:
                    m = phi - a
                    o = 16 + a - lo
                    ta, tar = scr_next()
                    tb, tbr = scr_next()
                    cur, curr, ext = a0, a0r, 15
                    bufs = [(ta, tar), (tb, tbr)]
                    sh = 1
                    lvl = 0
                    while sh < wsz:
                        nxt, nxtr = bufs[lvl % 2]
                        ext2 = ext - sh
                        P.dve(lambda e, nxt=nxt, cur=cur, o=o, m=m, ext2=ext2, sh=sh: e.tensor_tensor(
                            out=nxt[:, o - ext2:o + m], in0=cur[:, o - ext2:o + m], in1=cur[:, o - ext2 - sh:o + m - sh], op=ALU.add),
                            reads=[curr], writes=[nxtr])
                        cur, curr, ext = nxt, nxtr, ext2
                        yield
                        sh *= 2
                        lvl += 1
                    if a <= H and H + 16 <= phi:
                        oo = o + H - a
                        P.dve(lambda e, cur=cur, oo=oo, g=g: e.tensor_tensor(
                            out=cur[:, oo:oo + 16], in0=cur[:, oo:oo + 16], in1=prat[:, g * 16:(g + 1) * 16], op=ALU.mult),
                            reads=["prat", curr], writes=[curr])
                    dt_, dtr = bufs[lvl % 2]
                    dtb = dt_[:, :].bitcast(BF16)
                    P.dve(lambda e, dtb=dtb, cur=cur, a0=a0, o=o, m=m, wsz=wsz: e.scalar_tensor_tensor(
                        out=dtb[:, 0:m], in0=cur[:, o:o + m], scalar=1.0 / wsz, in1=a0[:, o:o + m], op0=ALU.mult, op1=ALU.subtract),
                        reads=[curr, a0r], writes=[dtr])
                    yield
                    ps2, ps2r = ps_next()
                    P.pe(lambda e, ps2=ps2, dtb=dtb, m=m, g=g: e.matmul(ps2[:, 0:m], lhsT=pw_s[:, l * 4 + g, :], rhs=dtb[:, 0:m], start=True, stop=True),
                         reads=[dtr, "pw"], writes=[ps2r])
                    P.act(lambda e, ps2=ps2, m=m, a=a, g=g: e.activation(
                        out=R2.ap(4 + g, a, a + m), in_=ps2[:, 0:m], func=AF.Copy, scale=psc_s[:, g, l:l + 1]),
                        reads=[ps2r, "psc"], writes=R2.res(4 + g, a, a + m))
                if hi == NT:
                    so = 16 + NP - lo
                    P.dve(lambda e, a0=a0, so=so, g=g: e.tensor_copy(out=pst[:, g, 0:15], in_=a0[:, so - 15:so]),
                          reads=[a0r], writes=[("pst", g)])
                    pv = pst[:, g, 15:75].rearrange("p (s r) -> p s r", r=15)
                    sv = stp[:, l, g, :].rearrange("p (s r) -> p s r", r=15)
                    P.dve(lambda e, pv=pv, sv=sv: e.tensor_copy(out=pv[:, :, 0:14], in_=sv[:, :, 1:15]), reads=["stp"], writes=[("pst", g)])
                    P.dve(lambda e, pv=pv, a0=a0, so=so: e.tensor_copy(out=pv[:, :, 14], in_=a0[:, so:so + 4]),
                          reads=[a0r], writes=[("pst", g)])
                    ws_ = small[:, 8:12]
                    if wsz == 16:
                        P.dve(lambda e, ws_=ws_, sv=sv: e.tensor_reduce(out=ws_, in_=sv[:, :, 0:15], axis=AX.X, op=ALU.add),
                              reads=["stp"], writes=["small"])
                    else:
                        P.dve(lambda e, ws_=ws_, sv=sv, wsz=wsz: e.tensor_reduce(out=ws_, in_=sv[:, :, 16 - wsz:15], axis=AX.X, op=ALU.add),
                              reads=["stp"], writes=["small"])
                    P.dve(lambda e, ws_=ws_, a0=a0, so=so: e.tensor_tensor(out=ws_, in0=ws_, in1=a0[:, so:so + 4], op=ALU.add),
                          reads=[a0r, "small"], writes=["small"])
                    dsb = small[:, 16:20].bitcast(BF16)
                    P.dve(lambda e, ws_=ws_, a0=a0, so=so, dsb=dsb, wsz=wsz: e.scalar_tensor_tensor(
                        out=dsb[:, 0:4], in0=ws_, scalar=1.0 / wsz, in1=a0[:, so:so + 4], op0=ALU.mult, op1=ALU.subtract),
                        reads=[a0r, "small"], writes=["small"])
                    ps2, ps2r = ps_next()
                    P.pe(lambda e, ps2=ps2, dsb=dsb, g=g: e.matmul(ps2[:, 0:4], lhsT=pw_s[:, l * 4 + g, :], rhs=dsb[:, 0:4], start=True, stop=True),
                         reads=["small", "pw"], writes=[ps2r])
                    P.act(lambda e, ps2=ps2, g=g: e.activation(
                        out=R2.ap(4 + g, NP, NT), in_=ps2[:, 0:4], func=AF.Copy, scale=psc_s[:, g, l:l + 1]),
                        reads=[ps2r, "psc"], writes=R2.res(4 + g, NP, NT))

            for gp in (0, 2):
                tps[gp] = proj_tile([(wcols(w_in, OPIN + 128 * gp), 0)], RPRE)
                tps[gp + 1] = proj_tile([(wcols(w_in, OPIN + 128 * (gp + 1)), 0)], RPRE)
                for gi in range(len(tps[gp])):
                    gens = [pool_body(gp, gi), pool_body(gp + 1, gi)]
                    while gens:
                        for gen in list(gens):
                            try:
                                next(gen)
                            except StopIteration:
                                gens.remove(gen)
            ps, psres = ps_next()
            for g in range(4):
                P.pe(lambda e, ps=ps, g=g: e.transpose(out=ps[0:75, g * 128:(g + 1) * 128], in_=pst[:, g, :], identity=ident_f[:, :]),
                     reads=[("pst", g), "ident_f"], writes=[psres])
            ot, otr = scr_next()
            P.act(lambda e, ot=ot, ps=ps: e.activation(out=ot[0:75, :], in_=ps[0:75, :], func=AF.Copy), reads=[psres], writes=[otr])
            P.dma("sp", lambda e, ot=ot: e.dma_start(out=poolp[l], in_=ot[0:15, :]), reads=[otr], writes=[("o_poolp", l)])
            P.dma("sp", lambda e, ot=ot: e.dma_start(out=pools[l], in_=ot[15:75, :]), reads=[otr], writes=[("o_pools", l)])

            stage(f'L{l}_pool')
            for m in range(NCH):
                sl, sres = load_slab([(w_o[l, 1024:2048, m * 128:(m + 1) * 128], 0)], 8)
                for (lo, hi) in groups(*RMX):
                    ps, psres = ps_next()
                    mm_group(ps, psres, sl, sres, 8, rhsR2, lo, hi)
                    P.dve(lambda e, ps=ps, lo=lo, hi=hi, m=m: e.scalar_tensor_tensor(
                        out=xT.ap(m, lo, hi), in0=xT.ap(m, lo, hi), scalar=ALPHA, in1=ps[:, 0:hi - lo], op0=ALU.mult, op1=ALU.add),
                        reads=[psres] + xT.res(m, lo, hi), writes=xT.res(m, lo, hi))

            stage(f'L{l}_wo1')
            qhead = {}
            for t in range(8):
                gA = 2 * (t // 4)
                hA = 4 * gA + t % 4
                hB = 4 * (gA + 1) + t % 4
                qhead[hA] = (t, 0)
                qhead[hB] = (t, 64)
                outs = proj_tile([(wcols(w_in, OQ + 64 * hA, 64), 0), (wcols(w_in, OQ + 64 * hB, 64), 64)], RMX)
                for (ps, psres, lo, hi) in outs:
                    P.act(lambda e, ps=ps, lo=lo, hi=hi, t=t: e.activation(out=R2.ap(t, lo, hi), in_=ps[:, 0:hi - lo], func=AF.Copy),
                          reads=[psres], writes=R2.res(t, lo, hi))
            for c in range(2):
                sl, sres = load_slab([(wcols(w_in, OK_ + 128 * c), 0)], 16)
                for (lo, hi) in groups(*RIN):
                    ps, psres = ps_next()
                    mm_group(ps, psres, sl, sres, 16, rhsX, lo, hi)
                    P.act(lambda e, ps=ps, lo=lo, hi=hi, c=c: e.activation(out=R2.ap(8 + c, lo, hi), in_=ps[:, 0:hi - lo], func=AF.Copy),
                          reads=[psres], writes=R2.res(8 + c, lo, hi))
                ps, psres = ps_next()
                for k in range(16):
                    P.pe(lambda e, ps=ps, k=k, sl=sl: e.matmul(ps[:, 0:128], lhsT=X.ap(k, NP - 128, NP), rhs=sl[:, k, :], start=(k == 0), stop=(k == 15)),
                         reads=[sres] + X.res(k, NP - 128, NP), writes=[psres])
                ot, otr = scr_next()
                P.act(lambda e, ps=ps, ot=ot: e.activation(out=ot[:, 0:128], in_=ps[:, 0:128], func=AF.Copy), reads=[psres], writes=[otr])
                P.dma("sp", lambda e, ot=ot, c=c: e.dma_start(out=kp[l, :, c * 128:(c + 1) * 128], in_=ot[:, 0:128]), reads=[otr], writes=[("o_kp", l, c)])
                ps, psres = ps_next()
                for k in range(16):
                    P.pe(lambda e, ps=ps, k=k, sl=sl: e.matmul(ps[0:NS, 0:128], lhsT=X.ap(k, NP, NT), rhs=sl[:, k, :], start=(k == 0), stop=(k == 15)),
                         reads=[sres] + X.res(k, NP, NT), writes=[psres])
                ot, otr = scr_next()
                P.act(lambda e, ps=ps, ot=ot: e.activation(out=ot[0:NS, 0:128], in_=ps[0:NS, 0:128], func=AF.Copy), reads=[psres], writes=[otr])
                P.dma("sp", lambda e, ot=ot, c=c: e.dma_start(out=ks[l, :, 127, c * 128:(c + 1) * 128], in_=ot[0:NS, 0:128]), reads=[otr],
                      writes=[("o_ks", l, 9, 1 + c)])
            for s in range(NS):
                P.dve(lambda e, s=s: e.tensor_copy(out=kcT[:, s, :, 0], in_=R2b_t[:, 0:2, NP + s]),
                      reads=R2.res(8, NP, NT) + R2.res(9, NP, NT), writes=[("kcT", s)])
            blocks = [(b * 128, min((b + 1) * 128, NT)) for b in range(RIN[0] // 128, 11)]
            for c in range(2):
                sl, sres = load_slab([(wcols(w_in, OV + 128 * c), 0)], 16)
                for bi in range(0, len(blocks), 4):
                    ps, psres = ps_next()
                    blk = blocks[bi:bi + 4]
                    for i, (lo, hi) in enumerate(blk):
                        for k in range(16):
                            P.pe(lambda e, ps=ps, k=k, sl=sl, lo=lo, hi=hi, i=i: e.matmul(
                                ps[0:hi - lo, i * 128:(i + 1) * 128], lhsT=X.ap(k, lo, hi), rhs=sl[:, k, :], start=(k == 0), stop=(k == 15)),
                                reads=[sres] + X.res(k, lo, hi), writes=[psres])
                    b0 = blk[0][0] // 128
                    nb = len(blk)
                    full = [x for x in blk if x[1] - x[0] == 128]
                    nf = len(full)
                    if nf:
                        P.act(lambda e, ps=ps, b0=b0, nf=nf, c=c: e.activation(
                            out=Vt[:, b0:b0 + nf, 2 * c:2 * c + 2, 0:64],
                            in_=ps[:, 0:nf * 128].rearrange("p (b h d) -> p b h d", h=2, d=64), func=AF.Copy),
                            reads=[psres], writes=[("Vt", b) for b in range(b0, b0 + nf)])
                    if nf < nb:
                        lo, hi = blk[-1]
                        n = hi - lo
                        P.act(lambda e, ps=ps, b0=b0, nf=nf, c=c, n=n: e.activation(
                            out=Vt[0:n, b0 + nf, 2 * c:2 * c + 2, 0:64],
                            in_=ps[0:n, nf * 128:(nf + 1) * 128].rearrange("p (h d) -> p h d", h=2), func=AF.Copy),
                            reads=[psres], writes=[("Vt", b0 + nf)])
                ps, psres = ps_next()
                for k in range(16):
                    P.pe(lambda e, ps=ps, k=k, sl=sl: e.matmul(ps[:, 0:128], lhsT=X.ap(k, NP - 128, NP), rhs=sl[:, k, :], start=(k == 0), stop=(k == 15)),
                         reads=[sres] + X.res(k, NP - 128, NP), writes=[psres])
                ot, otr = scr_next()
                P.act(lambda e, ps=ps, ot=ot: e.activation(out=ot[:, 0:128], in_=ps[:, 0:128], func=AF.Copy), reads=[psres], writes=[otr])
                P.dma("sp", lambda e, ot=ot, c=c: e.dma_start(out=vp[l, :, c * 128:(c + 1) * 128], in_=ot[:, 0:128]), reads=[otr], writes=[("o_vp", l, c)])
                ps, psres = ps_next()
                for k in range(16):
                    P.pe(lambda e, ps=ps, k=k, sl=sl: e.matmul(ps[0:NS, 0:128], lhsT=X.ap(k, NP, NT), rhs=sl[:, k, :], start=(k == 0), stop=(k == 15)),
                         reads=[sres] + X.res(k, NP, NT), writes=[psres])
                ot, otr = scr_next()
                P.act(lambda e, ps=ps, ot=ot: e.activation(out=ot[0:NS, 0:128], in_=ps[0:NS, 0:128], func=AF.Copy), reads=[psres], writes=[otr])
                P.dma("sp", lambda e, ot=ot, c=c: e.dma_start(out=vs[l, :, 127, c * 128:(c + 1) * 128], in_=ot[0:NS, 0:128]), reads=[otr],
                      writes=[("o_vs", l, 9, 1 + c)])
                for s in range(NS):
                    ps, psres = ps_next()
                    for k in range(16):
                        P.pe(lambda e, ps=ps, k=k, sl=sl, s=s: e.matmul(ps[0:1, 0:128], lhsT=X.ap(k, NP + s, NP + s + 1), rhs=sl[:, k, :],
                                                                          start=(k == 0), stop=(k == 15)),
                             reads=[sres] + X.res(k, NP, NT), writes=[psres])
                    P.act(lambda e, ps=ps, s=s, c=c: e.activation(
                        out=veff[0:1, s, 2 * c:2 * c + 2, :], in_=ps[0:1, 0:128].rearrange("p (h d) -> p h d", h=2), func=AF.Copy),
                        reads=[psres], writes=[("veff", s)])

            stage(f'L{l}_qkv')
            qblocks = [(q0, min(q0 + 128, NP)) for q0 in range(RMX[0], NP, 128)]
            DEPTH = 3
            for kvh in range(4):
                P.dma("sp", lambda e, kvh=kvh: e.dma_start(out=Gt[:, :, :], in_=tbl[4 * kvh:4 * kvh + 4].rearrange("h k q -> k h q")),
                      writes=["Gt"])
                for j in range(4):
                    P.dve(lambda e, j=j: e.tensor_tensor(out=Gt[:, j, :], in0=Gt[:, j, :], in1=gmask[:, :], op=ALU.add),
                          reads=["Gt", "gmask"], writes=["Gt"])
                kc = 8 + kvh // 2
                kb = 64 * (kvh % 2)
                pairs = [(bi, j) for bi in range(len(qblocks)) for j in range(4)]
                pinfo = {}
                binfo = {}
                pendC2 = []

                def stageA(pi):
                    bi, j = pairs[pi]
                    q0, q1 = qblocks[bi]
                    nq = q1 - q0
                    blk = q0 // 128
                    h = 4 * kvh + j
                    qt, qb = qhead[h]
                    assert qb == kb
                    pss, pssr = ps_next()
                    P.pe(lambda e: e.matmul(pss[:, 0:nq], lhsT=R2.ap(kc, q0 - 128, q0, kb, kb + 64), rhs=R2.ap(qt, q0, q1, kb, kb + 64),
                                            start=True, stop=True),
                         reads=R2.res(kc, q0 - 128, q0) + R2.res(qt, q0, q1), writes=[pssr])
                    P.pe(lambda e: e.matmul(pss[0:nq, 128:128 + nq], lhsT=R2.ap(kc, q0, q1, kb, kb + 64), rhs=R2.ap(qt, q0, q1, kb, kb + 64),
                                            start=True, stop=True),
                         reads=R2.res(kc, q0, q1) + R2.res(qt, q0, q1), writes=[pssr])
                    sbt, sbr = scr_next()
                    ptb = sbt[:, :].bitcast(BF16)[:, 512:768]
                    if nq == 128:
                        P.dve(lambda e: e.scalar_tensor_tensor(out=sbt[:, 0:256], in0=pss[:, 0:256], scalar=SCALE, in1=Gt[:, j, :],
                                                               op0=ALU.mult, op1=ALU.add), reads=[pssr, "Gt"], writes=[sbr])
                    else:
                        P.dve(lambda e: e.scalar_tensor_tensor(out=sbt[:, 0:nq], in0=pss[:, 0:nq], scalar=SCALE, in1=Gt[:, j, 0:nq],
                                                               op0=ALU.mult, op1=ALU.add), reads=[pssr, "Gt"], writes=[sbr])
                        P.dve(lambda e: e.scalar_tensor_tensor(out=sbt[0:nq, 128:128 + nq], in0=pss[0:nq, 128:128 + nq], scalar=SCALE,
                                                               in1=Gt[0:nq, j, 128:128 + nq], op0=ALU.mult, op1=ALU.add),
                              reads=[pssr, "Gt"], writes=[sbr])
                    if q0 >= 512 and nq == 128:
                        P.act(lambda e: e.activation(out=ptb[:, 0:256], in_=sbt[:, 0:256], func=AF.Exp), reads=[sbr], writes=[sbr])
                    else:
                        P.act(lambda e: e.activation(out=ptb[:, 0:nq], in_=sbt[:, 0:nq], func=AF.Exp, bias=keymask[:, blk - 1:blk]),
                              reads=[sbr, "keymask"], writes=[sbr])
                        P.act(lambda e: e.activation(out=ptb[0:nq, 128:128 + nq], in_=sbt[0:nq, 128:128 + nq], func=AF.Exp,
                                                     bias=keymask[0:nq, blk:blk + 1]), reads=[sbr, "keymask"], writes=[sbr])
                    pinfo[pi] = (ptb, sbr)

                def stageB(pi):
                    bi, j = pairs[pi]
                    q0, q1 = qblocks[bi]
                    nq = q1 - q0
                    blk = q0 // 128
                    if j == 0:
                        binfo[bi] = ps_next()
                    pso, psor = binfo[bi]
                    ptb, sbr = pinfo.pop(pi)
                    P.pe(lambda e: e.matmul(pso[0:nq, j * 66:j * 66 + 65], lhsT=ptb[:, 0:nq], rhs=Vt[:, blk - 1, kvh, 0:65], start=True, stop=False),
                         reads=[sbr, ("Vt", blk - 1), "Vt_ones"], writes=[psor])
                    P.pe(lambda e: e.matmul(pso[0:nq, j * 66:j * 66 + 65], lhsT=ptb[0:nq, 128:128 + nq], rhs=Vt[0:nq, blk, kvh, 0:65], start=False, stop=True),
                         reads=[sbr, ("Vt", blk), "Vt_ones"], writes=[psor])
                    if j == 3:
                        dn, dnr = scr_next()
                        atb = dn[:, :].bitcast(BF16)[:, 512:768]
                        P.dve(lambda e: e.tensor_tensor(
                            out=dn[0:nq, 0:4], in0=pso[0:nq, 0:264].rearrange("p (j c) -> p j c", c=66)[:, :, 64],
                            in1=es_s[0:nq, l * 16 + 4 * kvh:l * 16 + 4 * kvh + 4], op=ALU.add), reads=[psor, "es"], writes=[dnr])
                        P.dve(lambda e: e.reciprocal(out=dn[0:nq, 4:8], in_=dn[0:nq, 0:4]), reads=[dnr], writes=[dnr])
                        P.dve(lambda e: e.tensor_tensor(
                            out=atb[0:nq, 0:256].rearrange("p (j d) -> p j d", d=64),
                            in0=pso[0:nq, 0:264].rearrange("p (j c) -> p j c", c=66)[:, :, 0:64],
                            in1=dn[0:nq, 4:8].unsqueeze(2).to_broadcast([nq, 4, 64]), op=ALU.mult), reads=[psor, dnr], writes=[dnr])
                        pendC2.append((pi + 3, atb, dnr, q0, nq))

                def stageC2(atb, dnr, q0, nq):
                    pst_, pstr = ps_next()
                    pstb = pst_[:, :].bitcast(BF16)
                    for i in range(2):
                        P.pe(lambda e, i=i: e.transpose(out=pstb[:, i * 128:i * 128 + nq], in_=atb[0:nq, i * 128:(i + 1) * 128],
                                                        identity=ident_b[0:nq, 0:nq]), reads=[dnr, "ident_b"], writes=[pstr])
                    P.dve(lambda e: e.tensor_copy(
                        out=X_t[:, 2 * kvh:2 * kvh + 2, q0:q0 + nq], in_=pstb[:, 0:256].rearrange("p (i q) -> p i q", i=2)[:, :, 0:nq]),
                        reads=[pstr], writes=X.res(2 * kvh, q0, q0 + nq) + X.res(2 * kvh + 1, q0, q0 + nq))

                npair = len(pairs)
                for step in range(npair + DEPTH):
                    if step < npair:
                        stageA(step)
                    pb = step - DEPTH
                    if pb >= 0:
                        stageB(pb)
                        while pendC2 and pendC2[0][0] <= pb:
                            _, atb, dnr, q0_, nq_ = pendC2.pop(0)
                            stageC2(atb, dnr, q0_, nq_)
                while pendC2:
                    _, atb, dnr, q0_, nq_ = pendC2.pop(0)
                    stageC2(atb, dnr, q0_, nq_)

                sinfo = []
                for j in range(4):
                    h = 4 * kvh + j
                    qt, qb = qhead[h]
                    P.dve(lambda e, j=j: e.tensor_copy(out=sbias[:, j:j + 1], in_=Gt[:, j, 0:1]), reads=["Gt"], writes=[("sbias", j)])
                    P.dve(lambda e, j=j: e.tensor_copy(out=sbias[0:1, j:j + 1], in_=Gt[0:1, j, 128:129]), reads=["Gt"], writes=[("sbias", j)])
                for j in range(4):
                    h = 4 * kvh + j
                    qt, qb = qhead[h]
                    pss, pssr = ps_next()
                    for s in range(NS):
                        P.pe(lambda e, s=s: e.matmul(pss[:, s:s + 1], lhsT=kcT[kb:kb + 64, s, kvh // 2, :],
                                                     rhs=R2.ap(qt, NP + s, NP + s + 1, kb, kb + 64), start=True, stop=True),
                             reads=[("kcT", s)] + R2.res(qt, NP, NT), writes=[pssr])
                    sbt, sbr = scr_next()
                    P.dve(lambda e, j=j, sbt=sbt, pss=pss: e.tensor_scalar(out=sbt[:, 0:4], in0=pss[:, 0:4], scalar1=SCALE, scalar2=sbias[:, j:j + 1],
                                                                         op0=ALU.mult, op1=ALU.add), reads=[pssr, ("sbias", j)], writes=[sbr])
                    ptb = sbt[:, :].bitcast(BF16)
                    P.act(lambda e, sbt=sbt, ptb=ptb: e.activation(out=ptb[:, 512:516], in_=sbt[:, 0:4], func=AF.Exp), reads=[sbr], writes=[sbr])
                    sinfo.append((ptb, sbr))
                for j in range(4):
                    h = 4 * kvh + j
                    ob = 64 * (h % 2)
                    ptb, sbr = sinfo[j]
                    pso, psor = ps_next()
                    for s in range(NS):
                        P.pe(lambda e, s=s, pso=pso, ptb=ptb: e.matmul(pso[ob:ob + 64, s:s + 1], lhsT=veff[:, s, kvh, :], rhs=ptb[:, 512 + s:513 + s],
                                                                       start=True, stop=True), reads=[sbr, ("veff", s)], writes=[psor])
                    P.pe(lambda e, pso=pso, ptb=ptb: e.matmul(pso[ob:ob + 64, 4:8], lhsT=ones_b[:, 0:64], rhs=ptb[:, 512:516], start=True, stop=True),
                         reads=[sbr, "ones_b"], writes=[psor])
                    dn, dnr = scr_next()
                    P.dve(lambda e, dn=dn, pso=pso: e.tensor_scalar(out=dn[ob:ob + 64, 0:4], in0=pso[ob:ob + 64, 4:8],
                                                                    scalar1=es_s[ob:ob + 64, l * 16 + h:l * 16 + h + 1], scalar2=None, op0=ALU.add),
                          reads=[psor, "es"], writes=[dnr])
                    P.dve(lambda e, dn=dn: e.reciprocal(out=dn[ob:ob + 64, 4:8], in_=dn[ob:ob + 64, 0:4]), reads=[dnr], writes=[dnr])
                    P.dve(lambda e, dn=dn, pso=pso: e.tensor_tensor(out=X.ap(h // 2, NP, NT, ob, ob + 64), in0=pso[ob:ob + 64, 0:4],
                                                                    in1=dn[ob:ob + 64, 4:8], op=ALU.mult),
                          reads=[psor, dnr], writes=X.res(h // 2, NP, NT))

            stage(f'L{l}_attn')
            for m in range(NCH):
                sl, sres = load_slab([(w_o[l, 0:1024, m * 128:(m + 1) * 128], 0)], 8)
                for (lo, hi) in groups(*RMX):
                    ps, psres = ps_next()
                    mm_group(ps, psres, sl, sres, 8, rhsX, lo, hi)
                    P.dve(lambda e, ps=ps, lo=lo, hi=hi, m=m: e.tensor_tensor(
                        out=xT.ap(m, lo, hi), in0=xT.ap(m, lo, hi), in1=ps[:, 0:hi - lo], op=ALU.add),
                        reads=[psres] + xT.res(m, lo, hi), writes=xT.res(m, lo, hi))

            stage(f'L{l}_wo2')
            def layer_norm(gi_, bi_):
                G = groups(*RMX)
                info = {}

                def stats(gi):
                    lo, hi = G[gi]
                    n = hi - lo
                    ps1, ps1r = ps_next()
                    ps2, ps2r = ps_next()
                    for c in range(NCH):
                        P.pe(lambda e, c=c: e.matmul(ps1[:, 0:n], lhsT=ones_f[:, :], rhs=xT.ap(c, lo, hi), start=(c == 0), stop=(c == NCH - 1)),
                             reads=xT.res(c, lo, hi) + ["ones_f"], writes=[ps1r])
                        sq, sqr = scr_next()
                        sqb = sq[:, :].bitcast(BF16)
                        P.act(lambda e, c=c, sqb=sqb: e.activation(out=sqb[:, 0:n], in_=xT.ap(c, lo, hi), func=AF.Square),
                              reads=xT.res(c, lo, hi), writes=[sqr])
                        P.pe(lambda e, c=c, sqb=sqb: e.matmul(ps2[:, 0:n], lhsT=ones_b[:, :], rhs=sqb[:, 0:n], start=(c == 0), stop=(c == NCH - 1)),
                             reads=[sqr, "ones_b"], writes=[ps2r])
                    rstd = rstd_t[gi]
                    rr = ("rstd", gi)
                    P.dve(lambda e: e.tensor_scalar(out=rstd[:, 0:n], in0=ps1[:, 0:n], scalar1=1.0 / D, scalar2=None, op0=ALU.mult),
                          reads=[ps1r], writes=[rr])
                    P.dve(lambda e: e.tensor_tensor(out=rstd[:, 0:n], in0=rstd[:, 0:n], in1=rstd[:, 0:n], op=ALU.mult), reads=[rr], writes=[rr])
                    P.dve(lambda e: e.scalar_tensor_tensor(out=rstd[:, 0:n], in0=ps2[:, 0:n], scalar=1.0 / D, in1=rstd[:, 0:n],
                                                           op0=ALU.mult, op1=ALU.subtract), reads=[ps2r, rr], writes=[rr])
                    P.act(lambda e: e.activation(out=rstd[:, 0:n], in_=rstd[:, 0:n], func=AF.Ln, bias=eps_t[:, 0:1]), reads=[rr, "eps"], writes=[rr])
                    P.act(lambda e: e.activation(out=rstd[:, 0:n], in_=rstd[:, 0:n], func=AF.Exp, scale=-0.5), reads=[rr], writes=[rr])
                    info[gi] = (ps1, ps1r, rstd, rr)

                def norm(gi):
                    lo, hi = G[gi]
                    n = hi - lo
                    ps1, ps1r, rstd, rr = info[gi]
                    pend = []

                    def xcopy(c):
                        if c % 2 == 0:
                            P.act(lambda e: e.activation(out=X.ap(c, lo, hi), in_=xT.ap(c, lo, hi), func=AF.Copy),
                                  reads=xT.res(c, lo, hi), writes=X.res(c, lo, hi))
                        else:
                            P.dve(lambda e: e.tensor_copy(out=X.ap(c, lo, hi), in_=xT.ap(c, lo, hi)),
                                  reads=xT.res(c, lo, hi), writes=X.res(c, lo, hi))

                    def sub_(c):
                        P.dve(lambda e: e.scalar_tensor_tensor(out=xT.ap(c, lo, hi), in0=ps1[:, 0:n], scalar=-1.0 / D, in1=xT.ap(c, lo, hi),
                                                               op0=ALU.mult, op1=ALU.add),
                              reads=[ps1r] + xT.res(c, lo, hi), writes=xT.res(c, lo, hi))

                    def mul_aff(c):
                        P.dve(lambda e: e.tensor_tensor(out=xT.ap(c, lo, hi), in0=xT.ap(c, lo, hi), in1=rstd[:, 0:n], op=ALU.mult),
                              reads=[rr] + xT.res(c, lo, hi), writes=xT.res(c, lo, hi))
                        P.act(lambda e: e.activation(out=xT.ap(c, lo, hi), in_=xT.ap(c, lo, hi), func=AF.Identity,
                                                     scale=lnvec(c, gi_), bias=lnvec(c, bi_)),
                              reads=["lnp"] + xT.res(c, lo, hi), writes=xT.res(c, lo, hi))

                    sub_(0)
                    for c in range(NCH):
                        if c + 1 < NCH:
                            sub_(c + 1)
                        mul_aff(c)
                        pend.append(c)
                        if len(pend) > 2:
                            xcopy(pend.pop(0))
                    while pend:
                        xcopy(pend.pop(0))

                stats(0)
                for gi in range(len(G)):
                    if gi + 1 < len(G):
                        stats(gi + 1)
                    norm(gi)

            layer_norm(0, 1)

            stage(f'L{l}_ln1')
            FG = groups(*RMX)
            P.dma("sp", lambda e: e.dma_start(out=ffns[l].rearrange("(s r) c -> s r c", r=2)[:, 0, :],
                                              in_=s_ffn[l].rearrange("(s r) c -> s r c", r=2)[:, 1, :]), writes=[("o_ffns0", l)])
            for b in range(NBATCH):
                for jj in range(JB):
                    pair = b * JB + jj
                    hv_info = []
                    for hv in range(2):
                        tix = pair + hv * (NFT // 2)
                        sl, sres = load_slab([(wcols(w_up, tix * 128), 0)], 16)
                        hv_info.append((tix, sl, sres))
                    for (lo, hi) in FG:
                        glo = lo - 2
                        n2 = hi - glo
                        phi = min(hi, NP)
                        m = phi - lo
                        cv_pair = []
                        for hv in range(2):
                            tix, sl, sres = hv_info[hv]
                            fw0 = fcw_s[:, tix, l * 3 + 0:l * 3 + 1]
                            fw1 = fcw_s[:, tix, l * 3 + 1:l * 3 + 2]
                            fw2 = fcw_s[:, tix, l * 3 + 2:l * 3 + 3]
                            ps, psres = ps_next()
                            mm_group(ps, psres, sl, sres, 16, rhsX, glo, hi)
                            if glo <= H - 4 and H <= hi:
                                o = H - 4 - glo
                                P.dve(lambda e, ps=ps, o=o: e.tensor_tensor(out=ps[:, o:o + 4], in0=ps[:, o:o + 4], in1=smask[:, 0:4], op=ALU.mult),
                                      reads=["smask", psres], writes=[psres])
                            cvt, cvr = scr_next()
                            P.act(lambda e, cvt=cvt, ps=ps, fw0=fw0: e.activation(out=cvt[:, 0:m], in_=ps[:, 0:m], func=AF.Copy, scale=fw0),
                                  reads=[psres, "fcw"], writes=[cvr])
                            P.dve(lambda e, cvt=cvt, ps=ps, fw1=fw1: e.scalar_tensor_tensor(
                                out=cvt[:, 0:m], in0=ps[:, 1:1 + m], scalar=fw1, in1=cvt[:, 0:m], op0=ALU.mult, op1=ALU.add),
                                reads=[psres, "fcw", cvr], writes=[cvr])
                            P.dve(lambda e, cvt=cvt, ps=ps, fw2=fw2: e.scalar_tensor_tensor(
                                out=cvt[:, 0:m], in0=ps[:, 2:2 + m], scalar=fw2, in1=cvt[:, 0:m], op0=ALU.mult, op1=ALU.add),
                                reads=[psres, "fcw", cvr], writes=[cvr])
                            if hi == NT:
                                so2 = NP - glo
                                fi = jj + hv * JB
                                P.act(lambda e, ps=ps, fi=fi: e.activation(out=fst[:, fi, 0:6], in_=ps[:, so2 - 2:so2 + 4], func=AF.Copy),
                                      reads=[psres], writes=[("fst", fi)])
                            cv_pair.append((cvt, cvr))
                        (cg, cgr), (cvv, cvvr) = cv_pair
                        P.act(lambda e, cg=cg: e.activation(out=cg[:, 0:m], in_=cg[:, 0:m], func=AF.Silu), reads=[cgr], writes=[cgr])
                        P.dve(lambda e, cg=cg, cvv=cvv, lo=lo, jj=jj: e.tensor_tensor(
                            out=R2.ap(jj, lo, lo + m), in0=cg[:, 0:m], in1=cvv[:, 0:m], op=ALU.mult),
                            reads=[cgr, cvvr], writes=R2.res(jj, lo, lo + m))
                st_, str_ = scr_next()
                cs = st_[:, 0:88].rearrange("p (h t s) -> p h t s", h=2, s=4)
                t2 = st_[:, 128:216].rearrange("p (h t s) -> p h t s", h=2, s=4)
                fres = [("fst", i) for i in range(2 * JB)]
                for hv in range(2):
                    t0_ = b * JB + hv * (NFT // 2)
                    svv = stf[:, t0_:t0_ + JB, :].rearrange("p t (s r) -> p t r s", r=2)
                    wv = [fcw_s[:, t0_:t0_ + JB, l * 3 + kk:l * 3 + kk + 1].to_broadcast([128, JB, 4]) for kk in range(3)]
                    us = fst[:, hv * JB:(hv + 1) * JB, 2:6]
                    P.dve(lambda e, hv=hv, svv=svv, wv=wv: e.tensor_tensor(out=cs[:, hv], in0=svv[:, :, 0, :], in1=wv[0], op=ALU.mult),
                          reads=["stf", "fcw"], writes=[str_])
                    P.dve(lambda e, hv=hv, svv=svv, wv=wv: e.tensor_tensor(out=t2[:, hv], in0=svv[:, :, 1, :], in1=wv[1], op=ALU.mult),
                          reads=["stf", "fcw"], writes=[str_])
                    P.dve(lambda e, hv=hv: e.tensor_tensor(out=cs[:, hv], in0=cs[:, hv], in1=t2[:, hv], op=ALU.add), reads=[str_], writes=[str_])
                    P.dve(lambda e, hv=hv, us=us, wv=wv: e.tensor_tensor(out=t2[:, hv], in0=us, in1=wv[2], op=ALU.mult),
                          reads=fres + ["fcw"], writes=[str_])
                    P.dve(lambda e, hv=hv: e.tensor_tensor(out=cs[:, hv], in0=cs[:, hv], in1=t2[:, hv], op=ALU.add), reads=[str_], writes=[str_])
                P.act(lambda e: e.activation(out=cs[:, 0], in_=cs[:, 0], func=AF.Silu), reads=[str_], writes=[str_])
                P.dve(lambda e: e.tensor_tensor(out=R2a_t[:, 0:8, NP - 128:NT - 128], in0=cs[:, 0, 0:8, :], in1=cs[:, 1, 0:8, :], op=ALU.mult),
                      reads=[str_], writes=[r_ for c_ in range(8) for r_ in R2.res(c_, NP, NT)])
                P.dve(lambda e: e.tensor_tensor(out=R2b_t[:, 0:3, NP:NT], in0=cs[:, 0, 8:11, :], in1=cs[:, 1, 8:11, :], op=ALU.mult),
                      reads=[str_], writes=[r_ for c_ in range(8, 11) for r_ in R2.res(c_, NP, NT)])
                for hv in range(2):
                    colbase = hv * DFF + b * JB * 128
                    for t0 in range(0, JB, 4):
                        nt_ = min(4, JB - t0)
                        ps, psres = ps_next()
                        for i in range(nt_):
                            fi = hv * JB + t0 + i
                            P.pe(lambda e, ps=ps, fi=fi, i=i: e.transpose(out=ps[0:6, i * 128:(i + 1) * 128], in_=fst[:, fi, 0:6], identity=ident_f[:, :]),
                                 reads=[("fst", fi), "ident_f"], writes=[psres])
                        ot, otr = scr_next()
                        P.act(lambda e, ot=ot, ps=ps, nt_=nt_: e.activation(out=ot[0:6, 0:nt_ * 128], in_=ps[0:6, 0:nt_ * 128], func=AF.Copy),
                              reads=[psres], writes=[otr])
                        c0 = colbase + t0 * 128
                        P.dma("sp", lambda e, ot=ot, c0=c0, nt_=nt_: e.dma_start(out=ffnp[l, :, c0:c0 + nt_ * 128], in_=ot[0:2, 0:nt_ * 128]),
                              reads=[otr], writes=[("o_ffnp", l, c0)])
                        P.dma("sp", lambda e, ot=ot, c0=c0, nt_=nt_: e.dma_start(
                            out=ffns[l].rearrange("(s r) c -> s r c", r=2)[:, 1, c0:c0 + nt_ * 128], in_=ot[2:6, 0:nt_ * 128]),
                            reads=[otr], writes=[("o_ffns", l, c0)])
                for m in range(NCH):
                    sl, sres = load_slab([(w_down[l, b * JB * 128:(b + 1) * JB * 128, m * 128:(m + 1) * 128], 0)], JB)
                    for (lo, hi) in FG:
                        ps, psres = ps_next()
                        mm_group(ps, psres, sl, sres, JB, rhsR2, lo, hi)
                        if b == 0:
                            P.dve(lambda e, ps=ps, lo=lo, hi=hi, m=m: e.scalar_tensor_tensor(
                                out=xT.ap(m, lo, hi), in0=xT.ap(m, lo, hi), scalar=ALPHA, in1=ps[:, 0:hi - lo], op0=ALU.mult, op1=ALU.add),
                                reads=[psres] + xT.res(m, lo, hi), writes=xT.res(m, lo, hi))
                        else:
                            P.dve(lambda e, ps=ps, lo=lo, hi=hi, m=m: e.tensor_tensor(
                                out=xT.ap(m, lo, hi), in0=xT.ap(m, lo, hi), in1=ps[:, 0:hi - lo], op=ALU.add),
                                reads=[psres] + xT.res(m, lo, hi), writes=xT.res(m, lo, hi))
            stage(f'L{l}_ffn')
            layer_norm(2, 3)
            stage(f'L{l}_ln2')

        try:
            if stop == '__done__':
                raise _Stop()
            stage('setup')
            layer(0)
            layer(1)

            def out_block(col0, ntok, dst):
                for hf in range(4):
                    ps, psres = ps_next()
                    for i in range(4):
                        c = hf * 4 + i
                        P.pe(lambda e, ps=ps, c=c, i=i: e.transpose(out=ps[0:ntok, i * 128:(i + 1) * 128], in_=xT.ap(c, col0, col0 + ntok), identity=ident_f[:, :]),
                             reads=xT.res(c, col0, col0 + ntok) + ["ident_f"], writes=[psres])
                    ot, otr = scr_next()
                    if hf % 2 == 0:
                        P.act(lambda e, ot=ot, ps=ps: e.activation(out=ot[0:ntok, :], in_=ps[0:ntok, :], func=AF.Copy), reads=[psres], writes=[otr])
                    else:
                        P.dve(lambda e, ot=ot, ps=ps: e.tensor_copy(out=ot[0:ntok, :], in_=ps[0:ntok, :]), reads=[psres], writes=[otr])
                    P.dma("sp", lambda e, ot=ot, hf=hf: e.dma_start(out=dst[:, hf * 512:(hf + 1) * 512], in_=ot[0:ntok, :]),
                          reads=[otr], writes=[("o_y", col0, hf)])

            for b in range(8):
                out_block(H + b * 128, 128, yp[b * 128:(b + 1) * 128, :])
            out_block(NP, NS, ys)

        except _Stop:
            pass
        if dbg:
            allres = list(P.lastw.keys())
            P.dma("sp", lambda e: e.dma_start(out=dbg_xT.rearrange("p (c n) -> p c n", c=NCH), in_=xT_t[:, :, :]), reads=allres, writes=["dbg1"])
            P.dma("pool", lambda e: e.dma_start(out=dbg_X.rearrange("p (c n) -> p c n", c=NCH), in_=X_t[:, :, :]), reads=allres, writes=["dbg2"])
            P.dma("pool", lambda e: e.dma_start(out=dbg_R2a.rearrange("p (c n) -> p c n", c=8), in_=R2a_t[:, :, :]), reads=allres, writes=["dbg3"])
            P.dma("pool", lambda e: e.dma_start(out=dbg_R2b.rearrange("p (c n) -> p c n", c=3), in_=R2b_t[:, :, :]), reads=allres, writes=["dbg4"])
        P.emit()
        build.stats = P.stats
    return nc


def _t5_bucket_np(n):
    n = np.asarray(n)
    max_exact = 16
    nf = np.maximum(n, 1).astype(np.float32)
    large = max_exact + (np.log(nf / np.float32(max_exact)) / np.float32(math.log(128 / max_exact))
                         * np.float32(32 - max_exact)).astype(np.int32)
    large = np.minimum(large, 31)
    return np.where(n < max_exact, n, large)


_CACHE = {}


def make_in_maps(x_prompt, x_sample, cache_k, cache_v, state_conv, state_pool, state_ffn,
                 rel_table, w_in, conv_w, pool_w, pool_scale, sinks, w_o, ln1_g, ln1_b,
                 w_up, ffn_conv_w, w_down, ln2_g, ln2_b):
    f = lambda a: np.ascontiguousarray(np.asarray(a), dtype=np.float32)
    x_prompt, x_sample = f(x_prompt), f(x_sample)
    cache_k, cache_v = f(cache_k), f(cache_v)
    state_conv, state_pool, state_ffn = f(state_conv), f(state_pool), f(state_ffn)
    rel_table = f(rel_table)
    kk = np.arange(128)[:, None]
    qq = np.arange(128)[None, :]
    dist = np.concatenate([qq + 128 - kk, qq - kk], axis=1)
    valid = (dist >= 0) & (dist < WIN)
    bucket = _t5_bucket_np(np.arange(WIN))
    bidx = bucket[np.clip(dist, 0, WIN - 1)]
    tbl = np.ascontiguousarray(np.transpose(rel_table[bidx], (2, 0, 1)))
    gmask = np.where(valid, 0.0, NEG).astype(np.float32)
    ident = np.eye(128, dtype=np.float32)
    shared = {
        "w_in": f(w_in), "w_o": f(w_o), "w_up": f(w_up), "w_down": f(w_down),
        "conv_w": f(conv_w).reshape(6, 512), "ffn_cw": f(ffn_conv_w).reshape(6, 2 * DFF),
        "pool_w": f(pool_w), "pool_scale": f(pool_scale),
        "lnp": np.ascontiguousarray(np.stack([f(ln1_g)[0], f(ln1_b)[0], f(ln2_g)[0], f(ln2_b)[0],
                                               f(ln1_g)[1], f(ln1_b)[1], f(ln2_g)[1], f(ln2_b)[1]], 0)),
        "sinks": np.ascontiguousarray(np.broadcast_to(f(sinks).reshape(1, 32), (128, 32))), "tbl": tbl, "gmask": gmask, "ident": ident,
    }
    in_maps = []
    for c in range(8):
        b, half = c // 2, c % 2
        if half == 0:
            xh = np.concatenate([np.zeros((H, D), np.float32), x_prompt[b, 0:NM]], 0)
        else:
            xh = x_prompt[b, NM - H:SEQ]
        keymask = np.zeros((128, 11), np.float32)
        smask = np.ones((128, 16), np.float32)
        prat = np.ones((128, 4, 16), np.float32)
        if half == 0:
            cols = np.arange(11 * 128).reshape(11, 128).T
            keymask[cols < H] = NEG
            smask[:] = 0.0
            pos = np.arange(16)
            for g in range(4):
                w = 2 << g
                prat[:, g, :] = (np.float32(w) / np.minimum(pos + 1, w).astype(np.float32))[None, :]
        sl = slice(4 * c, 4 * c + 4)
        m = dict(shared)
        m.update({
            "xh": np.ascontiguousarray(xh), "xs": np.ascontiguousarray(x_sample[sl, 0, :]),
            "ck": np.ascontiguousarray(cache_k[:, sl].reshape(2, NS, 128, 256)),
            "cv": np.ascontiguousarray(cache_v[:, sl].reshape(2, NS, 128, 256)),
            "s_conv": np.ascontiguousarray(state_conv[:, sl].reshape(2, NS * 2, 512)),
            "s_pool": np.ascontiguousarray(state_pool[:, sl].reshape(2, NS * 15, 512)),
            "s_ffn": np.ascontiguousarray(state_ffn[:, sl].reshape(2, NS * 2, 2 * DFF)),
            "keymask": keymask, "smask": smask, "prat": np.ascontiguousarray(prat.reshape(128, 64)),
        })
        in_maps.append(m)

    return in_maps


def assemble(R):
    y_prompt = np.zeros((4, SEQ, D), np.float32)
    y_sample = np.zeros((32, 1, D), np.float32)
    k_prompt = np.zeros((2, 4, 128, 4, 64), np.float32)
    v_prompt = np.zeros((2, 4, 128, 4, 64), np.float32)
    conv_prompt = np.zeros((2, 4, 2, 512), np.float32)
    pool_prompt = np.zeros((2, 4, 15, 512), np.float32)
    ffn_prompt = np.zeros((2, 4, 2, 2 * DFF), np.float32)
    k_sample = np.zeros((2, 32, 128, 4, 64), np.float32)
    v_sample = np.zeros((2, 32, 128, 4, 64), np.float32)
    conv_sample = np.zeros((2, 32, 2, 512), np.float32)
    pool_sample = np.zeros((2, 32, 15, 512), np.float32)
    ffn_sample = np.zeros((2, 32, 2, 2 * DFF), np.float32)
    for c in range(8):
        b, half = c // 2, c % 2
        r = R[c]
        y_prompt[b, half * NM:(half + 1) * NM] = r["yp"]
        sl = slice(4 * c, 4 * c + 4)
        y_sample[sl, 0] = r["ys"]
        if half == 1:
            k_prompt[:, b] = r["kp"].reshape(2, 128, 4, 64)
            v_prompt[:, b] = r["vp"].reshape(2, 128, 4, 64)
            conv_prompt[:, b] = r["convp"]
            pool_prompt[:, b] = r["poolp"]
            ffn_prompt[:, b] = r["ffnp"]
        k_sample[:, sl] = r["ks"].reshape(2, NS, 128, 4, 64)
        v_sample[:, sl] = r["vs"].reshape(2, NS, 128, 4, 64)
        conv_sample[:, sl] = r["convs"].reshape(2, NS, 2, 512)
        pool_sample[:, sl] = r["pools"].reshape(2, NS, 15, 512)
        ffn_sample[:, sl] = r["ffns"].reshape(2, NS, 2, 2 * DFF)
    return (y_prompt, y_sample, k_prompt, v_prompt, conv_prompt, pool_prompt, ffn_prompt,
            k_sample, v_sample, conv_sample, pool_sample, ffn_sample)


def kernel(**inputs):
    if "nc" not in _CACHE:
        _CACHE["nc"] = build()
    nc = _CACHE["nc"]
    in_maps = make_in_maps(**inputs)
    res = run_bass_kernel_spmd(nc, in_maps, core_ids=list(range(8)))
    return assemble(res.results)
```

```python
import contextlib
import math

import numpy as np
import concourse.bass as bass
import concourse.mybir as mybir
from concourse.bass_utils import run_bass_kernel_spmd

F32 = mybir.dt.float32
BF16 = mybir.dt.bfloat16
ALU = mybir.AluOpType
AF = mybir.ActivationFunctionType
AX = mybir.AxisListType

D = 2048
NCH = 16
SEQ = 2048
H = 260
NM = 1024
NP = H + NM
NS = 4
NT = NP + NS
INW = 3584
DFF = 5632
NFT = 88
NHEAD = 16
WIN = 128
ALPHA = 4.0 ** 0.25
SCALE = 64 ** -0.5
EPS = 1e-5
NEG = -1e30
OQ, OK_, OV, OGB, OGC, OH_, OPIN = 0, 1024, 1280, 1536, 2048, 2560, 3072
JB = 11
NBATCH = 4


class _Ins:
    __slots__ = ("eng", "fn", "deps", "dma", "sig", "val", "semi", "pos", "fsz")

    def __init__(self, eng, fn, dma):
        self.eng = eng
        self.fn = fn
        self.deps = set()
        self.dma = dma
        self.sig = False
        self.val = 0
        self.semi = -1
        self.pos = 0
        self.fsz = 0


class _Rec:
    def __getattr__(self, name):
        return lambda *a, **k: (name, a, k)


_REC = _Rec()


class Prog:
    ENGS = ("pe", "act", "dve", "pool", "sp")
    NDMA = 8

    def __init__(self, nc):
        self.nc = nc
        self.ins = []
        self.lastw = {}
        self.readers = {}
        self.dma_n = {e: 0 for e in self.ENGS}
        self.dma_hist = {e: [] for e in self.ENGS}
        self.npos = {e: 0 for e in self.ENGS}

    def add(self, eng, fn, reads=(), writes=(), dma=False):
        idx = len(self.ins)
        ins = _Ins(eng, fn(_REC), dma)
        psr = [r for r in reads if isinstance(r, tuple) and r[0] == "ps"]
        if psr:
            writes = list(writes) + psr
        for r in reads:
            w = self.lastw.get(r)
            if w is not None:
                ins.deps.add(w)
        for r in writes:
            w = self.lastw.get(r)
            if w is not None:
                ins.deps.add(w)
            rl = self.readers.get(r)
            if rl:
                ins.deps.update(rl)
        for r in reads:
            self.readers.setdefault(r, []).append(idx)
        for r in writes:
            self.lastw[r] = idx
            self.readers[r] = []
        if dma:
            n = self.dma_n[eng]
            self.dma_n[eng] = n + 1
            ins.semi = n % self.NDMA
            ins.val = 16 * (n // self.NDMA + 1)
            hist = self.dma_hist[eng]
            if n >= self.NDMA:
                ins.deps.add(hist[n - self.NDMA])
            hist.append(idx)
        ins.deps.discard(idx)
        ins.pos = self.npos[eng]
        self.npos[eng] += 1
        try:
            o_ = ins.fn[2].get("out")
            ins.fsz = o_.free_size() if o_ is not None else ins.fn[1][0].free_size()
        except Exception:
            ins.fsz = 0
        self.ins.append(ins)
        return idx

    def pe(self, fn, reads=(), writes=()):
        return self.add("pe", fn, reads, writes)

    def act(self, fn, reads=(), writes=()):
        return self.add("act", fn, reads, writes)

    def dve(self, fn, reads=(), writes=()):
        return self.add("dve", fn, reads, writes)

    def dma(self, eng, fn, reads=(), writes=()):
        return self.add(eng, fn, reads, writes, dma=True)

    def emit(self, final_eng="sp"):
        nc = self.nc
        ins_all = list(self.ins)
        fin = _Ins(final_eng, None, False)
        last_by_eng = {}
        for i, ins in enumerate(ins_all):
            if ins.dma:
                fin.deps.add(i)
            last_by_eng[ins.eng] = i
        for e, i in last_by_eng.items():
            fin.deps.add(i)
        ins_all.append(fin)
        def self_hazard(ins, p):
            if ins.eng not in ("dve", "act", "pool") or p.dma or p.eng != ins.eng:
                return False
            return True

        for ins in ins_all:
            for d in ins.deps:
                p = ins_all[d]
                if (not p.dma) and (p.eng != ins.eng or self_hazard(ins, p)):
                    p.sig = True
        cnt = {e: 0 for e in self.ENGS}
        for ins in ins_all:
            if (not ins.dma) and ins.sig:
                cnt[ins.eng] += 1
                ins.val = cnt[ins.eng]
        per_eng = {e: [] for e in self.ENGS}
        for ins in ins_all:
            per_eng[ins.eng].append(ins)
        self.stats = {e: len(v) for e, v in per_eng.items()}

        with contextlib.ExitStack() as st:
            esem = {e: st.enter_context(nc.semaphore(f"es_{e}")) for e in self.ENGS}
            dsem = {
                e: [st.enter_context(nc.semaphore(f"ds_{e}{i}")) for i in range(self.NDMA)]
                for e in self.ENGS
                if self.dma_n[e] > 0
            }
            block = st.enter_context(nc.Block())

            def run(engname, engobj):
                waited = {}
                for ins in per_eng[engname]:
                    need = {}
                    for d in ins.deps:
                        p = ins_all[d]
                        if p.dma:
                            key = ("d", p.eng, p.semi)
                        elif p.eng != engname or self_hazard(ins, p):
                            key = ("e", p.eng)
                        else:
                            continue
                        if p.val > need.get(key, 0):
                            need[key] = p.val
                    for key, v in need.items():
                        if waited.get(key, 0) >= v:
                            continue
                        sem = dsem[key[1]][key[2]] if key[0] == "d" else esem[key[1]]
                        engobj.wait_ge(sem, v)
                        waited[key] = v
                    if ins.fn is None:
                        continue
                    name, a_, k_ = ins.fn
                    bi = getattr(engobj, name)(*a_, **k_)
                    if ins.dma:
                        bi.then_inc(dsem[engname][ins.semi], 16)
                    elif ins.sig:
                        bi.then_inc(esem[engname], 1)

            if per_eng["pe"]:
                block.tensor(lambda e: run("pe", e))
            if per_eng["act"]:
                block.scalar(lambda e: run("act", e))
            if per_eng["dve"]:
                block.vector(lambda e: run("dve", e))
            if per_eng["pool"]:
                block.gpsimd(lambda e: run("pool", e))
            if per_eng["sp"]:
                block.sync(lambda e: run("sp", e))


def groups(lo, hi, mx=512):
    n = hi - lo
    k = (n + mx - 1) // mx
    out = []
    base, rem = divmod(n, k)
    a = lo
    for i in range(k):
        b = a + base + (1 if i < rem else 0)
        out.append((a, b))
        a = b
    return out


class SB:
    def __init__(self, name, tile, col0=0):
        self.name, self.t, self.c0 = name, tile, col0

    def ap(self, c, lo, hi, p0=0, p1=128):
        return self.t[p0:p1, c, lo - self.c0:hi - self.c0]

    def res(self, c, lo, hi):
        return [(self.name, c, b) for b in range(lo // 128, (hi - 1) // 128 + 1)]


class _Stop(Exception):
    pass


def build(stop=None, dbg=False):
    def stage(name):
        nonlocal stop
        if stop == name:
            raise _Stop()

    nc = bass.Bass("TRN2", target_bir_lowering=False)

    def din(name, shape, dt=F32):
        return nc.dram_tensor(name, list(shape), dt, kind="ExternalInput").ap()

    def dout(name, shape):
        return nc.dram_tensor(name, list(shape), F32, kind="ExternalOutput").ap()

    xh = din("xh", [NP, D])
    xs = din("xs", [NS, D])
    ck = din("ck", [2, NS, 128, 256])
    cv = din("cv", [2, NS, 128, 256])
    s_conv = din("s_conv", [2, NS * 2, 512])
    s_pool = din("s_pool", [2, NS * 15, 512])
    s_ffn = din("s_ffn", [2, NS * 2, 2 * DFF])
    w_in = din("w_in", [2, D, INW])
    w_o = din("w_o", [2, D, D])
    w_up = din("w_up", [2, D, 2 * DFF])
    w_down = din("w_down", [2, DFF, D])
    conv_w = din("conv_w", [6, 512])
    ffn_cw = din("ffn_cw", [6, 2 * DFF])
    pool_w = din("pool_w", [2, 4, 128, 128])
    pool_scale = din("pool_scale", [2, 512])
    lnp = din("lnp", [8, D])
    sinks = din("sinks", [128, 32])
    tbl = din("tbl", [NHEAD, 128, 256])
    gmask_d = din("gmask", [128, 256])
    ident_d = din("ident", [128, 128])
    keymask_d = din("keymask", [128, 11])
    smask_d = din("smask", [128, 16])
    prat_d = din("prat", [128, 64])
    yp = dout("yp", [NM, D])
    ys = dout("ys", [NS, D])
    kp = dout("kp", [2, 128, 256])
    vp = dout("vp", [2, 128, 256])
    convp = dout("convp", [2, 2, 512])
    poolp = dout("poolp", [2, 15, 512])
    ffnp = dout("ffnp", [2, 2, 2 * DFF])
    ks = dout("ks", [2, NS, 128, 256])
    vs = dout("vs", [2, NS, 128, 256])
    convs = dout("convs", [2, NS * 2, 512])
    pools = dout("pools", [2, NS * 15, 512])
    ffns = dout("ffns", [2, NS * 2, 2 * DFF])

    if dbg:
        dbg_xT = dout("dbg_xT", [128, NCH * (NT - 128)])
        dbg_X = dout("dbg_X", [128, NCH * NT])
        dbg_R2a = dout("dbg_R2a", [128, 8 * (NT - 128)])
        dbg_R2b = dout("dbg_R2b", [128, 3 * NT])
    st = contextlib.ExitStack()
    with st:
        def sb(name, shape, dt=F32):
            return st.enter_context(nc.sbuf_tensor(name, list(shape), dt))

        xT_t = sb("xT", [128, NCH, NT - 128])
        X_t = sb("X", [128, NCH, NT], BF16)
        R2a_t = sb("R2a", [128, 8, NT - 128], BF16)
        R2b_t = sb("R2b", [128, 3, NT], BF16)
        Vt = sb("Vt", [128, 11, 4, 66], BF16)
        NSLAB = 5
        slabs = [sb(f"slab{i}", [128, 16, 128], BF16) for i in range(NSLAB)]
        NSCR = 8
        scr = [sb(f"scr{i}", [128, 512]) for i in range(NSCR)]
        Gt = sb("Gt", [128, 4, 256])
        gmask = sb("gmask_s", [128, 256])
        ident_f = sb("ident_f", [128, 128])
        ident_b = sb("ident_b", [128, 128], BF16)
        ones_b = sb("ones_b", [128, 128], BF16)
        ones_f = sb("ones_f", [128, 128])
        eps_t = sb("eps_t", [128, 2])
        rstd_t = [sb(f"rstd{i}", [128, 392]) for i in range(3)]
        lnp_s = sb("lnp_s", [128, NCH, 8])
        cw_s = sb("cw_s", [128, 4, 6])
        fcw_s = sb("fcw_s", [128, NFT, 6])
        psc_s = sb("psc_s", [128, 4, 2])
        pw_s = sb("pw_s", [128, 8, 128], BF16)
        es_s = sb("es_s", [128, 32])
        keymask = sb("keymask_s", [128, 11])
        smask = sb("smask_s", [128, 16])
        prat = sb("prat_s", [128, 64])
        stc = sb("stc", [128, 2, 4, 8])
        stp = sb("stp", [128, 2, 4, 60])
        stf = sb("stf", [128, NFT, 8])
        fst = sb("fst", [128, 2 * JB, 6])
        kcT = sb("kcT", [128, NS, 2, 128], BF16)
        veff = sb("veff", [128, NS, 4, 64], BF16)
        cst = sb("cst", [128, 4, 10])
        pst = sb("pst", [128, 4, 75])
        sbias = sb("sbias", [128, 4])
        small = sb("small", [128, 64])

        PSB = [st.enter_context(nc.psum_tensor(f"ps{i}", [128, 512], F32)) for i in range(8)]

        P = Prog(nc)
        xT = SB("xT", xT_t, 128)
        X = SB("X", X_t, 0)

        class R2cls:
            def ap(self, c, lo, hi, p0=0, p1=128):
                if c < 8:
                    return R2a_t[p0:p1, c, lo - 128:hi - 128]
                return R2b_t[p0:p1, c - 8, lo:hi]

            def res(self, c, lo, hi):
                return [("R2", c, b) for b in range(lo // 128, (hi - 1) // 128 + 1)]

        R2 = R2cls()

        state = {"ps": 0, "scr": 0, "slab": 0}

        def ps_next():
            i = state["ps"] % 8
            state["ps"] += 1
            return PSB[i], ("ps", i)

        def scr_next():
            i = state["scr"] % NSCR
            state["scr"] += 1
            return scr[i], ("scr", i)

        def slab_next():
            i = state["slab"] % NSLAB
            state["slab"] += 1
            return slabs[i], ("slab", i)

        def load_slab(pieces, nk):
            sl, sres = slab_next()
            for src, c0 in pieces:
                ncols = src.shape[1]
                P.dma("pool", lambda e, sl=sl, src=src, c0=c0, ncols=ncols: e.dma_start(
                    out=sl[:, 0:nk, c0:c0 + ncols], in_=src.rearrange("(k p) c -> p k c", p=128)),
                    writes=[sres])
            return sl, sres

        def mm_group(ps, psres, sl, sres, nk, rhs, lo, hi, m0=0, m1=128):
            for k in range(nk):
                rap, rres = rhs(k, lo, hi)
                P.pe(lambda e, k=k, rap=rap: e.matmul(ps[m0:m1, 0:hi - lo], lhsT=sl[:, k, m0:m1], rhs=rap,
                                                       start=(k == 0), stop=(k == nk - 1)),
                     reads=[sres] + rres, writes=[psres])

        def rhsX(k, lo, hi):
            return X.ap(k, lo, hi), X.res(k, lo, hi)

        def rhsR2(k, lo, hi):
            return R2.ap(k, lo, hi), R2.res(k, lo, hi)

        def load_T(src, R, C, dst_fn, dres, q="sp"):
            per = max(1, min(512 // R, 4))
            pc = 0
            while pc < C:
                w = min(512, C - pc)
                stg, sres = scr_next()
                P.dma(q, lambda e, stg=stg, pc=pc, w=w: e.dma_start(out=stg[0:R, 0:w], in_=src[:, pc:pc + w]),
                      writes=[sres])
                nchunk = w // 128
                c = 0
                while c < nchunk:
                    n = min(per, nchunk - c)
                    ps, psres = ps_next()
                    for i in range(n):
                        P.pe(lambda e, ps=ps, stg=stg, c=c, i=i: e.transpose(
                            out=ps[:, i * R:(i + 1) * R], in_=stg[0:R, (c + i) * 128:(c + i + 1) * 128],
                            identity=ident_f[0:R, 0:R]), reads=[sres, "ident_f"], writes=[psres])
                    dst = dst_fn(pc // 128 + c, n)
                    P.act(lambda e, ps=ps, dst=dst, n=n: e.activation(
                        out=dst, in_=ps[:, 0:n * R].rearrange("p (n r) -> p n r", r=R), func=AF.Copy),
                        reads=[psres], writes=[dres])
                    c += n
                pc += w

        try:
            def simple_load(tile_ap, src, res, q="sp"):
                P.dma(q, lambda e: e.dma_start(out=tile_ap, in_=src), writes=[res])

            simple_load(ident_f[:, :], ident_d, "ident_f")
            simple_load(gmask[:, :], gmask_d, "gmask")
            simple_load(keymask[:, :], keymask_d, "keymask")
            simple_load(smask[:, :], smask_d, "smask")
            simple_load(prat[:, :], prat_d, "prat")
            simple_load(es_s[:, :], sinks, "es")
            P.dma("pool", lambda e: e.dma_start(out=pw_s[:, :, :], in_=pool_w.rearrange("l g k m -> k (l g) m")),
                  writes=["pw"])
            P.act(lambda e: e.activation(out=es_s[:, :], in_=es_s[:, :], func=AF.Exp), reads=["es"], writes=["es"])
            P.dve(lambda e: e.tensor_copy(out=ident_b[:, :], in_=ident_f[:, :]), reads=["ident_f"], writes=["ident_b"])
            P.dve(lambda e: e.memset(ones_b[:, :], 1.0), writes=["ones_b"])
            P.dve(lambda e: e.memset(ones_f[:, :], 1.0), writes=["ones_f"])
            P.dve(lambda e: e.memset(eps_t[:, :], EPS), writes=["eps"])
            P.dve(lambda e: e.memset(Vt[:, :, :, 64:66], 1.0), writes=["Vt_ones"])
            P.dve(lambda e: e.memset(Vt[:, :, :, 0:64], 0.0), writes=[("Vt", b) for b in range(11)])

            stage('s0')
            load_T(lnp, 8, D, lambda c, n: lnp_s[:, c:c + n, :], "lnp")
            load_T(conv_w, 6, 512, lambda c, n: cw_s[:, c:c + n, :], "cw")
            load_T(ffn_cw, 6, 2 * DFF, lambda c, n: fcw_s[:, c:c + n, :], "fcw")
            load_T(pool_scale, 2, 512, lambda c, n: psc_s[:, c:c + n, :], "psc")
            for l in range(2):
                load_T(s_conv[l], 8, 512, lambda c, n, l=l: stc[:, l, c:c + n, :], "stc")
                load_T(s_pool[l], 60, 512, lambda c, n, l=l: stp[:, l, c:c + n, :], "stp")

            stage('s1')
            def load_x_block(src, ntok, col0):
                for hf in range(4):
                    stg, sres = scr_next()
                    P.dma("sp" if hf % 2 == 0 else "pool", lambda e, stg=stg, hf=hf: e.dma_start(out=stg[0:ntok, :], in_=src[:, hf * 512:(hf + 1) * 512]),
                          writes=[sres])
                    ps, psres = ps_next()
                    for i in range(4):
                        P.pe(lambda e, ps=ps, stg=stg, i=i: e.transpose(
                            out=ps[:, i * 128:i * 128 + ntok], in_=stg[0:ntok, i * 128:(i + 1) * 128],
                            identity=ident_f[0:ntok, 0:ntok]), reads=[sres, "ident_f"], writes=[psres])
                    c0 = hf * 4
                    src_ps = ps[:, :].rearrange("p (n r) -> p n r", r=128)[:, :, 0:ntok]
                    wrX = []
                    for c in range(c0, c0 + 4):
                        wrX += X.res(c, col0, col0 + ntok)
                    if col0 >= 128:
                        wrT = []
                        for c in range(c0, c0 + 4):
                            wrT += xT.res(c, col0, col0 + ntok)
                        P.act(lambda e, src_ps=src_ps, c0=c0: e.activation(
                            out=xT_t[:, c0:c0 + 4, col0 - 128:col0 - 128 + ntok], in_=src_ps, func=AF.Copy), reads=[psres], writes=wrT)
                        P.dve(lambda e, c0=c0: e.tensor_copy(
                            out=X_t[:, c0:c0 + 4, col0:col0 + ntok], in_=xT_t[:, c0:c0 + 4, col0 - 128:col0 - 128 + ntok]), reads=wrT, writes=wrX)
                    else:
                        P.act(lambda e, src_ps=src_ps, c0=c0: e.activation(
                            out=X_t[:, c0:c0 + 4, col0:col0 + ntok], in_=src_ps, func=AF.Copy), reads=[psres], writes=wrX)

            for b in range(10):
                load_x_block(xh[b * 128:(b + 1) * 128, :], 128, b * 128)
            load_x_block(xh[1280:1284, :], 4, 1280)
            load_x_block(xs[:, :], NS, NP)

        except _Stop:
            stop = '__done__'
        def layer(l):
            RIN = (0 if l == 0 else 128, NT)
            RMX = (128 if l == 0 else 256, NT)
            RPRE = (RMX[0] - 16, NT)

            def lnvec(c, which):
                return lnp_s[:, c, l * 4 + which:l * 4 + which + 1]

            load_T(s_ffn[l], 8, 2 * DFF, lambda c, n: stf[:, c:c + n, :], "stf")
            for s in range(NS):
                stg, sres = scr_next()
                P.dma("sp", lambda e, stg=stg, s=s: e.dma_start(out=stg[:, 0:256], in_=ck[l, s]), writes=[sres])
                ps, psres = ps_next()
                for c in range(2):
                    P.pe(lambda e, ps=ps, stg=stg, c=c: e.transpose(
                        out=ps[:, c * 128:(c + 1) * 128], in_=stg[:, c * 128:(c + 1) * 128], identity=ident_f[:, :]),
                        reads=[sres, "ident_f"], writes=[psres])
                P.act(lambda e, ps=ps, s=s: e.activation(
                    out=kcT[:, s, :, :], in_=ps[:, 0:256].rearrange("p (c k) -> p c k", c=2), func=AF.Copy),
                    reads=[psres], writes=[("kcT", s)])
                P.dma("pool", lambda e, s=s: e.dma_start(
                    out=veff[:, s, :, :], in_=cv[l, s].rearrange("k (h d) -> k h d", h=4)), writes=[("veff", s)])
                P.dma("sp", lambda e, s=s: e.dma_start(out=ks[l, s, 0:127, :], in_=ck[l, s, 1:128, :]),
                      writes=[("o_ks", l, s, 0)])
                P.dma("sp", lambda e, s=s: e.dma_start(out=vs[l, s, 0:127, :], in_=cv[l, s, 1:128, :]),
                      writes=[("o_vs", l, s, 0)])

            def proj_tile(pieces, rng, rhs=rhsX, nk=16):
                sl, sres = load_slab(pieces, nk)
                outs = []
                for (lo, hi) in groups(*rng):
                    ps, psres = ps_next()
                    mm_group(ps, psres, sl, sres, nk, rhs, lo, hi)
                    outs.append((ps, psres, lo, hi))
                return outs

            def wcols(w, c0, n=128, r0=0, r1=D):
                return w[l, r0:r1, c0:c0 + n]

            for j in range(4):
                slg = load_slab([(wcols(w_in, OGB + 128 * j), 0)], 16)
                slc = load_slab([(wcols(w_in, OGC + 128 * j), 0)], 16)
                slh = load_slab([(wcols(w_in, OH_ + 128 * j), 0)], 16)
                w0 = cw_s[:, j, l * 3 + 0:l * 3 + 1]
                w1 = cw_s[:, j, l * 3 + 1:l * 3 + 2]
                w2 = cw_s[:, j, l * 3 + 2:l * 3 + 3]
                prev = None
                for (lo, hi) in groups(*RPRE):
                    psg, rg = ps_next()
                    mm_group(psg, rg, slg[0], slg[1], 16, rhsX, lo, hi)
                    psc, rc = ps_next()
                    mm_group(psc, rc, slc[0], slc[1], 16, rhsX, lo, hi)
                    psh, rh = ps_next()
                    mm_group(psh, rh, slh[0], slh[1], 16, rhsX, lo, hi)
                    n = hi - lo
                    gbt, gbr = scr_next()
                    gct, gcr = scr_next()
                    ut, ur = scr_next()
                    cvt, cvr = scr_next()
                    P.act(lambda e, gbt=gbt, psg=psg, n=n: e.activation(out=gbt[:, 0:n], in_=psg[:, 0:n], func=AF.Copy),
                          reads=[rg], writes=[gbr])
                    P.act(lambda e, gct=gct, psc=psc, n=n: e.activation(out=gct[:, 0:n], in_=psc[:, 0:n], func=AF.Copy),
                          reads=[rc], writes=[gcr])
                    P.dve(lambda e, ut=ut, gct=gct, psh=psh, n=n: e.tensor_tensor(
                        out=ut[:, 2:2 + n], in0=gct[:, 0:n], in1=psh[:, 0:n], op=ALU.mult), reads=[gcr, rh], writes=[ur])
                    if prev is None:
                        P.dve(lambda e, ut=ut: e.memset(ut[:, 0:2], 0.0), writes=[ur])
                    else:
                        put, pur, pn = prev
                        P.dve(lambda e, ut=ut, put=put, pn=pn: e.tensor_copy(out=ut[:, 0:2], in_=put[:, pn:pn + 2]),
                              reads=[pur], writes=[ur])
                    if lo <= H - 4 and H <= hi:
                        P.dve(lambda e, ut=ut, lo=lo: e.tensor_tensor(
                            out=ut[:, 2 + H - 4 - lo:2 + H - lo], in0=ut[:, 2 + H - 4 - lo:2 + H - lo],
                            in1=smask[:, 0:4], op=ALU.mult), reads=["smask", ur], writes=[ur])
                    prev = (ut, ur, n)
                    phi = min(hi, NP)
                    a = max(lo, RMX[0])
                    if phi > a:
                        m = phi - a
                        o = a - lo
                        P.act(lambda e, cvt=cvt, ut=ut, o=o, m=m: e.activation(
                            out=cvt[:, 0:m], in_=ut[:, o:o + m], func=AF.Copy, scale=w0), reads=[ur, "cw"], writes=[cvr])
                        P.dve(lambda e, cvt=cvt, ut=ut, o=o, m=m: e.scalar_tensor_tensor(
                            out=cvt[:, 0:m], in0=ut[:, o + 1:o + 1 + m], scalar=w1, in1=cvt[:, 0:m],
                            op0=ALU.mult, op1=ALU.add), reads=[ur, "cw", cvr], writes=[cvr])
                        P.dve(lambda e, cvt=cvt, ut=ut, o=o, m=m: e.scalar_tensor_tensor(
                            out=cvt[:, 0:m], in0=ut[:, o + 2:o + 2 + m], scalar=w2, in1=cvt[:, 0:m],
                            op0=ALU.mult, op1=ALU.add), reads=[ur, "cw", cvr], writes=[cvr])
                        P.dve(lambda e, cvt=cvt, gbt=gbt, o=o, m=m, a=a, j=j: e.tensor_tensor(
                            out=R2.ap(j, a, a + m), in0=gbt[:, o:o + m], in1=cvt[:, 0:m], op=ALU.mult),
                            reads=[gbr, cvr], writes=R2.res(j, a, a + m))
                    if hi == NT:
                        so = NP - lo
                        st0 = stc[:, l, j, :].rearrange("p (s r) -> p r s", r=2)[:, 0, :]
                        st1 = stc[:, l, j, :].rearrange("p (s r) -> p r s", r=2)[:, 1, :]
                        cs = small[:, 0:4]
                        P.dve(lambda e, cs=cs, st0=st0: e.tensor_scalar(out=cs, in0=st0, scalar1=w0, scalar2=None, op0=ALU.mult),
                              reads=["stc", "cw"], writes=["small"])
                        P.dve(lambda e, cs=cs, st1=st1: e.scalar_tensor_tensor(
                            out=cs, in0=st1, scalar=w1, in1=cs, op0=ALU.mult, op1=ALU.add), reads=["stc", "cw", "small"], writes=["small"])
                        P.dve(lambda e, cs=cs, ut=ut, so=so: e.scalar_tensor_tensor(
                            out=cs, in0=ut[:, 2 + so:2 + so + 4], scalar=w2, in1=cs, op0=ALU.mult, op1=ALU.add),
                            reads=[ur, "cw", "small"], writes=["small"])
                        P.dve(lambda e, cs=cs, gbt=gbt, so=so, j=j: e.tensor_tensor(
                            out=R2.ap(j, NP, NT), in0=gbt[:, so:so + 4], in1=cs, op=ALU.mult),
                            reads=[gbr, "small"], writes=R2.res(j, NP, NT))
                        P.dve(lambda e, ut=ut, so=so, j=j: e.tensor_copy(out=cst[:, j, 0:2], in_=ut[:, so:so + 2]),
                              reads=[ur], writes=[("cst", j)])
                        cv_ = cst[:, j, 2:10].rearrange("p (s r) -> p r s", r=2)
                        P.dve(lambda e, cv_=cv_, st1=st1: e.tensor_copy(out=cv_[:, 0, :], in_=st1), reads=["stc"], writes=[("cst", j)])
                        P.dve(lambda e, cv_=cv_, ut=ut, so=so: e.tensor_copy(out=cv_[:, 1, :], in_=ut[:, 2 + so:2 + so + 4]),
                              reads=[ur], writes=[("cst", j)])
            ps, psres = ps_next()
            for j in range(4):
                P.pe(lambda e, ps=ps, j=j: e.transpose(out=ps[0:10, j * 128:(j + 1) * 128], in_=cst[:, j, :], identity=ident_f[:, :]),
                     reads=[("cst", j), "ident_f"], writes=[psres])
            ot, otr = scr_next()
            P.act(lambda e, ot=ot, ps=ps: e.activation(out=ot[0:10, :], in_=ps[0:10, :], func=AF.Copy), reads=[psres], writes=[otr])
            P.dma("sp", lambda e, ot=ot: e.dma_start(out=convp[l], in_=ot[0:2, :]), reads=[otr], writes=[("o_convp", l)])
            P.dma("sp", lambda e, ot=ot: e.dma_start(out=convs[l], in_=ot[2:10, :]), reads=[otr], writes=[("o_convs", l)])

            stage(f'L{l}_conv')
            tps = {}
            prevp_d = {}

            def pool_body(g, gi):
                wsz = 2 << g
                tp = tps[g]
                psp, rp, lo, hi = tp[gi]
                n = hi - lo
                a0, a0r = scr_next()
                P.act(lambda e, a0=a0, psp=psp, n=n: e.activation(out=a0[:, 16:16 + n], in_=psp[:, 0:n], func=AF.Copy),
                      reads=[rp], writes=[a0r])
                prevp = prevp_d.get(g)
                if prevp is None:
                    P.dve(lambda e, a0=a0: e.memset(a0[:, 0:16], 0.0), writes=[a0r])
                else:
                    pa, par, pn = prevp
                    P.dve(lambda e, a0=a0, pa=pa, pn=pn: e.tensor_copy(out=a0[:, 0:16], in_=pa[:, pn:pn + 16]),
                          reads=[par], writes=[a0r])
                if lo <= H - 16 and H <= hi:
                    o = 16 + H - 16 - lo
                    P.dve(lambda e, a0=a0, o=o: e.tensor_tensor(out=a0[:, o:o + 16], in0=a0[:, o:o + 16], in1=smask[:, :], op=ALU.mult),
                          reads=["smask", a0r], writes=[a0r])
                prevp_d[g] = (a0, a0r, n)
                yield
                phi = min(hi, NP)
                a = max(lo, RMX[0])
                if phi > a:
                    m = phi - a
                    o = 16 + a - lo
                    ta, tar = scr_next()
                    tb, tbr = scr_next()
                    cur, curr, ext = a0, a0r, 15
                    bufs = [(ta, tar), (tb, tbr)]
                    sh = 1
                    lvl = 0
                    while sh < wsz:
                        nxt, nxtr = bufs[lvl % 2]
                        ext2 = ext - sh
                        P.dve(lambda e, nxt=nxt, cur=cur, o=o, m=m, ext2=ext2, sh=sh: e.tensor_tensor(
                            out=nxt[:, o - ext2:o + m], in0=cur[:, o - ext2:o + m], in1=cur[:, o - ext2 - sh:o + m - sh], op=ALU.add),
                            reads=[curr], writes=[nxtr])
                        cur, curr, ext = nxt, nxtr, ext2
                        yield
                        sh *= 2
                        lvl += 1
                    if a <= H and H + 16 <= phi:
                        oo = o + H - a
                        P.dve(lambda e, cur=cur, oo=oo, g=g: e.tensor_tensor(
                            out=cur[:, oo:oo + 16], in0=cur[:, oo:oo + 16], in1=prat[:, g * 16:(g + 1) * 16], op=ALU.mult),
                            reads=["prat", curr], writes=[curr])
                    dt_, dtr = bufs[lvl % 2]
                    dtb = dt_[:, :].bitcast(BF16)
                    P.dve(lambda e, dtb=dtb, cur=cur, a0=a0, o=o, m=m, wsz=wsz: e.scalar_tensor_tensor(
                        out=dtb[:, 0:m], in0=cur[:, o:o + m], scalar=1.0 / wsz, in1=a0[:, o:o + m], op0=ALU.mult, op1=ALU.subtract),
                        reads=[curr, a0r], writes=[dtr])
                    yield
                    ps2, ps2r = ps_next()
                    P.pe(lambda e, ps2=ps2, dtb=dtb, m=m, g=g: e.matmul(ps2[:, 0:m], lhsT=pw_s[:, l * 4 + g, :], rhs=dtb[:, 0:m], start=True, stop=True),
                         reads=[dtr, "pw"], writes=[ps2r])
                    P.act(lambda e, ps2=ps2, m=m, a=a, g=g: e.activation(
                        out=R2.ap(4 + g, a, a + m), in_=ps2[:, 0:m], func=AF.Copy, scale=psc_s[:, g, l:l + 1]),
                        reads=[ps2r, "psc"], writes=R2.res(4 + g, a, a + m))
                if hi == NT:
                    so = 16 + NP - lo
                    P.dve(lambda e, a0=a0, so=so, g=g: e.tensor_copy(out=pst[:, g, 0:15], in_=a0[:, so - 15:so]),
                          reads=[a0r], writes=[("pst", g)])
                    pv = pst[:, g, 15:75].rearrange("p (s r) -> p s r", r=15)
                    sv = stp[:, l, g, :].rearrange("p (s r) -> p s r", r=15)
                    P.dve(lambda e, pv=pv, sv=sv: e.tensor_copy(out=pv[:, :, 0:14], in_=sv[:, :, 1:15]), reads=["stp"], writes=[("pst", g)])
                    P.dve(lambda e, pv=pv, a0=a0, so=so: e.tensor_copy(out=pv[:, :, 14], in_=a0[:, so:so + 4]),
                          reads=[a0r], writes=[("pst", g)])
                    ws_ = small[:, 8:12]
                    if wsz == 16:
                        P.dve(lambda e, ws_=ws_, sv=sv: e.tensor_reduce(out=ws_, in_=sv[:, :, 0:15], axis=AX.X, op=ALU.add),
                              reads=["stp"], writes=["small"])
                    else:
                        P.dve(lambda e, ws_=ws_, sv=sv, wsz=wsz: e.tensor_reduce(out=ws_, in_=sv[:, :, 16 - wsz:15], axis=AX.X, op=ALU.add),
                              reads=["stp"], writes=["small"])
                    P.dve(lambda e, ws_=ws_, a0=a0, so=so: e.tensor_tensor(out=ws_, in0=ws_, in1=a0[:, so:so + 4], op=ALU.add),
                          reads=[a0r, "small"], writes=["small"])
                    dsb = small[:, 16:20].bitcast(BF16)
                    P.dve(lambda e, ws_=ws_, a0=a0, so=so, dsb=dsb, wsz=wsz: e.scalar_tensor_tensor(
                        out=dsb[:, 0:4], in0=ws_, scalar=1.0 / wsz, in1=a0[:, so:so + 4], op0=ALU.mult, op1=ALU.subtract),
                        reads=[a0r, "small"], writes=["small"])
                    ps2, ps2r = ps_next()
                    P.pe(lambda e, ps2=ps2, dsb=dsb, g=g: e.matmul(ps2[:, 0:4], lhsT=pw_s[:, l * 4 + g, :], rhs=dsb[:, 0:4], start=True, stop=True),
                         reads=["small", "pw"], writes=[ps2r])
                    P.act(lambda e, ps2=ps2, g=g: e.activation(
                        out=R2.ap(4 + g, NP, NT), in_=ps2[:, 0:4], func=AF.Copy, scale=psc_s[:, g, l:l + 1]),
                        reads=[ps2r, "psc"], writes=R2.res(4 + g, NP, NT))

            for gp in (0, 2):
                tps[gp] = proj_tile([(wcols(w_in, OPIN + 128 * gp), 0)], RPRE)
                tps[gp + 1] = proj_tile([(wcols(w_in, OPIN + 128 * (gp + 1)), 0)], RPRE)
                for gi in range(len(tps[gp])):
                    gens = [pool_body(gp, gi), pool_body(gp + 1, gi)]
                    while gens:
                        for gen in list(gens):
                            try:
                                next(gen)
                            except StopIteration:
                                gens.remove(gen)
            ps, psres = ps_next()
            for g in range(4):
                P.pe(lambda e, ps=ps, g=g: e.transpose(out=ps[0:75, g * 128:(g + 1) * 128], in_=pst[:, g, :], identity=ident_f[:, :]),
                     reads=[("pst", g), "ident_f"], writes=[psres])
            ot, otr = scr_next()
            P.act(lambda e, ot=ot, ps=ps: e.activation(out=ot[0:75, :], in_=ps[0:75, :], func=AF.Copy), reads=[psres], writes=[otr])
            P.dma("sp", lambda e, ot=ot: e.dma_start(out=poolp[l], in_=ot[0:15, :]), reads=[otr], writes=[("o_poolp", l)])
            P.dma("sp", lambda e, ot=ot: e.dma_start(out=pools[l], in_=ot[15:75, :]), reads=[otr], writes=[("o_pools", l)])

            stage(f'L{l}_pool')
            for m in range(NCH):
                sl, sres = load_slab([(w_o[l, 1024:2048, m * 128:(m + 1) * 128], 0)], 8)
                for (lo, hi) in groups(*RMX):
                    ps, psres = ps_next()
                    mm_group(ps, psres, sl, sres, 8, rhsR2, lo, hi)
                    P.dve(lambda e, ps=ps, lo=lo, hi=hi, m=m: e.scalar_tensor_tensor(
                        out=xT.ap(m, lo, hi), in0=xT.ap(m, lo, hi), scalar=ALPHA, in1=ps[:, 0:hi - lo], op0=ALU.mult, op1=ALU.add),
                        reads=[psres] + xT.res(m, lo, hi), writes=xT.res(m, lo, hi))

            stage(f'L{l}_wo1')
            qhead = {}
            for t in range(8):
                gA = 2 * (t // 4)
                hA = 4 * gA + t % 4
                hB = 4 * (gA + 1) + t % 4
                qhead[hA] = (t, 0)
                qhead[hB] = (t, 64)
                outs = proj_tile([(wcols(w_in, OQ + 64 * hA, 64), 0), (wcols(w_in, OQ + 64 * hB, 64), 64)], RMX)
                for (ps, psres, lo, hi) in outs:
                    P.act(lambda e, ps=ps, lo=lo, hi=hi, t=t: e.activation(out=R2.ap(t, lo, hi), in_=ps[:, 0:hi - lo], func=AF.Copy),
                          reads=[psres], writes=R2.res(t, lo, hi))
            for c in range(2):
                sl, sres = load_slab([(wcols(w_in, OK_ + 128 * c), 0)], 16)
                for (lo, hi) in groups(*RIN):
                    ps, psres = ps_next()
                    mm_group(ps, psres, sl, sres, 16, rhsX, lo, hi)
                    P.act(lambda e, ps=ps, lo=lo, hi=hi, c=c: e.activation(out=R2.ap(8 + c, lo, hi), in_=ps[:, 0:hi - lo], func=AF.Copy),
                          reads=[psres], writes=R2.res(8 + c, lo, hi))
                ps, psres = ps_next()
                for k in range(16):
                    P.pe(lambda e, ps=ps, k=k, sl=sl: e.matmul(ps[:, 0:128], lhsT=X.ap(k, NP - 128, NP), rhs=sl[:, k, :], start=(k == 0), stop=(k == 15)),
                         reads=[sres] + X.res(k, NP - 128, NP), writes=[psres])
                ot, otr = scr_next()
                P.act(lambda e, ps=ps, ot=ot: e.activation(out=ot[:, 0:128], in_=ps[:, 0:128], func=AF.Copy), reads=[psres], writes=[otr])
                P.dma("sp", lambda e, ot=ot, c=c: e.dma_start(out=kp[l, :, c * 128:(c + 1) * 128], in_=ot[:, 0:128]), reads=[otr], writes=[("o_kp", l, c)])
                ps, psres = ps_next()
                for k in range(16):
                    P.pe(lambda e, ps=ps, k=k, sl=sl: e.matmul(ps[0:NS, 0:128], lhsT=X.ap(k, NP, NT), rhs=sl[:, k, :], start=(k == 0), stop=(k == 15)),
                         reads=[sres] + X.res(k, NP, NT), writes=[psres])
                ot, otr = scr_next()
                P.act(lambda e, ps=ps, ot=ot: e.activation(out=ot[0:NS, 0:128], in_=ps[0:NS, 0:128], func=AF.Copy), reads=[psres], writes=[otr])
                P.dma("sp", lambda e, ot=ot, c=c: e.dma_start(out=ks[l, :, 127, c * 128:(c + 1) * 128], in_=ot[0:NS, 0:128]), reads=[otr],
                      writes=[("o_ks", l, 9, 1 + c)])
            for s in range(NS):
                P.dve(lambda e, s=s: e.tensor_copy(out=kcT[:, s, :, 0], in_=R2b_t[:, 0:2, NP + s]),
                      reads=R2.res(8, NP, NT) + R2.res(9, NP, NT), writes=[("kcT", s)])
            blocks = [(b * 128, min((b + 1) * 128, NT)) for b in range(RIN[0] // 128, 11)]
            for c in range(2):
                sl, sres = load_slab([(wcols(w_in, OV + 128 * c), 0)], 16)
                for bi in range(0, len(blocks), 4):
                    ps, psres = ps_next()
                    blk = blocks[bi:bi + 4]
                    for i, (lo, hi) in enumerate(blk):
                        for k in range(16):
                            P.pe(lambda e, ps=ps, k=k, sl=sl, lo=lo, hi=hi, i=i: e.matmul(
                                ps[0:hi - lo, i * 128:(i + 1) * 128], lhsT=X.ap(k, lo, hi), rhs=sl[:, k, :], start=(k == 0), stop=(k == 15)),
                                reads=[sres] + X.res(k, lo, hi), writes=[psres])
                    b0 = blk[0][0] // 128
                    nb = len(blk)
                    full = [x for x in blk if x[1] - x[0] == 128]
                    nf = len(full)
                    if nf:
                        P.act(lambda e, ps=ps, b0=b0, nf=nf, c=c: e.activation(
                            out=Vt[:, b0:b0 + nf, 2 * c:2 * c + 2, 0:64],
                            in_=ps[:, 0:nf * 128].rearrange("p (b h d) -> p b h d", h=2, d=64), func=AF.Copy),
                            reads=[psres], writes=[("Vt", b) for b in range(b0, b0 + nf)])
                    if nf < nb:
                        lo, hi = blk[-1]
                        n = hi - lo
                        P.act(lambda e, ps=ps, b0=b0, nf=nf, c=c, n=n: e.activation(
                            out=Vt[0:n, b0 + nf, 2 * c:2 * c + 2, 0:64],
                            in_=ps[0:n, nf * 128:(nf + 1) * 128].rearrange("p (h d) -> p h d", h=2), func=AF.Copy),
                            reads=[psres], writes=[("Vt", b0 + nf)])
                ps, psres = ps_next()
                for k in range(16):
                    P.pe(lambda e, ps=ps, k=k, sl=sl: e.matmul(ps[:, 0:128], lhsT=X.ap(k, NP - 128, NP), rhs=sl[:, k, :], start=(k == 0), stop=(k == 15)),
                         reads=[sres] + X.res(k, NP - 128, NP), writes=[psres])
                ot, otr = scr_next()
                P.act(lambda e, ps=ps, ot=ot: e.activation(out=ot[:, 0:128], in_=ps[:, 0:128], func=AF.Copy), reads=[psres], writes=[otr])
                P.dma("sp", lambda e, ot=ot, c=c: e.dma_start(out=vp[l, :, c * 128:(c + 1) * 128], in_=ot[:, 0:128]), reads=[otr], writes=[("o_vp", l, c)])
                ps, psres = ps_next()
                for k in range(16):
                    P.pe(lambda e, ps=ps, k=k, sl=sl: e.matmul(ps[0:NS, 0:128], lhsT=X.ap(k, NP, NT), rhs=sl[:, k, :], start=(k == 0), stop=(k == 15)),
                         reads=[sres] + X.res(k, NP, NT), writes=[psres])
                ot, otr = scr_next()
                P.act(lambda e, ps=ps, ot=ot: e.activation(out=ot[0:NS, 0:128], in_=ps[0:NS, 0:128], func=AF.Copy), reads=[psres], writes=[otr])
                P.dma("sp", lambda e, ot=ot, c=c: e.dma_start(out=vs[l, :, 127, c * 128:(c + 1) * 128], in_=ot[0:NS, 0:128]), reads=[otr],
                      writes=[("o_vs", l, 9, 1 + c)])
                for s in range(NS):
                    ps, psres = ps_next()
                    for k in range(16):
                        P.pe(lambda e, ps=ps, k=k, sl=sl, s=s: e.matmul(ps[0:1, 0:128], lhsT=X.ap(k, NP + s, NP + s + 1), rhs=sl[:, k, :],
                                                                          start=(k == 0), stop=(k == 15)),
                             reads=[sres] + X.res(k, NP, NT), writes=[psres])
                    P.act(lambda e, ps=ps, s=s, c=c: e.activation(
                        out=veff[0:1, s, 2 * c:2 * c + 2, :], in_=ps[0:1, 0:128].rearrange("p (h d) -> p h d", h=2), func=AF.Copy),
                        reads=[psres], writes=[("veff", s)])

            stage(f'L{l}_qkv')
            qblocks = [(q0, min(q0 + 128, NP)) for q0 in range(RMX[0], NP, 128)]
            DEPTH = 3
            for kvh in range(4):
                P.dma("sp", lambda e, kvh=kvh: e.dma_start(out=Gt[:, :, :], in_=tbl[4 * kvh:4 * kvh + 4].rearrange("h k q -> k h q")),
                      writes=["Gt"])
                for j in range(4):
                    P.dve(lambda e, j=j: e.tensor_tensor(out=Gt[:, j, :], in0=Gt[:, j, :], in1=gmask[:, :], op=ALU.add),
                          reads=["Gt", "gmask"], writes=["Gt"])
                kc = 8 + kvh // 2
                kb = 64 * (kvh % 2)
                pairs = [(bi, j) for bi in range(len(qblocks)) for j in range(4)]
                pinfo = {}
                binfo = {}
                pendC2 = []

                def stageA(pi):
                    bi, j = pairs[pi]
                    q0, q1 = qblocks[bi]
                    nq = q1 - q0
                    blk = q0 // 128
                    h = 4 * kvh + j
                    qt, qb = qhead[h]
                    assert qb == kb
                    pss, pssr = ps_next()
                    P.pe(lambda e: e.matmul(pss[:, 0:nq], lhsT=R2.ap(kc, q0 - 128, q0, kb, kb + 64), rhs=R2.ap(qt, q0, q1, kb, kb + 64),
                                            start=True, stop=True),
                         reads=R2.res(kc, q0 - 128, q0) + R2.res(qt, q0, q1), writes=[pssr])
                    P.pe(lambda e: e.matmul(pss[0:nq, 128:128 + nq], lhsT=R2.ap(kc, q0, q1, kb, kb + 64), rhs=R2.ap(qt, q0, q1, kb, kb + 64),
                                            start=True, stop=True),
                         reads=R2.res(kc, q0, q1) + R2.res(qt, q0, q1), writes=[pssr])
                    sbt, sbr = scr_next()
                    ptb = sbt[:, :].bitcast(BF16)[:, 512:768]
                    if nq == 128:
                        P.dve(lambda e: e.scalar_tensor_tensor(out=sbt[:, 0:256], in0=pss[:, 0:256], scalar=SCALE, in1=Gt[:, j, :],
                                                               op0=ALU.mult, op1=ALU.add), reads=[pssr, "Gt"], writes=[sbr])
                    else:
                        P.dve(lambda e: e.scalar_tensor_tensor(out=sbt[:, 0:nq], in0=pss[:, 0:nq], scalar=SCALE, in1=Gt[:, j, 0:nq],
                                                               op0=ALU.mult, op1=ALU.add), reads=[pssr, "Gt"], writes=[sbr])
                        P.dve(lambda e: e.scalar_tensor_tensor(out=sbt[0:nq, 128:128 + nq], in0=pss[0:nq, 128:128 + nq], scalar=SCALE,
                                                               in1=Gt[0:nq, j, 128:128 + nq], op0=ALU.mult, op1=ALU.add),
                              reads=[pssr, "Gt"], writes=[sbr])
                    if q0 >= 512 and nq == 128:
                        P.act(lambda e: e.activation(out=ptb[:, 0:256], in_=sbt[:, 0:256], func=AF.Exp), reads=[sbr], writes=[sbr])
                    else:
                        P.act(lambda e: e.activation(out=ptb[:, 0:nq], in_=sbt[:, 0:nq], func=AF.Exp, bias=keymask[:, blk - 1:blk]),
                              reads=[sbr, "keymask"], writes=[sbr])
                        P.act(lambda e: e.activation(out=ptb[0:nq, 128:128 + nq], in_=sbt[0:nq, 128:128 + nq], func=AF.Exp,
                                                     bias=keymask[0:nq, blk:blk + 1]), reads=[sbr, "keymask"], writes=[sbr])
                    pinfo[pi] = (ptb, sbr)

                def stageB(pi):
                    bi, j = pairs[pi]
                    q0, q1 = qblocks[bi]
                    nq = q1 - q0
                    blk = q0 // 128
                    if j == 0:
                        binfo[bi] = ps_next()
                    pso, psor = binfo[bi]
                    ptb, sbr = pinfo.pop(pi)
                    P.pe(lambda e: e.matmul(pso[0:nq, j * 66:j * 66 + 65], lhsT=ptb[:, 0:nq], rhs=Vt[:, blk - 1, kvh, 0:65], start=True, stop=False),
                         reads=[sbr, ("Vt", blk - 1), "Vt_ones"], writes=[psor])
                    P.pe(lambda e: e.matmul(pso[0:nq, j * 66:j * 66 + 65], lhsT=ptb[0:nq, 128:128 + nq], rhs=Vt[0:nq, blk, kvh, 0:65], start=False, stop=True),
                         reads=[sbr, ("Vt", blk), "Vt_ones"], writes=[psor])
                    if j == 3:
                        dn, dnr = scr_next()
                        atb = dn[:, :].bitcast(BF16)[:, 512:768]
                        P.dve(lambda e: e.tensor_tensor(
                            out=dn[0:nq, 0:4], in0=pso[0:nq, 0:264].rearrange("p (j c) -> p j c", c=66)[:, :, 64],
                            in1=es_s[0:nq, l * 16 + 4 * kvh:l * 16 + 4 * kvh + 4], op=ALU.add), reads=[psor, "es"], writes=[dnr])
                        P.dve(lambda e: e.reciprocal(out=dn[0:nq, 4:8], in_=dn[0:nq, 0:4]), reads=[dnr], writes=[dnr])
                        P.dve(lambda e: e.tensor_tensor(
                            out=atb[0:nq, 0:256].rearrange("p (j d) -> p j d", d=64),
                            in0=pso[0:nq, 0:264].rearrange("p (j c) -> p j c", c=66)[:, :, 0:64],
                            in1=dn[0:nq, 4:8].unsqueeze(2).to_broadcast([nq, 4, 64]), op=ALU.mult), reads=[psor, dnr], writes=[dnr])
                        pendC2.append((pi + 3, atb, dnr, q0, nq))

                def stageC2(atb, dnr, q0, nq):
                    pst_, pstr = ps_next()
                    pstb = pst_[:, :].bitcast(BF16)
                    for i in range(2):
                        P.pe(lambda e, i=i: e.transpose(out=pstb[:, i * 128:i * 128 + nq], in_=atb[0:nq, i * 128:(i + 1) * 128],
                                                        identity=ident_b[0:nq, 0:nq]), reads=[dnr, "ident_b"], writes=[pstr])
                    P.dve(lambda e: e.tensor_copy(
                        out=X_t[:, 2 * kvh:2 * kvh + 2, q0:q0 + nq], in_=pstb[:, 0:256].rearrange("p (i q) -> p i q", i=2)[:, :, 0:nq]),
                        reads=[pstr], writes=X.res(2 * kvh, q0, q0 + nq) + X.res(2 * kvh + 1, q0, q0 + nq))

                npair = len(pairs)
                for step in range(npair + DEPTH):
                    if step < npair:
                        stageA(step)
                    pb = step - DEPTH
                    if pb >= 0:
                        stageB(pb)
                        while pendC2 and pendC2[0][0] <= pb:
                            _, atb, dnr, q0_, nq_ = pendC2.pop(0)
                            stageC2(atb, dnr, q0_, nq_)
                while pendC2:
                    _, atb, dnr, q0_, nq_ = pendC2.pop(0)
                    stageC2(atb, dnr, q0_, nq_)

                sinfo = []
                for j in range(4):
                    h = 4 * kvh + j
                    qt, qb = qhead[h]
                    P.dve(lambda e, j=j: e.tensor_copy(out=sbias[:, j:j + 1], in_=Gt[:, j, 0:1]), reads=["Gt"], writes=[("sbias", j)])
                    P.dve(lambda e, j=j: e.tensor_copy(out=sbias[0:1, j:j + 1], in_=Gt[0:1, j, 128:129]), reads=["Gt"], writes=[("sbias", j)])
                for j in range(4):
                    h = 4 * kvh + j
                    qt, qb = qhead[h]
                    pss, pssr = ps_next()
                    for s in range(NS):
                        P.pe(lambda e, s=s: e.matmul(pss[:, s:s + 1], lhsT=kcT[kb:kb + 64, s, kvh // 2, :],
                                                     rhs=R2.ap(qt, NP + s, NP + s + 1, kb, kb + 64), start=True, stop=True),
                             reads=[("kcT", s)] + R2.res(qt, NP, NT), writes=[pssr])
                    sbt, sbr = scr_next()
                    P.dve(lambda e, j=j, sbt=sbt, pss=pss: e.tensor_scalar(out=sbt[:, 0:4], in0=pss[:, 0:4], scalar1=SCALE, scalar2=sbias[:, j:j + 1],
                                                                         op0=ALU.mult, op1=ALU.add), reads=[pssr, ("sbias", j)], writes=[sbr])
                    ptb = sbt[:, :].bitcast(BF16)
                    P.act(lambda e, sbt=sbt, ptb=ptb: e.activation(out=ptb[:, 512:516], in_=sbt[:, 0:4], func=AF.Exp), reads=[sbr], writes=[sbr])
                    sinfo.append((ptb, sbr))
                for j in range(4):
                    h = 4 * kvh + j
                    ob = 64 * (h % 2)
                    ptb, sbr = sinfo[j]
                    pso, psor = ps_next()
                    for s in range(NS):
                        P.pe(lambda e, s=s, pso=pso, ptb=ptb: e.matmul(pso[ob:ob + 64, s:s + 1], lhsT=veff[:, s, kvh, :], rhs=ptb[:, 512 + s:513 + s],
                                                                       start=True, stop=True), reads=[sbr, ("veff", s)], writes=[psor])
                    P.pe(lambda e, pso=pso, ptb=ptb: e.matmul(pso[ob:ob + 64, 4:8], lhsT=ones_b[:, 0:64], rhs=ptb[:, 512:516], start=True, stop=True),
                         reads=[sbr, "ones_b"], writes=[psor])
                    dn, dnr = scr_next()
                    P.dve(lambda e, dn=dn, pso=pso: e.tensor_scalar(out=dn[ob:ob + 64, 0:4], in0=pso[ob:ob + 64, 4:8],
                                                                    scalar1=es_s[ob:ob + 64, l * 16 + h:l * 16 + h + 1], scalar2=None, op0=ALU.add),
                          reads=[psor, "es"], writes=[dnr])
                    P.dve(lambda e, dn=dn: e.reciprocal(out=dn[ob:ob + 64, 4:8], in_=dn[ob:ob + 64, 0:4]), reads=[dnr], writes=[dnr])
                    P.dve(lambda e, dn=dn, pso=pso: e.tensor_tensor(out=X.ap(h // 2, NP, NT, ob, ob + 64), in0=pso[ob:ob + 64, 0:4],
                                                                    in1=dn[ob:ob + 64, 4:8], op=ALU.mult),
                          reads=[psor, dnr], writes=X.res(h // 2, NP, NT))

            stage(f'L{l}_attn')
            for m in range(NCH):
                sl, sres = load_slab([(w_o[l, 0:1024, m * 128:(m + 1) * 128], 0)], 8)
                for (lo, hi) in groups(*RMX):
                    ps, psres = ps_next()
                    mm_group(ps, psres, sl, sres, 8, rhsX, lo, hi)
                    P.dve(lambda e, ps=ps, lo=lo, hi=hi, m=m: e.tensor_tensor(
                        out=xT.ap(m, lo, hi), in0=xT.ap(m, lo, hi), in1=ps[:, 0:hi - lo], op=ALU.add),
                        reads=[psres] + xT.res(m, lo, hi), writes=xT.res(m, lo, hi))

            stage(f'L{l}_wo2')
            def layer_norm(gi_, bi_):
                G = groups(*RMX)
                info = {}

                def stats(gi):
                    lo, hi = G[gi]
                    n = hi - lo
                    ps1, ps1r = ps_next()
                    ps2, ps2r = ps_next()
                    for c in range(NCH):
                        P.pe(lambda e, c=c: e.matmul(ps1[:, 0:n], lhsT=ones_f[:, :], rhs=xT.ap(c, lo, hi), start=(c == 0), stop=(c == NCH - 1)),
                             reads=xT.res(c, lo, hi) + ["ones_f"], writes=[ps1r])
                        sq, sqr = scr_next()
                        sqb = sq[:, :].bitcast(BF16)
                        P.act(lambda e, c=c, sqb=sqb: e.activation(out=sqb[:, 0:n], in_=xT.ap(c, lo, hi), func=AF.Square),
                              reads=xT.res(c, lo, hi), writes=[sqr])
                        P.pe(lambda e, c=c, sqb=sqb: e.matmul(ps2[:, 0:n], lhsT=ones_b[:, :], rhs=sqb[:, 0:n], start=(c == 0), stop=(c == NCH - 1)),
                             reads=[sqr, "ones_b"], writes=[ps2r])
                    rstd = rstd_t[gi]
                    rr = ("rstd", gi)
                    P.dve(lambda e: e.tensor_scalar(out=rstd[:, 0:n], in0=ps1[:, 0:n], scalar1=1.0 / D, scalar2=None, op0=ALU.mult),
                          reads=[ps1r], writes=[rr])
                    P.dve(lambda e: e.tensor_tensor(out=rstd[:, 0:n], in0=rstd[:, 0:n], in1=rstd[:, 0:n], op=ALU.mult), reads=[rr], writes=[rr])
                    P.dve(lambda e: e.scalar_tensor_tensor(out=rstd[:, 0:n], in0=ps2[:, 0:n], scalar=1.0 / D, in1=rstd[:, 0:n],
                                                           op0=ALU.mult, op1=ALU.subtract), reads=[ps2r, rr], writes=[rr])
                    P.act(lambda e: e.activation(out=rstd[:, 0:n], in_=rstd[:, 0:n], func=AF.Ln, bias=eps_t[:, 0:1]), reads=[rr, "eps"], writes=[rr])
                    P.act(lambda e: e.activation(out=rstd[:, 0:n], in_=rstd[:, 0:n], func=AF.Exp, scale=-0.5), reads=[rr], writes=[rr])
                    info[gi] = (ps1, ps1r, rstd, rr)

                def norm(gi):
                    lo, hi = G[gi]
                    n = hi - lo
                    ps1, ps1r, rstd, rr = info[gi]
                    pend = []

                    def xcopy(c):
                        if c % 2 == 0:
                            P.act(lambda e: e.activation(out=X.ap(c, lo, hi), in_=xT.ap(c, lo, hi), func=AF.Copy),
                                  reads=xT.res(c, lo, hi), writes=X.res(c, lo, hi))
                        else:
                            P.dve(lambda e: e.tensor_copy(out=X.ap(c, lo, hi), in_=xT.ap(c, lo, hi)),
                                  reads=xT.res(c, lo, hi), writes=X.res(c, lo, hi))

                    def sub_(c):
                        P.dve(lambda e: e.scalar_tensor_tensor(out=xT.ap(c, lo, hi), in0=ps1[:, 0:n], scalar=-1.0 / D, in1=xT.ap(c, lo, hi),
                                                               op0=ALU.mult, op1=ALU.add),
                              reads=[ps1r] + xT.res(c, lo, hi), writes=xT.res(c, lo, hi))

                    def mul_aff(c):
                        P.dve(lambda e: e.tensor_tensor(out=xT.ap(c, lo, hi), in0=xT.ap(c, lo, hi), in1=rstd[:, 0:n], op=ALU.mult),
                              reads=[rr] + xT.res(c, lo, hi), writes=xT.res(c, lo, hi))
                        P.act(lambda e: e.activation(out=xT.ap(c, lo, hi), in_=xT.ap(c, lo, hi), func=AF.Identity,
                                                     scale=lnvec(c, gi_), bias=lnvec(c, bi_)),
                              reads=["lnp"] + xT.res(c, lo, hi), writes=xT.res(c, lo, hi))

                    sub_(0)
                    for c in range(NCH):
                        if c + 1 < NCH:
                            sub_(c + 1)
                        mul_aff(c)
                        pend.append(c)
                        if len(pend) > 2:
                            xcopy(pend.pop(0))
                    while pend:
                        xcopy(pend.pop(0))

                stats(0)
                for gi in range(len(G)):
                    if gi + 1 < len(G):
                        stats(gi + 1)
                    norm(gi)

            layer_norm(0, 1)

            stage(f'L{l}_ln1')
            FG = groups(*RMX)
            P.dma("sp", lambda e: e.dma_start(out=ffns[l].rearrange("(s r) c -> s r c", r=2)[:, 0, :],
                                              in_=s_ffn[l].rearrange("(s r) c -> s r c", r=2)[:, 1, :]), writes=[("o_ffns0", l)])
            for b in range(NBATCH):
                for jj in range(JB):
                    pair = b * JB + jj
                    hv_info = []
                    for hv in range(2):
                        tix = pair + hv * (NFT // 2)
                        sl, sres = load_slab([(wcols(w_up, tix * 128), 0)], 16)
                        hv_info.append((tix, sl, sres))
                    for (lo, hi) in FG:
                        glo = lo - 2
                        n2 = hi - glo
                        phi = min(hi, NP)
                        m = phi - lo
                        cv_pair = []
                        for hv in range(2):
                            tix, sl, sres = hv_info[hv]
                            fw0 = fcw_s[:, tix, l * 3 + 0:l * 3 + 1]
                            fw1 = fcw_s[:, tix, l * 3 + 1:l * 3 + 2]
                            fw2 = fcw_s[:, tix, l * 3 + 2:l * 3 + 3]
                            ps, psres = ps_next()
                            mm_group(ps, psres, sl, sres, 16, rhsX, glo, hi)
                            if glo <= H - 4 and H <= hi:
                                o = H - 4 - glo
                                P.dve(lambda e, ps=ps, o=o: e.tensor_tensor(out=ps[:, o:o + 4], in0=ps[:, o:o + 4], in1=smask[:, 0:4], op=ALU.mult),
                                      reads=["smask", psres], writes=[psres])
                            cvt, cvr = scr_next()
                            P.act(lambda e, cvt=cvt, ps=ps, fw0=fw0: e.activation(out=cvt[:, 0:m], in_=ps[:, 0:m], func=AF.Copy, scale=fw0),
                                  reads=[psres, "fcw"], writes=[cvr])
                            P.dve(lambda e, cvt=cvt, ps=ps, fw1=fw1: e.scalar_tensor_tensor(
                                out=cvt[:, 0:m], in0=ps[:, 1:1 + m], scalar=fw1, in1=cvt[:, 0:m], op0=ALU.mult, op1=ALU.add),
                                reads=[psres, "fcw", cvr], writes=[cvr])
                            P.dve(lambda e, cvt=cvt, ps=ps, fw2=fw2: e.scalar_tensor_tensor(
                                out=cvt[:, 0:m], in0=ps[:, 2:2 + m], scalar=fw2, in1=cvt[:, 0:m], op0=ALU.mult, op1=ALU.add),
                                reads=[psres, "fcw", cvr], writes=[cvr])
                            if hi == NT:
                                so2 = NP - glo
                                fi = jj + hv * JB
                                P.act(lambda e, ps=ps, fi=fi: e.activation(out=fst[:, fi, 0:6], in_=ps[:, so2 - 2:so2 + 4], func=AF.Copy),
                                      reads=[psres], writes=[("fst", fi)])
                            cv_pair.append((cvt, cvr))
                        (cg, cgr), (cvv, cvvr) = cv_pair
                        P.act(lambda e, cg=cg: e.activation(out=cg[:, 0:m], in_=cg[:, 0:m], func=AF.Silu), reads=[cgr], writes=[cgr])
                        P.dve(lambda e, cg=cg, cvv=cvv, lo=lo, jj=jj: e.tensor_tensor(
                            out=R2.ap(jj, lo, lo + m), in0=cg[:, 0:m], in1=cvv[:, 0:m], op=ALU.mult),
                            reads=[cgr, cvvr], writes=R2.res(jj, lo, lo + m))
                st_, str_ = scr_next()
                cs = st_[:, 0:88].rearrange("p (h t s) -> p h t s", h=2, s=4)
                t2 = st_[:, 128:216].rearrange("p (h t s) -> p h t s", h=2, s=4)
                fres = [("fst", i) for i in range(2 * JB)]
                for hv in range(2):
                    t0_ = b * JB + hv * (NFT // 2)
                    svv = stf[:, t0_:t0_ + JB, :].rearrange("p t (s r) -> p t r s", r=2)
                    wv = [fcw_s[:, t0_:t0_ + JB, l * 3 + kk:l * 3 + kk + 1].to_broadcast([128, JB, 4]) for kk in range(3)]
                    us = fst[:, hv * JB:(hv + 1) * JB, 2:6]
                    P.dve(lambda e, hv=hv, svv=svv, wv=wv: e.tensor_tensor(out=cs[:, hv], in0=svv[:, :, 0, :], in1=wv[0], op=ALU.mult),
                          reads=["stf", "fcw"], writes=[str_])
                    P.dve(lambda e, hv=hv, svv=svv, wv=wv: e.tensor_tensor(out=t2[:, hv], in0=svv[:, :, 1, :], in1=wv[1], op=ALU.mult),
                          reads=["stf", "fcw"], writes=[str_])
                    P.dve(lambda e, hv=hv: e.tensor_tensor(out=cs[:, hv], in0=cs[:, hv], in1=t2[:, hv], op=ALU.add), reads=[str_], writes=[str_])
                    P.dve(lambda e, hv=hv, us=us, wv=wv: e.tensor_tensor(out=t2[:, hv], in0=us, in1=wv[2], op=ALU.mult),
                          reads=fres + ["fcw"], writes=[str_])
                    P.dve(lambda e, hv=hv: e.tensor_tensor(out=cs[:, hv], in0=cs[:, hv], in1=t2[:, hv], op=ALU.add), reads=[str_], writes=[str_])
                P.act(lambda e: e.activation(out=cs[:, 0], in_=cs[:, 0], func=AF.Silu), reads=[str_], writes=[str_])
                P.dve(lambda e: e.tensor_tensor(out=R2a_t[:, 0:8, NP - 128:NT - 128], in0=cs[:, 0, 0:8, :], in1=cs[:, 1, 0:8, :], op=ALU.mult),
                      reads=[str_], writes=[r_ for c_ in range(8) for r_ in R2.res(c_, NP, NT)])
                P.dve(lambda e: e.tensor_tensor(out=R2b_t[:, 0:3, NP:NT], in0=cs[:, 0, 8:11, :], in1=cs[:, 1, 8:11, :], op=ALU.mult),
                      reads=[str_], writes=[r_ for c_ in range(8, 11) for r_ in R2.res(c_, NP, NT)])
                for hv in range(2):
                    colbase = hv * DFF + b * JB * 128
                    for t0 in range(0, JB, 4):
                        nt_ = min(4, JB - t0)
                        ps, psres = ps_next()
                        for i in range(nt_):
                            fi = hv * JB + t0 + i
                            P.pe(lambda e, ps=ps, fi=fi, i=i: e.transpose(out=ps[0:6, i * 128:(i + 1) * 128], in_=fst[:, fi, 0:6], identity=ident_f[:, :]),
                                 reads=[("fst", fi), "ident_f"], writes=[psres])
                        ot, otr = scr_next()
                        P.act(lambda e, ot=ot, ps=ps, nt_=nt_: e.activation(out=ot[0:6, 0:nt_ * 128], in_=ps[0:6, 0:nt_ * 128], func=AF.Copy),
                              reads=[psres], writes=[otr])
                        c0 = colbase + t0 * 128
                        P.dma("sp", lambda e, ot=ot, c0=c0, nt_=nt_: e.dma_start(out=ffnp[l, :, c0:c0 + nt_ * 128], in_=ot[0:2, 0:nt_ * 128]),
                              reads=[otr], writes=[("o_ffnp", l, c0)])
                        P.dma("sp", lambda e, ot=ot, c0=c0, nt_=nt_: e.dma_start(
                            out=ffns[l].rearrange("(s r) c -> s r c", r=2)[:, 1, c0:c0 + nt_ * 128], in_=ot[2:6, 0:nt_ * 128]),
                            reads=[otr], writes=[("o_ffns", l, c0)])
                for m in range(NCH):
                    sl, sres = load_slab([(w_down[l, b * JB * 128:(b + 1) * JB * 128, m * 128:(m + 1) * 128], 0)], JB)
                    for (lo, hi) in FG:
                        ps, psres = ps_next()
                        mm_group(ps, psres, sl, sres, JB, rhsR2, lo, hi)
                        if b == 0:
                            P.dve(lambda e, ps=ps, lo=lo, hi=hi, m=m: e.scalar_tensor_tensor(
                                out=xT.ap(m, lo, hi), in0=xT.ap(m, lo, hi), scalar=ALPHA, in1=ps[:, 0:hi - lo], op0=ALU.mult, op1=ALU.add),
                                reads=[psres] + xT.res(m, lo, hi), writes=xT.res(m, lo, hi))
                        else:
                            P.dve(lambda e, ps=ps, lo=lo, hi=hi, m=m: e.tensor_tensor(
                                out=xT.ap(m, lo, hi), in0=xT.ap(m, lo, hi), in1=ps[:, 0:hi - lo], op=ALU.add),
                                reads=[psres] + xT.res(m, lo, hi), writes=xT.res(m, lo, hi))
            stage(f'L{l}_ffn')
            layer_norm(2, 3)
            stage(f'L{l}_ln2')

        try:
            if stop == '__done__':
                raise _Stop()
            stage('setup')
            layer(0)
            layer(1)

            def out_block(col0, ntok, dst):
                for hf in range(4):
                    ps, psres = ps_next()
                    for i in range(4):
                        c = hf * 4 + i
                        P.pe(lambda e, ps=ps, c=c, i=i: e.transpose(out=ps[0:ntok, i * 128:(i + 1) * 128], in_=xT.ap(c, col0, col0 + ntok), identity=ident_f[:, :]),
                             reads=xT.res(c, col0, col0 + ntok) + ["ident_f"], writes=[psres])
                    ot, otr = scr_next()
                    if hf % 2 == 0:
                        P.act(lambda e, ot=ot, ps=ps: e.activation(out=ot[0:ntok, :], in_=ps[0:ntok, :], func=AF.Copy), reads=[psres], writes=[otr])
                    else:
                        P.dve(lambda e, ot=ot, ps=ps: e.tensor_copy(out=ot[0:ntok, :], in_=ps[0:ntok, :]), reads=[psres], writes=[otr])
                    P.dma("sp", lambda e, ot=ot, hf=hf: e.dma_start(out=dst[:, hf * 512:(hf + 1) * 512], in_=ot[0:ntok, :]),
                          reads=[otr], writes=[("o_y", col0, hf)])

            for b in range(8):
                out_block(H + b * 128, 128, yp[b * 128:(b + 1) * 128, :])
            out_block(NP, NS, ys)

        except _Stop:
            pass
        if dbg:
            allres = list(P.lastw.keys())
            P.dma("sp", lambda e: e.dma_start(out=dbg_xT.rearrange("p (c n) -> p c n", c=NCH), in_=xT_t[:, :, :]), reads=allres, writes=["dbg1"])
            P.dma("pool", lambda e: e.dma_start(out=dbg_X.rearrange("p (c n) -> p c n", c=NCH), in_=X_t[:, :, :]), reads=allres, writes=["dbg2"])
            P.dma("pool", lambda e: e.dma_start(out=dbg_R2a.rearrange("p (c n) -> p c n", c=8), in_=R2a_t[:, :, :]), reads=allres, writes=["dbg3"])
            P.dma("pool", lambda e: e.dma_start(out=dbg_R2b.rearrange("p (c n) -> p c n", c=3), in_=R2b_t[:, :, :]), reads=allres, writes=["dbg4"])
        P.emit()
        build.stats = P.stats
    return nc


def _t5_bucket_np(n):
    n = np.asarray(n)
    max_exact = 16
    nf = np.maximum(n, 1).astype(np.float32)
    large = max_exact + (np.log(nf / np.float32(max_exact)) / np.float32(math.log(128 / max_exact))
                         * np.float32(32 - max_exact)).astype(np.int32)
    large = np.minimum(large, 31)
    return np.where(n < max_exact, n, large)


_CACHE = {}


def make_in_maps(x_prompt, x_sample, cache_k, cache_v, state_conv, state_pool, state_ffn,
                 rel_table, w_in, conv_w, pool_w, pool_scale, sinks, w_o, ln1_g, ln1_b,
                 w_up, ffn_conv_w, w_down, ln2_g, ln2_b):
    f = lambda a: np.ascontiguousarray(np.asarray(a), dtype=np.float32)
    x_prompt, x_sample = f(x_prompt), f(x_sample)
    cache_k, cache_v = f(cache_k), f(cache_v)
    state_conv, state_pool, state_ffn = f(state_conv), f(state_pool), f(state_ffn)
    rel_table = f(rel_table)
    kk = np.arange(128)[:, None]
    qq = np.arange(128)[None, :]
    dist = np.concatenate([qq + 128 - kk, qq - kk], axis=1)
    valid = (dist >= 0) & (dist < WIN)
    bucket = _t5_bucket_np(np.arange(WIN))
    bidx = bucket[np.clip(dist, 0, WIN - 1)]
    tbl = np.ascontiguousarray(np.transpose(rel_table[bidx], (2, 0, 1)))
    gmask = np.where(valid, 0.0, NEG).astype(np.float32)
    ident = np.eye(128, dtype=np.float32)
    shared = {
        "w_in": f(w_in), "w_o": f(w_o), "w_up": f(w_up), "w_down": f(w_down),
        "conv_w": f(conv_w).reshape(6, 512), "ffn_cw": f(ffn_conv_w).reshape(6, 2 * DFF),
        "pool_w": f(pool_w), "pool_scale": f(pool_scale),
        "lnp": np.ascontiguousarray(np.stack([f(ln1_g)[0], f(ln1_b)[0], f(ln2_g)[0], f(ln2_b)[0],
                                               f(ln1_g)[1], f(ln1_b)[1], f(ln2_g)[1], f(ln2_b)[1]], 0)),
        "sinks": np.ascontiguousarray(np.broadcast_to(f(sinks).reshape(1, 32), (128, 32))), "tbl": tbl, "gmask": gmask, "ident": ident,
    }
    in_maps = []
    for c in range(8):
        b, half = c // 2, c % 2
        if half == 0:
            xh = np.concatenate([np.zeros((H, D), np.float32), x_prompt[b, 0:NM]], 0)
        else:
            xh = x_prompt[b, NM - H:SEQ]
        keymask = np.zeros((128, 11), np.float32)
        smask = np.ones((128, 16), np.float32)
        prat = np.ones((128, 4, 16), np.float32)
        if half == 0:
            cols = np.arange(11 * 128).reshape(11, 128).T
            keymask[cols < H] = NEG
            smask[:] = 0.0
            pos = np.arange(16)
            for g in range(4):
                w = 2 << g
                prat[:, g, :] = (np.float32(w) / np.minimum(pos + 1, w).astype(np.float32))[None, :]
        sl = slice(4 * c, 4 * c + 4)
        m = dict(shared)
        m.update({
            "xh": np.ascontiguousarray(xh), "xs": np.ascontiguousarray(x_sample[sl, 0, :]),
            "ck": np.ascontiguousarray(cache_k[:, sl].reshape(2, NS, 128, 256)),
            "cv": np.ascontiguousarray(cache_v[:, sl].reshape(2, NS, 128, 256)),
            "s_conv": np.ascontiguousarray(state_conv[:, sl].reshape(2, NS * 2, 512)),
            "s_pool": np.ascontiguousarray(state_pool[:, sl].reshape(2, NS * 15, 512)),
            "s_ffn": np.ascontiguousarray(state_ffn[:, sl].reshape(2, NS * 2, 2 * DFF)),
            "keymask": keymask, "smask": smask, "prat": np.ascontiguousarray(prat.reshape(128, 64)),
        })
        in_maps.append(m)

    return in_maps


def assemble(R):
    y_prompt = np.zeros((4, SEQ, D), np.float32)
    y_sample = np.zeros((32, 1, D), np.float32)
    k_prompt = np.zeros((2, 4, 128, 4, 64), np.float32)
    v_prompt = np.zeros((2, 4, 128, 4, 64), np.float32)
    conv_prompt = np.zeros((2, 4, 2, 512), np.float32)
    pool_prompt = np.zeros((2, 4, 15, 512), np.float32)
    ffn_prompt = np.zeros((2, 4, 2, 2 * DFF), np.float32)
    k_sample = np.zeros((2, 32, 128, 4, 64), np.float32)
    v_sample = np.zeros((2, 32, 128, 4, 64), np.float32)
    conv_sample = np.zeros((2, 32, 2, 512), np.float32)
    pool_sample = np.zeros((2, 32, 15, 512), np.float32)
    ffn_sample = np.zeros((2, 32, 2, 2 * DFF), np.float32)
    for c in range(8):
        b, half = c // 2, c % 2
        r = R[c]
        y_prompt[b, half * NM:(half + 1) * NM] = r["yp"]
        sl = slice(4 * c, 4 * c + 4)
        y_sample[sl, 0] = r["ys"]
        if half == 1:
            k_prompt[:, b] = r["kp"].reshape(2, 128, 4, 64)
            v_prompt[:, b] = r["vp"].reshape(2, 128, 4, 64)
            conv_prompt[:, b] = r["convp"]
            pool_prompt[:, b] = r["poolp"]
            ffn_prompt[:, b] = r["ffnp"]
        k_sample[:, sl] = r["ks"].reshape(2, NS, 128, 4, 64)
        v_sample[:, sl] = r["vs"].reshape(2, NS, 128, 4, 64)
        conv_sample[:, sl] = r["convs"].reshape(2, NS, 2, 512)
        pool_sample[:, sl] = r["pools"].reshape(2, NS, 15, 512)
        ffn_sample[:, sl] = r["ffns"].reshape(2, NS, 2, 2 * DFF)
    return (y_prompt, y_sample, k_prompt, v_prompt, conv_prompt, pool_prompt, ffn_prompt,
            k_sample, v_sample, conv_sample, pool_sample, ffn_sample)


def kernel(**inputs):
    if "nc" not in _CACHE:
        _CACHE["nc"] = build()
    nc = _CACHE["nc"]
    in_maps = make_in_maps(**inputs)
    res = run_bass_kernel_spmd(nc, in_maps, core_ids=list(range(8)))
    return assemble(res.results)
```
